# Optimizing a Trainium2 kernel written in Bass

```python
import jax, jax.numpy as jnp
from jax import lax
import numpy as np

D_MODEL = 1024
BATCH = 4
SEQ = 8192
DEPTH = 2
DEC_BATCH = 32
DEC_SEQ = 64
PAST_LEN = 1024

CHUNK = 64
Q_BLOCK = 128
N_MIXERS = 2
EPS = 1e-6
NEG_INF = -1e30

FOX_HEADS = 16
FOX_HEAD_DIM = 64
FOX_WIDTH = FOX_HEADS * FOX_HEAD_DIM
FOX_IN = 4 * FOX_WIDTH + FOX_HEADS

GLA_HEADS = 4
GLA_KEY_WIDTH = D_MODEL // 2
GLA_VAL_WIDTH = D_MODEL
GLA_HEAD_K = GLA_KEY_WIDTH // GLA_HEADS
GLA_HEAD_V = GLA_VAL_WIDTH // GLA_HEADS
GLA_GATE_RANK = 16
GLA_GATE_TEMP = 16.0
GLA_IN = 2 * GLA_KEY_WIDTH + 2 * GLA_VAL_WIDTH + GLA_GATE_RANK

kernel_name = "fox_gla_streaming_encoder_step"


def rmsnorm(x, g):
    xf = x.astype(jnp.float32)
    y = xf * lax.rsqrt(jnp.mean(xf * xf, axis=-1, keepdims=True) + EPS)
    return (y * g.astype(jnp.float32)).astype(x.dtype)


def fox_project(h, w_in, b_f):
    B, L, _ = h.shape
    z = h @ w_in
    q, k, v, gate, fl = jnp.split(z, [FOX_WIDTH, 2 * FOX_WIDTH, 3 * FOX_WIDTH, 4 * FOX_WIDTH], axis=-1)
    shp = (B, L, FOX_HEADS, FOX_HEAD_DIM)
    logf = jax.nn.log_sigmoid((fl + b_f).astype(jnp.float32))
    return q.reshape(shp), k.reshape(shp), v.reshape(shp), gate, logf


def fox_attend(q, cq, qpos, k, v, ck, kpos):
    s = jnp.einsum('bqhd,bkhd->bhqk', q, k, preferred_element_type=jnp.float32) * (FOX_HEAD_DIM ** -0.5)
    s = s + jnp.swapaxes(cq, 1, 2)[..., :, None] - jnp.swapaxes(ck, 1, 2)[..., None, :]
    s = jnp.where(kpos[None, :] <= qpos[:, None], s, NEG_INF)
    p = jax.nn.softmax(s, axis=-1)
    return jnp.einsum('bhqk,bkhd->bqhd', p.astype(v.dtype), v)


def fox_attend_prompt(q, k, v, logf):
    B, L, H, Dh = q.shape
    c = jnp.cumsum(logf.astype(jnp.float32), axis=1)
    nb = L // Q_BLOCK
    pos = jnp.arange(L)
    qb = q.reshape(B, nb, Q_BLOCK, H, Dh).transpose(1, 0, 2, 3, 4)
    cb = c.reshape(B, nb, Q_BLOCK, H).transpose(1, 0, 2, 3)
    pb = pos.reshape(nb, Q_BLOCK)

    def block(args):
        qi, ci, pi = args
        return fox_attend(qi, ci, pi, k, v, c, pos)

    o = lax.map(block, (qb, cb, pb))
    return o.transpose(1, 0, 2, 3, 4).reshape(B, L, H, Dh)


def fox_attend_sample(q, k, v, logf, cache_k, cache_v, cache_logf):
    L = q.shape[1]
    P = cache_k.shape[1]
    k_all = jnp.concatenate([cache_k.astype(k.dtype), k], axis=1)
    v_all = jnp.concatenate([cache_v.astype(v.dtype), v], axis=1)
    lf_all = jnp.concatenate([cache_logf.astype(jnp.float32), logf], axis=1)
    c = jnp.cumsum(lf_all, axis=1)
    kpos = jnp.arange(P + L)
    return fox_attend(q, c[:, P:], P + jnp.arange(L), k_all, v_all, c, kpos)


def gated_out(o, gate, w_out):
    B, L = o.shape[:2]
    return (o.reshape(B, L, -1) * jax.nn.silu(gate)) @ w_out


def gla_project(h, w_in, w_a2, b_a):
    B, L, _ = h.shape
    z = h @ w_in
    q, k, v, gate, a1 = jnp.split(
        z, [GLA_KEY_WIDTH, 2 * GLA_KEY_WIDTH, 2 * GLA_KEY_WIDTH + GLA_VAL_WIDTH,
            2 * GLA_KEY_WIDTH + 2 * GLA_VAL_WIDTH], axis=-1)
    log_alpha = jax.nn.log_sigmoid((a1 @ w_a2 + b_a).astype(jnp.float32)) / GLA_GATE_TEMP
    kshp = (B, L, GLA_HEADS, GLA_HEAD_K)
    q = q.reshape(kshp) * (GLA_HEAD_K ** -0.5)
    return (q, k.reshape(kshp), v.reshape(B, L, GLA_HEADS, GLA_HEAD_V), gate,
            log_alpha.reshape(kshp))


def gla_chunked(q, k, v, g, s0, chunk):
    B, L, H, dk = q.shape
    dv = v.shape[-1]
    n = L // chunk

    def to_chunks(t):
        return t.astype(jnp.float32).reshape(B, n, chunk, H, t.shape[-1]).transpose(1, 0, 3, 2, 4)

    qc, kc, vc, gc = to_chunks(q), to_chunks(k), to_chunks(v), to_chunks(g)
    causal = jnp.tril(jnp.ones((chunk, chunk), dtype=bool))

    def step(S, inp):
        qi, ki, vi, gi = inp
        b = jnp.cumsum(gi, axis=2)
        b_last = b[:, :, -1:, :]
        qe = qi * jnp.exp(b)
        ke = ki * jnp.exp(-b)
        a = jnp.where(causal, jnp.einsum('bhtd,bhsd->bhts', qe, ke), 0.0)
        o = jnp.einsum('bhts,bhsv->bhtv', a, vi) + jnp.einsum('bhtd,bhdv->bhtv', qe, S)
        kd = ki * jnp.exp(b_last - b)
        S = jnp.exp(b_last)[:, :, 0, :, None] * S + jnp.einsum('bhsd,bhsv->bhdv', kd, vi)
        return S, o

    S, o = lax.scan(step, s0.astype(jnp.float32), (qc, kc, vc, gc))
    o = o.transpose(1, 3, 0, 2, 4).reshape(B, L, H, dv)
    return o, S


def gla_output(o, g_o, gate, w_out, dtype):
    o = rmsnorm(o, g_o).astype(dtype)
    return gated_out(o, gate, w_out)


def setup_inputs(seed: int = 0) -> dict:
    key = jax.random.key(seed)
    ks = jax.random.split(key, 20)
    f32 = jnp.float32

    def nrm(k, shape, scale=1.0):
        return jax.random.normal(k, shape, f32) * scale

    return {
        "x_prompt": nrm(ks[0], (BATCH, SEQ, D_MODEL)),
        "x_sample": nrm(ks[1], (DEC_BATCH, DEC_SEQ, D_MODEL)),
        "cache_fox_k": nrm(ks[2], (DEC_BATCH, PAST_LEN, FOX_HEADS, FOX_HEAD_DIM)),
        "cache_fox_v": nrm(ks[3], (DEC_BATCH, PAST_LEN, FOX_HEADS, FOX_HEAD_DIM)),
        "cache_fox_logf": jax.nn.log_sigmoid(2.0 + nrm(ks[4], (DEC_BATCH, PAST_LEN, FOX_HEADS))),
        "state_gla": nrm(ks[5], (DEC_BATCH, GLA_HEADS, GLA_HEAD_K, GLA_HEAD_V), 0.5),
        "g_norm_fox": 1.0 + nrm(ks[6], (D_MODEL,), 0.02),
        "w_in_fox": nrm(ks[7], (D_MODEL, FOX_IN), D_MODEL ** -0.5),
        "b_fox_f": 1.0 + 2.0 * jax.random.uniform(ks[8], (FOX_HEADS,), f32),
        "w_out_fox": nrm(ks[9], (FOX_WIDTH, D_MODEL), FOX_WIDTH ** -0.5),
        "g_norm_gla": 1.0 + nrm(ks[10], (D_MODEL,), 0.02),
        "w_in_gla": nrm(ks[11], (D_MODEL, GLA_IN), D_MODEL ** -0.5),
        "w_gla_a2": nrm(ks[12], (GLA_GATE_RANK, GLA_KEY_WIDTH), GLA_GATE_RANK ** -0.5),
        "b_gla_a": nrm(ks[13], (GLA_KEY_WIDTH,), 0.02),
        "g_gla_o": 1.0 + nrm(ks[14], (GLA_HEAD_V,), 0.02),
        "w_out_gla": nrm(ks[15], (GLA_VAL_WIDTH, D_MODEL), GLA_VAL_WIDTH ** -0.5),
        "g_final": 1.0 + nrm(ks[16], (D_MODEL,), 0.02),
    }


def reference(x_prompt, x_sample, cache_fox_k, cache_fox_v, cache_fox_logf, state_gla,
              g_norm_fox, w_in_fox, b_fox_f, w_out_fox,
              g_norm_gla, w_in_gla, w_gla_a2, b_gla_a, g_gla_o, w_out_gla, g_final):
    yp, ys = x_prompt, x_sample
    Lp, Ls = yp.shape[1], ys.shape[1]
    for i in range(DEPTH):
        if i % N_MIXERS == 0:
            hp = rmsnorm(yp, g_norm_fox)
            q, k, v, gate, logf = fox_project(hp, w_in_fox, b_fox_f)
            o = fox_attend_prompt(q, k, v, logf)
            yp = yp + gated_out(o, gate, w_out_fox)
            fox_k_prompt, fox_v_prompt, fox_logf_prompt = k, v, logf

            hs = rmsnorm(ys, g_norm_fox)
            q, k, v, gate, logf = fox_project(hs, w_in_fox, b_fox_f)
            o = fox_attend_sample(q, k, v, logf, cache_fox_k, cache_fox_v, cache_fox_logf)
            ys = ys + gated_out(o, gate, w_out_fox)
            fox_k_sample, fox_v_sample, fox_logf_sample = k, v, logf
        else:
            hp = rmsnorm(yp, g_norm_gla)
            q, k, v, gate, ga = gla_project(hp, w_in_gla, w_gla_a2, b_gla_a)
            s0 = jnp.zeros((yp.shape[0], GLA_HEADS, GLA_HEAD_K, GLA_HEAD_V), jnp.float32)
            o, S = gla_chunked(q, k, v, ga, s0, CHUNK)
            yp = yp + gla_output(o, g_gla_o, gate, w_out_gla, yp.dtype)
            gla_state_prompt = S.astype(state_gla.dtype)

            hs = rmsnorm(ys, g_norm_gla)
            q, k, v, gate, ga = gla_project(hs, w_in_gla, w_gla_a2, b_gla_a)
            o, S = gla_chunked(q, k, v, ga, state_gla, Ls)
            ys = ys + gla_output(o, g_gla_o, gate, w_out_gla, ys.dtype)
            gla_state_sample = S.astype(state_gla.dtype)
    y_prompt = rmsnorm(yp, g_final)
    y_sample = rmsnorm(ys, g_final)
    return (y_prompt, y_sample,
            fox_k_prompt, fox_v_prompt, fox_logf_prompt, gla_state_prompt,
            fox_k_sample, fox_v_sample, fox_logf_sample, gla_state_sample)
```

```python
import numpy as np
import concourse.bass as bass
import concourse.mybir as mybir
from concourse.bass_utils import run_bass_kernel_spmd

F32 = mybir.dt.float32
BF16 = mybir.dt.bfloat16
AF = mybir.ActivationFunctionType
ALU = mybir.AluOpType

D = 1024
EPS = 1e-6
NCORES = 8
SEQ_PER_CORE = 4
PAST = 1024
DEC = 64


class Prog:
    ENG = ("pe", "act", "dve", "pool", "sp")

    def __init__(self):
        self.ops = []
        self.last_w = {}
        self.readers = {}

    def op(self, eng, fn, reads=(), writes=(), dma=None):
        oid = len(self.ops)
        deps = set()
        for r in reads:
            if r in self.last_w:
                deps.add(self.last_w[r])
        for w in writes:
            if w in self.last_w:
                deps.add(self.last_w[w])
            for rd in self.readers.get(w, ()):
                deps.add(rd)
        deps.discard(oid)
        for r in reads:
            self.readers.setdefault(r, []).append(oid)
        for w in writes:
            self.last_w[w] = oid
            self.readers[w] = []
        self.ops.append(dict(eng=eng, fn=fn, deps=deps, dma=dma, sig=False))
        return oid

    def emit(self, nc, block_engines, sems, dma_sems):
        ops = self.ops
        for o in ops:
            if o["eng"] == "pe" and o["dma"] is None:
                o["deps"] = {d for d in o["deps"] if not (ops[d]["eng"] == "pe" and ops[d]["dma"] is None)}
        for o in ops:
            for d in o["deps"]:
                ops[d]["sig"] = True
        lastop = {}
        for o in ops:
            if o["dma"] is None:
                lastop[o["eng"]] = o
        for o in lastop.values():
            o["sig"] = True
        cnt = {e: 0 for e in self.ENG}
        dcnt = {}
        for o in ops:
            if o["dma"] is not None:
                k = o["dma"]
                dcnt[k] = dcnt.get(k, 0) + 16
                o["semkey"] = ("dma", k)
                o["semval"] = dcnt[k]
            elif o["sig"]:
                cnt[o["eng"]] += 1
                o["semkey"] = ("eng", o["eng"])
                o["semval"] = cnt[o["eng"]]
        per_eng = {e: [] for e in self.ENG}
        for o in ops:
            per_eng[o["eng"]].append(o)

        def run(engname, eng):
            waited = {}
            for o in per_eng[engname]:
                need = {}
                for d in o["deps"]:
                    dk, dv = ops[d]["semkey"], ops[d]["semval"]
                    if need.get(dk, 0) < dv:
                        need[dk] = dv
                pend = [(dk, dv) for dk, dv in need.items() if waited.get(dk, 0) < dv]
                attach = None
                if engname == "pe" and pend:
                    latest = max((d for d in o["deps"]), key=lambda d: d)
                    lk = ops[latest]["semkey"]
                    for it in pend:
                        if it[0] == lk:
                            attach = it
                    if attach is None:
                        attach = pend[-1]
                    pend = [it for it in pend if it is not attach]
                for dk, dv in pend:
                    s = dma_sems[dk[1]] if dk[0] == "dma" else sems[dk[1]]
                    eng.wait_ge(s, dv)
                    waited[dk] = dv
                ins = o["fn"](eng)
                if isinstance(ins, tuple):
                    first_ins, ins = ins
                else:
                    first_ins = ins
                if attach is not None:
                    dk, dv = attach
                    s = dma_sems[dk[1]] if dk[0] == "dma" else sems[dk[1]]
                    first_ins._wait_ge(s, dv)
                    waited[dk] = dv
                if o["dma"] is not None:
                    ins.then_inc(dma_sems[o["dma"]], 16)
                elif o["sig"]:
                    ins.then_inc(sems[engname], 1)
            for k, v in dcnt.items():
                eng.wait_ge(dma_sems[k], v)
            for e2, v in cnt.items():
                if v > 0:
                    eng.wait_ge(sems[e2], v)

        for engname, reg in block_engines.items():
            reg(lambda eng, _n=engname: run(_n, eng))


def build_nc(T):
    import contextlib
    NT = T // 512
    NBLK = max(T // 128, SEQ_PER_CORE * 12 + 4)
    TS = 512
    NG, NH, NP = 4, 4, 2
    GW = NH * 64
    nc = bass.Bass("TRN2", target_bir_lowering=False)

    def din(name, shape, dt=F32):
        return nc.dram_tensor(name, list(shape), dt, kind="ExternalInput").ap()

    def dout(name, shape, dt=F32):
        return nc.dram_tensor(name, list(shape), dt, kind="ExternalOutput").ap()

    xp = din("xp", [T, D]); xs = din("xs", [TS, D])
    ck = din("ck", [SEQ_PER_CORE, PAST, 1024]); cv = din("cv", [SEQ_PER_CORE, PAST, 1024])
    clf = din("clf", [SEQ_PER_CORE, PAST, 16]); sg = din("sg", [SEQ_PER_CORE, 4, 128, 256])
    wfox = din("wfox", [NG, D, 4 * GW + NH]); gfox = din("gfox", [128, 8]); bff = din("bff", [128, 16])
    wo0 = din("wo0", [D, D]); ggla = din("ggla", [128, 8]); wgla = din("wgla", [D, 3088])
    wa2 = din("wa2", [16, 512]); bga = din("bga", [128, 512]); ggo = din("ggo", [128, 1024])
    wo1 = din("wo1", [D, D]); gfin = din("gfin", [128, 1024])
    c_ident = din("c_ident", [128, 128]); c_trii = din("c_trii", [128, 128]); c_trir = din("c_trir", [128, 128])
    c_ones = din("c_ones", [128, 128]); c_mneg = din("c_mneg", [128, 128]); c_gsc = din("c_gsc", [128, 2])

    yp_o = dout("yp", [T, D]); ys_o = dout("ys", [TS, D])
    kp_o = dout("kp", [T, 1024]); vp_o = dout("vp", [T, 1024]); lp_o = dout("lp", [T, 16])
    sp_o = dout("stp", [4, 128, 256])
    ks_o = dout("ks", [TS, 1024]); vs_o = dout("vs", [TS, 1024]); ls_o = dout("ls", [TS, 16])
    ss_o = dout("sts", [SEQ_PER_CORE, 4, 128, 256])
    onscr = nc.dram_tensor("onscr", [T + TS, 1024], BF16).ap()
    sgscr = nc.dram_tensor("sgscr", [T + TS, 1024], BF16).ap()
    ypscr = nc.dram_tensor("ypscr", [T + TS, 1024], F32).ap()
    hTscr = nc.dram_tensor("hTscr", [T // 512 + 1, 128, 8 * 512], BF16).ap()
    ogs = nc.dram_tensor("ogscr", [1024, T + TS], BF16).ap()

    with contextlib.ExitStack() as es:
        def mkS(stack):
            def S(name, shape, dt):
                return stack.enter_context(nc.sbuf_tensor(name, list(shape), dt))
            return S

        S = mkS(es)
        wst = S("wst", [128, 1024], F32)
        hb = S("hb", [128, 4, D], BF16)
        hT = S("hT", [128, 8, 512], BF16)
        junk = S("junk", [128, D], BF16)
        ssq = S("ssq", [128, 8], F32)
        rstd = S("rstd", [128, 8], F32)
        ident = S("ident", [128, 128], BF16)
        identf = S("identf", [128, 128], F32)
        trii = S("trii", [128, 128], F32)
        trir = S("trir", [128, 128], F32)
        onesf = S("onesf", [128, 128], F32)
        mneg = S("mneg", [128, 128], BF16)
        mnegf = S("mnegf", [128, 128], F32)
        gsc = S("gsc", [128, 2], F32)
        gfx = S("gfx", [128, 8], F32)
        ggl = S("ggl", [128, 8], F32)
        bfb = S("bfb", [128, 16], F32)
        psG = [es.enter_context(nc.psum_tensor(f"psG{i}", [128, 512], F32)) for i in range(8)]
        psT = {5: psG[5][:, :].bitcast(BF16), 6: psG[6][:, :].bitcast(BF16), 7: psG[7][:, :].bitcast(BF16)}
        rr = {"g": 0, "t": 0}

        def gbank():
            i = rr["g"] % 6
            rr["g"] += 1
            return i

        def tbank():
            i = 6 + rr["t"] % 2
            rr["t"] += 1
            return i

        P = [None]

        def dma(out, in_, reads, writes, key, q=None):
            if q is None:
                q = "pool" if (reads and not writes) or (writes == ("ogscr",)) else "sp"
            P[0].op(q, lambda e, o=out, i=in_: e.dma_start(out=o, in_=i), reads=reads, writes=writes, dma=key)

        def mm_group(psname, mms):
            reads = set()
            for m in mms:
                reads.update(m[5])

            def fn(e, mms=mms):
                ins = None
                first = None
                for (o, l, r, st, sp_, _) in mms:
                    ins = e.matmul(o, lhsT=l, rhs=r, start=st, stop=sp_)
                    if first is None:
                        first = ins
                return (first, ins)
            P[0].op("pe", fn, reads=tuple(reads), writes=(psname,))

        wrr = [0]

        def load_weight(dst, dstname, src2d, ncols, gname, gt, wst2=None):
            for kc in range(8):
                for c0 in range(0, ncols, 1024):
                    c1 = min(ncols, c0 + 1024)
                    wi = 0
                    if wst2 is not None:
                        wi = wrr[0] % 2
                        wrr[0] += 1
                    wb = wst if wi == 0 else wst2
                    wn = "wst" if wi == 0 else "wst2"
                    dma(wb[:, 0:c1 - c0], src2d[kc * 128:(kc + 1) * 128, c0:c1], (), (wn,), wn)
                    if gt is None:
                        P[0].op("dve", lambda e, kc=kc, c0=c0, c1=c1, wb=wb: e.tensor_copy(
                            out=dst[:, kc, c0:c1], in_=wb[:, 0:c1 - c0]), reads=(wn,), writes=(dstname,))
                    elif wi == 0:
                        P[0].op("act", lambda e, kc=kc, c0=c0, c1=c1, wb=wb: e.activation(
                            out=dst[:, kc, c0:c1], in_=wb[:, 0:c1 - c0], func=AF.Copy, scale=gt[:, kc:kc + 1]),
                            reads=(wn, gname), writes=(dstname,))
                    else:
                        P[0].op("dve", lambda e, kc=kc, c0=c0, c1=c1, wb=wb: e.tensor_scalar(
                            out=dst[:, kc, c0:c1], in0=wb[:, 0:c1 - c0], scalar1=gt[:, kc:kc + 1], scalar2=None,
                            op0=ALU.mult), reads=(wn, gname), writes=(dstname,))

        def rms_rstd(src, srcname, nsub, n_el):
            for s in range(nsub):
                P[0].op("act", lambda e, s=s: e.activation(out=junk[:, 0:n_el], in_=src(s), func=AF.Square,
                                                           accum_out=ssq[:, s:s + 1]),
                        reads=(srcname,), writes=("junk", "ssq" + str(s)))
            P[0].op("dve", lambda e: e.tensor_scalar(out=rstd[:, 0:nsub], in0=ssq[:, 0:nsub], scalar1=1.0 / n_el,
                                                     scalar2=EPS, op0=ALU.mult, op1=ALU.add),
                    reads=tuple("ssq" + str(s) for s in range(nsub)), writes=("rstd",))
            P[0].op("act", lambda e: e.activation(out=rstd[:, 0:nsub], in_=rstd[:, 0:nsub], func=AF.Ln), reads=("rstd",), writes=("rstd",))
            P[0].op("act", lambda e: e.activation(out=rstd[:, 0:nsub], in_=rstd[:, 0:nsub], func=AF.Exp, scale=-0.5), reads=("rstd",), writes=("rstd",))

        def transpose_tok(src_fn, srcname, nsub, dst, dstname):
            for kc in range(8):
                tb = tbank()

                def fn(e, kc=kc, tb=tb):
                    ins = None
                    first = None
                    for s in range(nsub):
                        ins = e.transpose(psT[tb][:, s * 128:(s + 1) * 128], src_fn(s)[:, kc * 128:(kc + 1) * 128],
                                          ident[:])
                        if first is None:
                            first = ins
                    return (first, ins)
                P[0].op("pe", fn, reads=(srcname, "ident"), writes=(f"psG{tb}",))
                P[0].op("dve", lambda e, kc=kc, tb=tb: e.tensor_copy(out=dst[:, kc, 0:nsub * 128],
                                                                    in_=psT[tb][:, 0:nsub * 128]),
                        reads=(f"psG{tb}",), writes=(dstname,))

        def emit_phase(tag):
            dma_keys = sorted({o["dma"] for o in P[0].ops if o["dma"] is not None})
            sems = {e: es.enter_context(nc.semaphore(f"s{tag}_{e}")) for e in Prog.ENG}
            dsems = {k: es.enter_context(nc.semaphore(f"d{tag}_{k}")) for k in dma_keys}
            with nc.Block() as block:
                P[0].emit(nc, {"pe": block.tensor, "act": block.scalar, "dve": block.vector, "pool": block.gpsimd,
                               "sp": block.sync}, sems, dsems)

        with contextlib.ExitStack() as esA:
            S = mkS(esA)
            P[0] = Prog()
            KA = [S(f"KA{h}", [68, NBLK * 128], BF16) for h in range(NH)]
            VA = S("VA", [128, NBLK, NH, 65], BF16)
            cT = S("cT", [128, NBLK, NH], F32)
            carry = S("carry", [128, NBLK + 1, NH], F32)
            biasT = S("biasT", [128, NBLK, NH], F32)
            W0 = S("W0", [128, 8, 4 * GW + NH], BF16)
            xt = S("xt", [128, 4, D], F32)
            QA = [S(f"QA{i}", [68, NH, 512], BF16) for i in range(2)]
            csp = [S(f"csp{i}", [NH, 512], F32) for i in range(2)]
            csb = [S(f"csb{i}", [NH, 512], BF16) for i in range(3)]
            ctmp = S("ctmp", [128, 4, NH], F32)
            arow = S("arow", [NH, 512], BF16)
            sgT = [S(f"sgT{i}", [128, NP, 512], BF16) for i in range(2)]
            ktok = S("ktok", [128, 4, GW], F32)
            ktb = S("ktb", [128, 4, GW], BF16)
            vtok = S("vtok", [128, 4, GW], F32)
            lft = S("lft", [128, 4, NH], F32)
            lfe = S("lfe", [128, 4, NH], F32)
            PT = [S(f"PT{i}", [128, 512], BF16) for i in range(5)]
            rd = S("rd", [128, 512], F32)
            otmp = S("otmp", [64, 512], F32)
            ogst = [S(f"ogst{i}", [64, NH, 512], BF16) for i in range(2)]
            ckt = S("ckt", [128, GW], F32)
            ckb = S("ckb", [128, GW], BF16)
            clt = S("clt", [128, 8, NH], F32)

            dma(identf[:], c_ident[:, :], (), ("identf",), "identf")
            dma(trii[:], c_trii[:, :], (), ("trii",), "trii")
            dma(trir[:], c_trir[:, :], (), ("trir",), "trir")
            dma(onesf[:], c_ones[:, :], (), ("onesf",), "onesf")
            dma(gsc[:], c_gsc[:, :], (), ("gsc",), "gsc")
            dma(gfx[:], gfox[:, :], (), ("gfx",), "gfx")
            dma(ggl[:], ggla[:, :], (), ("ggl",), "ggl")
            dma(bfb[:], bff[:, :], (), ("bfb",), "bfb")
            dma(mnegf[:], c_mneg[:, :], (), ("mnegf",), "mnegf")
            P[0].op("dve", lambda e: e.tensor_copy(out=ident[:], in_=identf[:]), reads=("identf",), writes=("ident",))
            P[0].op("dve", lambda e: e.tensor_copy(out=mneg[:], in_=mnegf[:]), reads=("mnegf",), writes=("mneg",))

            def KAn(h, blk):
                return f"KA{h}_{blk // 4}"

            def VAn(blk):
                return f"VA_{blk // 4}"
            NTB = NBLK // 4
            ALLKA = [tuple(f"KA{h}_{t}" for t in range(NTB)) for h in range(NH)]
            ALLVA = tuple(f"VA_{t}" for t in range(NTB))
            pa_rr = [0]

            def pbank():
                i = 5 + pa_rr[0] % 3
                pa_rr[0] += 1
                return i

            def transpose_tok_a(src_fn, srcname, nsub, dst, dstname):
                for kc in range(8):
                    tb = pbank()

                    def fn(e, kc=kc, tb=tb):
                        ins = None
                        first = None
                        for s in range(nsub):
                            ins = e.transpose(psT[tb][:, s * 128:(s + 1) * 128],
                                              src_fn(s)[:, kc * 128:(kc + 1) * 128], ident[:])
                            if first is None:
                                first = ins
                        return (first, ins)
                    P[0].op("pe", fn, reads=(srcname, "ident"), writes=(f"psG{tb}",))
                    P[0].op("dve", lambda e, kc=kc, tb=tb: e.tensor_copy(out=dst[:, kc, 0:nsub * 128],
                                                                        in_=psT[tb][:, 0:nsub * 128]),
                            reads=(f"psG{tb}",), writes=(dstname,))
                    yield

            def fox_inproj_gen(hg, xsrc, blk0, kout, vout, lout, row0, par, do_cumsum, tix):
                QAp, sgTp = QA[par], sgT[par]
                qan, sgn = f"QA{par}", f"sgT{par}"
                kt_col0 = blk0 * 128
                if hg == 0:
                    dma(xt[:], xsrc.rearrange("(s p) d -> p s d", p=128), (), ("xt",), "xt")
                    yield
                    rms_rstd(lambda s: xt[:, s, :], "xt", 4, D)
                    yield
                    for s in range(4):
                        P[0].op("dve", lambda e, s=s: e.tensor_scalar(out=hb[:, s, :], in0=xt[:, s, :],
                                                                      scalar1=rstd[:, s:s + 1], scalar2=None,
                                                                      op0=ALU.mult),
                                reads=("xt", "rstd"), writes=("hb",))
                        yield
                    yield from transpose_tok_a(lambda s: hb[:, s, :], "hb", 4, hT, "hT")
                    dma(hTscr[tix].rearrange("p (c t) -> p c t", c=8), hT[:], ("hT",), ("hTscr",), "hTst", q="pool")
                else:
                    dma(hT[:], hTscr[tix].rearrange("p (c t) -> p c t", c=8), ("hTscr",), ("hT",), "hT", q="sp")
                    yield
                for kind in (0, 2):
                    for pr in range(NP):
                        col = {0: 0, 1: GW, 2: 3 * GW}[kind] + pr * 128
                        g = pbank()
                        mm_group(f"psG{g}", [(psG[g][:, :], W0[:, kc, col:col + 128], hT[:, kc, :], kc == 0, kc == 7,
                                              ("W0", "hT")) for kc in range(8)])
                        if kind == 0:
                            for hf in range(2):
                                P[0].op("dve", lambda e, g=g, pr=pr, hf=hf: e.tensor_scalar(
                                    out=QAp[0:64, 2 * pr + hf, :], in0=psG[g][hf * 64:hf * 64 + 64, :],
                                    scalar1=0.125, scalar2=None, op0=ALU.mult),
                                    reads=(f"psG{g}",), writes=(qan,))
                        elif kind == 1:
                            for hf in range(2):
                                P[0].op("dve", lambda e, g=g, pr=pr, hf=hf: e.tensor_copy(
                                    out=KA[2 * pr + hf][0:64, kt_col0:kt_col0 + 512],
                                    in_=psG[g][hf * 64:hf * 64 + 64, :]),
                                    reads=(f"psG{g}",), writes=(KAn(2 * pr + hf, blk0),))
                        else:
                            P[0].op("act", lambda e, g=g, pr=pr: e.activation(out=sgTp[:, pr, :], in_=psG[g][:, :],
                                                                             func=AF.Silu),
                                    reads=(f"psG{g}",), writes=(sgn,))
                        yield
                for s in range(4):
                    g = pbank()
                    mm_group(f"psG{g}", [(psG[g][:, 0:2 * GW], hT[:, kc, s * 128:(s + 1) * 128],
                                          W0[:, kc, GW:3 * GW], kc == 0, kc == 7, ("W0", "hT")) for kc in range(8)])
                    P[0].op("dve", lambda e, g=g, s=s: e.tensor_copy(out=ktok[:, s, :], in_=psG[g][:, 0:GW]),
                            reads=(f"psG{g}",), writes=("ktok",))
                    P[0].op("act", lambda e, s=s: e.activation(out=ktb[:, s, :], in_=ktok[:, s, :], func=AF.Copy),
                            reads=("ktok",), writes=("ktb",))
                    P[0].op("dve", lambda e, g=g, s=s: e.tensor_copy(out=vtok[:, s, :], in_=psG[g][:, GW:2 * GW]),
                            reads=(f"psG{g}",), writes=("vtok",))
                    P[0].op("pool", lambda e, s=s: e.tensor_copy(out=VA[:, blk0 + s, :, 0:64],
                                                                 in_=vtok[:, s, :].rearrange("p (h d) -> p h d", h=NH)),
                            reads=("vtok",), writes=(VAn(blk0),))
                    yield
                    g = pbank()
                    mm_group(f"psG{g}", [(psG[g][:, 0:NH], hT[:, kc, s * 128:(s + 1) * 128],
                                          W0[:, kc, 4 * GW:4 * GW + NH], kc == 0, kc == 7, ("W0", "hT"))
                                         for kc in range(8)])
                    P[0].op("dve", lambda e, g=g, s=s: e.tensor_tensor(out=lfe[:, s, :], in0=psG[g][:, 0:NH],
                                                                       in1=bfb[:, hg * NH:(hg + 1) * NH], op=ALU.add),
                            reads=(f"psG{g}", "bfb"), writes=("lfe",))
                    yield
                for pr in range(NP):
                    tb = pbank()

                    def fn_kt(e, tb=tb, pr=pr):
                        ins = None
                        first = None
                        for s in range(4):
                            ins = e.transpose(psT[tb][:, s * 128:(s + 1) * 128],
                                              ktb[:, s, pr * 128:(pr + 1) * 128], ident[:])
                            if first is None:
                                first = ins
                        return (first, ins)
                    P[0].op("pe", fn_kt, reads=("ktb", "ident"), writes=(f"psG{tb}",))
                    for hf in range(2):
                        h = 2 * pr + hf
                        P[0].op("dve", lambda e, tb=tb, h=h, hf=hf: e.tensor_copy(
                            out=KA[h][0:64, kt_col0:kt_col0 + 512], in_=psT[tb][hf * 64:hf * 64 + 64, 0:512]),
                            reads=(f"psG{tb}",), writes=(KAn(h, blk0),))
                yield
                P[0].op("act", lambda e: e.activation(out=lfe[:], in_=lfe[:], func=AF.Exp, scale=-1.0),
                        reads=("lfe",), writes=("lfe",))
                P[0].op("act", lambda e: e.activation(out=lfe[:], in_=lfe[:], func=AF.Ln, bias=1.0),
                        reads=("lfe",), writes=("lfe",))
                P[0].op("dve", lambda e: e.tensor_scalar(out=lft[:], in0=lfe[:], scalar1=-1.0, scalar2=None,
                                                         op0=ALU.mult),
                        reads=("lfe",), writes=("lft",))
                yield
                dst3 = lambda o, w: o[row0:row0 + 512, hg * w:(hg + 1) * w].rearrange("(s p) c -> p s c", p=128)
                dma(dst3(kout, GW), ktok[:], ("ktok",), (), "ktok")
                dma(dst3(vout, GW), vtok[:], ("vtok",), (), "vtok")
                dma(dst3(lout, NH), lft[:], ("lft",), (), "lft")
                yield
                if do_cumsum:
                    for s in range(4):
                        cumsum_block(lft[:, s, :], "lft", blk0 + s)
                        yield
                    arow_prep(par, 0, 512, blk0 + 2, blk0, prompt=True)
                    yield

            def drain(gen):
                if gen is not None:
                    for _ in gen:
                        pass

            def cumsum_block(lsrc, lname, blk):
                g = pbank()
                mm_group(f"psG{g}", [(psG[g][:, 0:NH], trii[:, :], lsrc, True, True, (lname, "trii")),
                                     (psG[g][:, 8:8 + NH], onesf[:, :], lsrc, True, True, (lname, "onesf"))])
                P[0].op("dve", lambda e, g=g: e.tensor_tensor(out=cT[:, blk, :], in0=psG[g][:, 0:NH],
                                                              in1=carry[:, blk, :], op=ALU.add),
                        reads=(f"psG{g}", "carry"), writes=("cT",))
                P[0].op("dve", lambda e, g=g: e.tensor_tensor(out=carry[:, blk + 1, :], in0=psG[g][:, 8:8 + NH],
                                                              in1=carry[:, blk, :], op=ALU.add),
                        reads=(f"psG{g}", "carry"), writes=("carry",))

            def arow_prep(par, qc0, nq, ref_idx, qblk0, prompt=False):
                QAp, qan = QA[par], f"QA{par}"
                nqb = nq // 128
                if prompt:
                    src_fn = lambda qb: cT[:, qblk0 + qb, :]
                    srcname = "cT"
                else:
                    P[0].op("dve", lambda e: e.tensor_tensor(
                        out=ctmp[:, 0:nqb, :], in0=cT[:, qblk0:qblk0 + nqb, :],
                        in1=carry[:, ref_idx:ref_idx + 1, :].to_broadcast([128, nqb, NH]), op=ALU.subtract),
                        reads=("cT", "carry"), writes=("ctmp",))
                    src_fn = lambda qb: ctmp[:, qb, :]
                    srcname = "ctmp"
                ga_ = pbank()

                def fn_tr(e, ga_=ga_):
                    ins = None
                    first = None
                    for qb in range(nqb):
                        ins = e.transpose(psG[ga_][0:NH, qb * 128:(qb + 1) * 128], src_fn(qb), identf[:])
                        if first is None:
                            first = ins
                    return (first, ins)
                P[0].op("pe", fn_tr, reads=(srcname, "identf"), writes=(f"psG{ga_}",))
                P[0].op("dve", lambda e, ga_=ga_: e.tensor_copy(out=arow[:, 0:nq], in_=psG[ga_][0:NH, 0:nq]),
                        reads=(f"psG{ga_}",), writes=("arow",))
                for hh in range(NH):
                    dma(QAp[64:65, hh, qc0:qc0 + nq], arow[hh:hh + 1, 0:nq], ("arow",), (qan,), "arow", q="sp")
                if prompt:
                    n1, r1 = csp
                    hi, mid, lo = csb
                    P[0].op("dve", lambda e, ga_=ga_: e.tensor_scalar(out=n1[:], in0=psG[ga_][0:NH, 0:512],
                                                                      scalar1=-1.0, scalar2=None, op0=ALU.mult),
                            reads=(f"psG{ga_}",), writes=("csp0",))
                    P[0].op("dve", lambda e: e.tensor_copy(out=hi[:], in_=n1[:]), reads=("csp0",), writes=("csb0",))
                    P[0].op("dve", lambda e: e.tensor_tensor(out=r1[:], in0=n1[:], in1=hi[:], op=ALU.subtract),
                            reads=("csp0", "csb0"), writes=("csp1",))
                    P[0].op("dve", lambda e: e.tensor_copy(out=mid[:], in_=r1[:]), reads=("csp1",), writes=("csb1",))
                    P[0].op("dve", lambda e: e.tensor_tensor(out=n1[:], in0=r1[:], in1=mid[:], op=ALU.subtract),
                            reads=("csp1", "csb1"), writes=("csp0",))
                    P[0].op("dve", lambda e: e.tensor_copy(out=lo[:], in_=n1[:]), reads=("csp0",), writes=("csb2",))
                    c0 = qblk0 * 128
                    for hh in range(NH):
                        for j in range(3):
                            dma(KA[hh][65 + j:66 + j, c0:c0 + 512], csb[j][hh:hh + 1, :], (f"csb{j}",),
                                (KAn(hh, qblk0),), f"csb{j}", q="sp")

            pt_rr = [0]
            s_rr = [0]
            LA = 2

            def attention_qtile(hg, qc0, nq, kblocks, ref_idx, scr_col0, stbuf, qblk0, par, filler=None, use_bias=True):
                QAp, sgTp = QA[par], sgT[par]
                qan, sgn = f"QA{par}", f"sgT{par}"
                blks = [b for (b, _, _) in kblocks]
                b_lo, b_hi = min(blks), max(blks) + 1
                if use_bias:
                    P[0].op("dve", lambda e: e.tensor_tensor(
                        out=biasT[:, b_lo:b_hi, :],
                        in0=carry[:, ref_idx:ref_idx + 1, :].to_broadcast([128, b_hi - b_lo, NH]),
                        in1=cT[:, b_lo:b_hi, :], op=ALU.subtract), reads=("carry", "cT"), writes=("biasT",))
                og = ogst[stbuf]
                ogname = f"ogst{stbuf}"
                nkb = len(kblocks)
                units = [(hh, bi) for hh in range(NH) for bi in range(nkb)]
                nu = len(units)
                sb_of, pt_of = {}, {}

                def issue_mm1(u):
                    hh, bi = units[u]
                    blk, qoff, masked = kblocks[bi]
                    n = nq - qoff
                    gs_ = s_rr[0] % 3
                    s_rr[0] += 1
                    sb_of[u] = gs_
                    mms = [(psG[gs_][:, 0:n], KA[hh][0:68, blk * 128:(blk + 1) * 128],
                            QAp[0:68, hh, qc0 + qoff:qc0 + nq], True, not masked, (KAn(hh, blk), qan))]
                    if masked:
                        mms.append((psG[gs_][:, 0:128], ident[:, :], mneg[:, :], False, True, ("ident", "mneg")))
                    mm_group(f"psG{gs_}", mms)

                def issue_exp(u):
                    hh, bi = units[u]
                    blk, qoff, masked = kblocks[bi]
                    n = nq - qoff
                    gs_ = sb_of[u]
                    pi = pt_rr[0] % 5
                    pt_rr[0] += 1
                    pt_of[u] = pi
                    if use_bias:
                        P[0].op("act", lambda e: e.activation(
                            out=PT[pi][:, 0:n], in_=psG[gs_][:, 0:n], func=AF.Exp, bias=biasT[:, blk, hh:hh + 1],
                            scale=1.0), reads=(f"psG{gs_}", "biasT"), writes=(f"PT{pi}",))
                    else:
                        P[0].op("act", lambda e: e.activation(
                            out=PT[pi][:, 0:n], in_=psG[gs_][:, 0:n], func=AF.Exp),
                            reads=(f"psG{gs_}",), writes=(f"PT{pi}",))

                def issue_mm2(u):
                    hh, bi = units[u]
                    blk, qoff, masked = kblocks[bi]
                    n = nq - qoff
                    go = 3 + hh % 2
                    pi = pt_of[u]
                    mm_group(f"psG{go}", [(psG[go][0:65, qoff:nq], VA[:, blk, hh, :], PT[pi][:, 0:n], bi == 0,
                                           bi == nkb - 1, (VAn(blk), f"PT{pi}"))])

                def norm_a(hh):
                    go = 3 + hh % 2
                    P[0].op("dve", lambda e: e.reciprocal(out=rd[64:65, 0:nq], in_=psG[go][64:65, 0:nq]),
                            reads=(f"psG{go}",), writes=("rd",))

                def norm_b(hh):
                    go = 3 + hh % 2
                    pr, pb = hh // 2, (hh % 2) * 64
                    gb = s_rr[0] % 3
                    s_rr[0] += 1
                    mm_group(f"psG{gb}", [(psG[gb][0:64, 0:nq], onesf[64:65, 0:64], rd[64:65, 0:nq], True, True,
                                           ("onesf", "rd"))])
                    P[0].op("dve", lambda e: e.tensor_tensor(
                        out=otmp[:, 0:nq], in0=psG[go][0:64, 0:nq], in1=sgTp[pb:pb + 64, pr, qc0:qc0 + nq],
                        op=ALU.mult), reads=(f"psG{go}", sgn), writes=("otmp",))
                    P[0].op("dve", lambda e: e.tensor_tensor(
                        out=og[:, hh, 0:nq], in0=psG[gb][0:64, 0:nq], in1=otmp[:, 0:nq], op=ALU.mult),
                        reads=(f"psG{gb}", "otmp"), writes=(ogname,))

                stride = max(1, nu // 48)
                pending = []
                for u in range(min(LA, nu)):
                    issue_mm1(u)
                for u in range(nu):
                    issue_exp(u)
                    if u + LA < nu:
                        issue_mm1(u + LA)
                    issue_mm2(u)
                    for it in pending:
                        it[0] -= 1
                    while pending and pending[0][0] <= 0:
                        norm_b(pending.pop(0)[1])
                    hh, bi = units[u]
                    if bi == nkb - 1:
                        norm_a(hh)
                        pending.append([min(4, nkb), hh])
                    if filler is not None and u % stride == 0:
                        next(filler, None)
                while pending:
                    norm_b(pending.pop(0)[1])
                dst = ogs[hg * GW:(hg + 1) * GW, scr_col0:scr_col0 + nq].rearrange("(h p) t -> p h t", p=64)
                dma(dst, og[:, :, 0:nq], (ogname,), ("ogscr",), ogname)

            P[0].op("pool", lambda e: e.memset(VA[:, :, :, 64:65], 1.0), reads=(), writes=ALLVA)
            for h in range(NH):
                P[0].op("pool", lambda e, h=h: e.memset(KA[h][64:68, :], 1.0), reads=(), writes=ALLKA[h])
            for hg in range(NG):
                load_weight(W0, "W0", wfox[hg], 4 * GW + NH, "gfx", gfx)
                P[0].op("pool", lambda e: e.memset(carry[:, 0, :], 0.0), reads=(), writes=("carry",))
                for par_ in range(2):
                    P[0].op("pool", lambda e, par_=par_: e.memset(QA[par_][64:68, :, :], 1.0), reads=(),
                            writes=(f"QA{par_}",))
                drain(fox_inproj_gen(hg, xp[0:512, :], 0, kp_o, vp_o, lp_o, 0, 0, True, 0))
                for i in range(NT):
                    nxt = None
                    if i + 1 < NT:
                        nxt = fox_inproj_gen(hg, xp[(i + 1) * 512:(i + 2) * 512, :], (i + 1) * 4, kp_o, vp_o, lp_o,
                                             (i + 1) * 512, (i + 1) % 2, True, i + 1)
                    kb = [(j, 0, False) for j in range(4 * i)] + [(4 * i + jj, 128 * jj, True) for jj in range(4)]
                    attention_qtile(hg, 0, 512, kb, 4 * i + 2, i * 512, i % 2, 4 * i, i % 2, nxt, use_bias=False)
                    drain(nxt)
                for sq in range(SEQ_PER_CORE):
                    base = sq * 12
                    P[0].op("pool", lambda e, base=base: e.memset(carry[:, base, :], 0.0), reads=(),
                            writes=("carry",))
                    dma(clt[:], clf[sq, :, hg * NH:(hg + 1) * NH].rearrange("(b p) h -> p b h", p=128), (), ("clt",),
                        "clt")
                    xk = xt[:, 0:2, :].rearrange("p a (b c) -> p (a b) c", c=GW)
                    xv = xt[:, 2:4, :].rearrange("p a (b c) -> p (a b) c", c=GW)
                    hk = hb[:, 0:2, :].rearrange("p a (b c) -> p (a b) c", c=GW)
                    dma(xk, ck[sq, :, hg * GW:(hg + 1) * GW].rearrange("(b p) c -> p b c", p=128), (), ("xt",), "xt")
                    dma(xv, cv[sq, :, hg * GW:(hg + 1) * GW].rearrange("(b p) c -> p b c", p=128), (), ("xt",), "xt")
                    P[0].op("dve", lambda e: e.tensor_copy(out=hb[:, 0:2, :], in_=xt[:, 0:2, :]), reads=("xt",),
                            writes=("hb",))
                    P[0].op("act", lambda e, base=base, xv=xv: e.activation(
                        out=VA[:, base:base + 8, :, 0:64], in_=xv.rearrange("p b (h d) -> p b h d", h=NH),
                        func=AF.Copy), reads=("xt",), writes=(VAn(base), VAn(base + 4)))
                    for b in range(8):
                        cumsum_block(clt[:, b, :], "clt", base + b)
                    for half in range(2):
                        for pr in range(NP):
                            tb = pbank()

                            def fn(e, tb=tb, half=half, pr=pr, hk=hk):
                                ins = None
                                first = None
                                for j in range(4):
                                    ins = e.transpose(psT[tb][:, j * 128:(j + 1) * 128],
                                                      hk[:, 4 * half + j, pr * 128:(pr + 1) * 128], ident[:])
                                    if first is None:
                                        first = ins
                                return (first, ins)
                            P[0].op("pe", fn, reads=("hb", "ident"), writes=(f"psG{tb}",))
                            for hf in range(2):
                                h = 2 * pr + hf
                                c0 = (base + 4 * half) * 128
                                P[0].op("dve", lambda e, tb=tb, h=h, hf=hf, c0=c0: e.tensor_copy(
                                    out=KA[h][0:64, c0:c0 + 512], in_=psT[tb][hf * 64:hf * 64 + 64, 0:512]),
                                    reads=(f"psG{tb}",), writes=(KAn(h, base + 4 * half),))
                P[0].op("pool", lambda e: e.memset(QA[0][64:68, :, :], 0.0), reads=(), writes=("QA0",))
                drain(fox_inproj_gen(hg, xs[:, :], NBLK - 4, ks_o, vs_o, ls_o, 0, 0, False, T // 512))
                for sq in range(SEQ_PER_CORE):
                    base = sq * 12
                    src_blk = NBLK - 4 + sq
                    for h in range(NH):
                        P[0].op("pool", lambda e, base=base, src_blk=src_blk, h=h: e.tensor_copy(
                            out=KA[h][0:64, (base + 8) * 128:(base + 9) * 128],
                            in_=KA[h][0:64, src_blk * 128:(src_blk + 1) * 128]), reads=(KAn(h, src_blk),),
                            writes=(KAn(h, base + 8),))
                    P[0].op("pool", lambda e, base=base, src_blk=src_blk: e.tensor_copy(
                        out=VA[:, base + 8, :, :], in_=VA[:, src_blk, :, :]), reads=(VAn(src_blk),),
                        writes=(VAn(base + 8),))
                    cumsum_block(lft[:, sq, :], "lft", base + 8)
                    kb = [(base + b, 0, False) for b in range(8)] + [(base + 8, 0, True)]
                    arow_prep(0, sq * 128, 128, base + 8, base + 8)
                    attention_qtile(hg, sq * 128, 128, kb, base + 8, T + sq * 128, sq % 2, base + 8, 0)
            emit_phase("A")

        with contextlib.ExitStack() as esD:
            S = mkS(esD)
            P[0] = Prog()
            WO0 = S("WO0", [128, 8, D], BF16)
            WG = S("WG", [128, 8, 3088], BF16)
            wa2s = S("wa2s", [16, 512], F32)
            bgab = S("bgab", [128, 512], F32)
            ggob = S("ggob", [128, 1024], F32)
            gfinb = S("gfinb", [128, 1024], F32)
            ogT = S("ogT", [128, 8, 512], BF16)
            ypt = S("ypt", [128, 4, D], F32)
            qTf = S("qTf", [128, 4, 512], F32)
            kTf = S("kTf", [128, 4, 512], F32)
            a1T = S("a1T", [16, 512], F32)
            k2 = S("k2", [128, 512], F32)
            v2 = [S(f"v2{i}", [128, 1024], BF16) for i in range(2)]
            sgg = [S(f"sgg{i}", [128, 1024], BF16) for i in range(2)]
            gg = S("gg", [128, 512], F32)
            ge = S("ge", [128, 512], F32)
            ebT = [[S(f"ebT{q}{h}", [128, 128], F32) for h in range(4)] for q in range(2)]
            enbT = [S(f"enbT{h}", [128, 128], F32) for h in range(4)]
            qeT = [[S(f"qeT{q}{h}", [128, 128], BF16) for h in range(4)] for q in range(2)]
            keT = [S(f"keT{h}", [128, 128], BF16) for h in range(4)]
            ebr = S("ebr", [128, 512], F32)
            kd = [S(f"kd{i}", [128, 512], BF16) for i in range(2)]
            ATs = [[S(f"ATs{q}{h}", [128, 128], BF16) for h in range(4)] for q in range(2)]
            Sst = S("Sst", [128, 4, 256], F32)
            Sbf = S("Sbf", [128, 4, 256], BF16)
            og1 = S("og1", [128, D], BF16)
            ss2 = S("ss2", [128, 8], F32)
            rs2 = S("rs2", [128, 8], F32)

            dma(wa2s[:], wa2[:, :], (), ("wa2s",), "wa2s")
            dma(bgab[:], bga[:, :], (), ("bgab",), "bgab")
            dma(ggob[:], ggo[:, :], (), ("ggob",), "ggob")
            dma(gfinb[:], gfin[:, :], (), ("gfinb",), "gfinb")
            wstD = S("wstD", [128, 1024], F32)
            load_weight(WO0, "WO0", wo0, D, None, None, wstD)
            load_weight(WG, "WG", wgla, 3088, "ggl", ggl, wstD)
            ALLS = tuple(f"Sst{h}" for h in range(4))
            ALLB = tuple(f"Sbf{h}" for h in range(4))
            P[0].op("pool", lambda e: e.memset(Sst[:], 0.0), reads=(), writes=ALLS)
            P[0].op("pool", lambda e: e.memset(Sbf[:], 0.0), reads=(), writes=ALLB)

            gba_rr = [0]

            def gba():
                i = 4 + gba_rr[0] % 4
                gba_rr[0] += 1
                return i

            def phase_d_tile(xsrc, scr_col0, rowbase, sample):
                dma(ypt[:], xsrc.rearrange("(s p) d -> p s d", p=128), (), ("ypt",), "ypt")
                dma(ogT[:], ogs[:, scr_col0:scr_col0 + 512].rearrange("(c p) t -> p c t", p=128), (), ("ogT",), "ogT")
                for s in range(4):
                    for nh in range(2):
                        g = gbank()
                        mm_group(f"psG{g}", [(psG[g][:, :], ogT[:, fc, s * 128:(s + 1) * 128],
                                              WO0[:, fc, nh * 512:(nh + 1) * 512], fc == 0, fc == 7, ("ogT", "WO0"))
                                             for fc in range(8)])
                        P[0].op("dve", lambda e, g=g, s=s, nh=nh: e.tensor_tensor(
                            out=ypt[:, s, nh * 512:(nh + 1) * 512], in0=psG[g][:, :],
                            in1=ypt[:, s, nh * 512:(nh + 1) * 512], op=ALU.add),
                            reads=(f"psG{g}", "ypt"), writes=("ypt",))
                dma(ypscr[rowbase:rowbase + 512, :].rearrange("(s p) d -> p s d", p=128), ypt[:], ("ypt",), (), "ypt")
                rms_rstd(lambda s: ypt[:, s, :], "ypt", 4, D)
                for s in range(4):
                    if s % 2 == 0:
                        P[0].op("act", lambda e, s=s: e.activation(out=hb[:, s, :], in_=ypt[:, s, :], func=AF.Copy,
                                                                   scale=rstd[:, s:s + 1]),
                                reads=("ypt", "rstd"), writes=("hb",))
                    else:
                        P[0].op("dve", lambda e, s=s: e.tensor_scalar(out=hb[:, s, :], in0=ypt[:, s, :],
                                                                      scalar1=rstd[:, s:s + 1], scalar2=None,
                                                                      op0=ALU.mult),
                                reads=("ypt", "rstd"), writes=("hb",))
                transpose_tok(lambda s: hb[:, s, :], "hb", 4, hT, "hT")
                for kind in range(2):
                    for h in range(4):
                        col = kind * 512 + h * 128
                        g = gbank()
                        mm_group(f"psG{g}", [(psG[g][:, :], WG[:, kc, col:col + 128], hT[:, kc, :], kc == 0, kc == 7,
                                              ("WG", "hT")) for kc in range(8)])
                        if kind == 0:
                            P[0].op("dve", lambda e, g=g, h=h: e.tensor_scalar(out=qTf[:, h, :], in0=psG[g][:, :],
                                                                              scalar1=128.0 ** -0.5, scalar2=None,
                                                                              op0=ALU.mult),
                                    reads=(f"psG{g}",), writes=("qTf",))
                        else:
                            P[0].op("dve", lambda e, g=g, h=h: e.tensor_copy(out=kTf[:, h, :], in_=psG[g][:, :]),
                                    reads=(f"psG{g}",), writes=("kTf",))
                g = gbank()
                mm_group(f"psG{g}", [(psG[g][0:16, :], WG[:, kc, 3072:3088], hT[:, kc, :], kc == 0, kc == 7,
                                      ("WG", "hT")) for kc in range(8)])
                P[0].op("dve", lambda e, g=g: e.tensor_copy(out=a1T[:, :], in_=psG[g][0:16, :]), reads=(f"psG{g}",),
                        writes=("a1T",))
                def stage_a(s, q):
                    ts_ = slice(s * 128, (s + 1) * 128)
                    v2q, kdq, sggq = v2[q], kd[q], sgg[q]
                    g = gba()
                    mm_group(f"psG{g}", [(psG[g][:, :], hT[:, kc, ts_], WG[:, kc, 512:1024], kc == 0, kc == 7,
                                          ("WG", "hT")) for kc in range(8)])
                    P[0].op("dve", lambda e, g=g: e.tensor_copy(out=k2[:], in_=psG[g][:, :]), reads=(f"psG{g}",),
                            writes=("k2",))
                    for nh in range(2):
                        g = gba()
                        mm_group(f"psG{g}", [(psG[g][:, :], hT[:, kc, ts_],
                                              WG[:, kc, 1024 + nh * 512:1536 + nh * 512], kc == 0, kc == 7,
                                              ("WG", "hT")) for kc in range(8)])
                        P[0].op("dve", lambda e, g=g, nh=nh: e.tensor_copy(out=v2q[:, nh * 512:(nh + 1) * 512],
                                                                          in_=psG[g][:, :]),
                                reads=(f"psG{g}",), writes=(f"v2{q}",))
                    for nh in range(2):
                        g = gba()
                        mm_group(f"psG{g}", [(psG[g][:, :], hT[:, kc, ts_],
                                              WG[:, kc, 2048 + nh * 512:2560 + nh * 512], kc == 0, kc == 7,
                                              ("WG", "hT")) for kc in range(8)])
                        P[0].op("act", lambda e, g=g, nh=nh: e.activation(out=sggq[:, nh * 512:(nh + 1) * 512],
                                                                         in_=psG[g][:, :], func=AF.Silu),
                                reads=(f"psG{g}",), writes=(f"sgg{q}",))
                    dma(sgscr[rowbase + s * 128:rowbase + (s + 1) * 128, :], sggq[:], (f"sgg{q}",), (), f"sgg{q}")
                    g = gba()
                    mm_group(f"psG{g}", [(psG[g][:, :], a1T[0:16, ts_], wa2s[0:16, :], True, True, ("a1T", "wa2s"))])
                    P[0].op("dve", lambda e, g=g: e.tensor_tensor(out=ge[:], in0=psG[g][:, :], in1=bgab[:],
                                                                  op=ALU.add),
                            reads=(f"psG{g}", "bgab"), writes=("ge",))
                    P[0].op("act", lambda e: e.activation(out=ge[:], in_=ge[:], func=AF.Exp, scale=-1.0),
                            reads=("ge",), writes=("ge",))
                    P[0].op("act", lambda e: e.activation(out=ge[:], in_=ge[:], func=AF.Ln, bias=1.0), reads=("ge",),
                            writes=("ge",))
                    gcol = 1 if sample else 0
                    P[0].op("dve", lambda e, gcol=gcol: e.tensor_scalar(out=gg[:], in0=ge[:],
                                                                        scalar1=gsc[:, gcol:gcol + 1], scalar2=None,
                                                                        op0=ALU.mult),
                            reads=("ge", "gsc"), writes=("gg",))
                    g = gba()
                    mm_group(f"psG{g}", [(psG[g][:, :], trir[:, :], gg[:, :], True, True, ("trir", "gg"))])
                    P[0].op("act", lambda e, g=g: e.activation(out=ebr[:], in_=psG[g][:, :], func=AF.Exp),
                            reads=(f"psG{g}",), writes=("ebr",))
                    P[0].op("dve", lambda e: e.tensor_tensor(out=kdq[:], in0=k2[:], in1=ebr[:], op=ALU.mult),
                            reads=("k2", "ebr"), writes=(f"kd{q}",))
                    for h in range(4):
                        hs = slice(h * 128, (h + 1) * 128)
                        g = gba()
                        mm_group(f"psG{g}", [(psG[g][:, 0:128], gg[:, hs], trii[:, :], True, True, ("gg", "trii"))])
                        P[0].op("act", lambda e, g=g, h=h: e.activation(out=ebT[q][h][:], in_=psG[g][:, 0:128],
                                                                        func=AF.Exp),
                                reads=(f"psG{g}",), writes=(f"ebT{q}{h}",))
                        P[0].op("act", lambda e, g=g, h=h: e.activation(out=enbT[h][:], in_=psG[g][:, 0:128],
                                                                        func=AF.Exp, scale=-1.0),
                                reads=(f"psG{g}",), writes=(f"enbT{h}",))
                    for h in range(4):
                        P[0].op("dve", lambda e, h=h: e.tensor_tensor(out=qeT[q][h][:], in0=qTf[:, h, ts_],
                                                                      in1=ebT[q][h][:], op=ALU.mult),
                                reads=("qTf", f"ebT{q}{h}"), writes=(f"qeT{q}{h}",))
                        P[0].op("dve", lambda e, h=h: e.tensor_tensor(out=keT[h][:], in0=kTf[:, h, ts_],
                                                                      in1=enbT[h][:], op=ALU.mult),
                                reads=("kTf", f"enbT{h}"), writes=(f"keT{h}",))
                    for h in range(4):
                        g2 = gba()
                        mm_group(f"psG{g2}", [(psG[g2][:, 0:128], keT[h][:, :], qeT[q][h][:, :], True, True,
                                               (f"keT{h}", f"qeT{q}{h}"))])
                        P[0].op("dve", lambda e, g2=g2, h=h: e.tensor_tensor(out=ATs[q][h][:], in0=psG[g2][:, 0:128],
                                                                             in1=trii[:], op=ALU.mult),
                                reads=(f"psG{g2}", "trii"), writes=(f"ATs{q}{h}",))

                def stage_b(s, q):
                    v2q, kdq = v2[q], kd[q]
                    go = (0, 1) if q == 0 else (2, 3)
                    if sample:
                        dma(Sst[:], sg[s].rearrange("h k v -> k h v"), (), ALLS, "Sst", q="sp")
                        P[0].op("pool", lambda e: e.tensor_copy(out=Sbf[:], in_=Sst[:]), reads=ALLS,
                                writes=ALLB)
                    for h in range(4):
                        hs = slice(h * 128, (h + 1) * 128)
                        ob = go[h // 2]
                        oc = slice((h % 2) * 256, (h % 2) * 256 + 256)
                        vs_ = slice(h * 256, (h + 1) * 256)
                        mm_group(f"psG{ob}", [(psG[ob][:, oc], ATs[q][h][:, :], v2q[:, vs_], True, False,
                                               (f"ATs{q}{h}", f"v2{q}")),
                                              (psG[ob][:, oc], qeT[q][h][:, :], Sbf[:, h, :], False, True,
                                               (f"qeT{q}{h}", f"Sbf{h}"))])
                        g3 = gba()
                        mm_group(f"psG{g3}", [(psG[g3][:, 0:256], kdq[:, hs], v2q[:, vs_], True, True,
                                               (f"kd{q}", f"v2{q}"))])
                        P[0].op("dve", lambda e, g3=g3, h=h: e.scalar_tensor_tensor(
                            out=Sst[:, h, :], in0=Sst[:, h, :], scalar=ebT[q][h][:, 127:128], in1=psG[g3][:, 0:256],
                            op0=ALU.mult, op1=ALU.add), reads=(f"psG{g3}", f"ebT{q}{h}", f"Sst{h}"),
                            writes=(f"Sst{h}",))
                        P[0].op("act", lambda e, h=h: e.activation(out=Sbf[:, h, :], in_=Sst[:, h, :], func=AF.Copy),
                                reads=(f"Sst{h}",), writes=(f"Sbf{h}",))
                    if sample:
                        dma(ss_o[s].rearrange("h k v -> k h v"), Sst[:], ALLS, (), "Sst", q="pool")
                    for h in range(4):
                        ob = go[h // 2]
                        oc = slice((h % 2) * 256, (h % 2) * 256 + 256)
                        P[0].op("act", lambda e, ob=ob, oc=oc, h=h: e.activation(
                            out=junk[:, 0:256], in_=psG[ob][:, oc], func=AF.Square, accum_out=ss2[:, h:h + 1]),
                            reads=(f"psG{ob}",), writes=("junk", f"ss2{h}"))
                    P[0].op("dve", lambda e: e.tensor_scalar(out=rs2[:, 0:4], in0=ss2[:, 0:4], scalar1=1.0 / 256,
                                                             scalar2=EPS, op0=ALU.mult, op1=ALU.add),
                            reads=tuple(f"ss2{h}" for h in range(4)), writes=("rs2",))
                    P[0].op("act", lambda e: e.activation(out=rs2[:, 0:4], in_=rs2[:, 0:4], func=AF.Ln),
                            reads=("rs2",), writes=("rs2",))
                    P[0].op("act", lambda e: e.activation(out=rs2[:, 0:4], in_=rs2[:, 0:4], func=AF.Exp, scale=-0.5),
                            reads=("rs2",), writes=("rs2",))
                    for h in range(4):
                        ob = go[h // 2]
                        oc = slice((h % 2) * 256, (h % 2) * 256 + 256)
                        P[0].op("dve", lambda e, ob=ob, oc=oc, h=h: e.scalar_tensor_tensor(
                            out=og1[:, h * 256:(h + 1) * 256], in0=psG[ob][:, oc], scalar=rs2[:, h:h + 1],
                            in1=ggob[:, h * 256:(h + 1) * 256], op0=ALU.mult, op1=ALU.mult),
                            reads=(f"psG{ob}", "rs2", "ggob"), writes=("og1",))
                    dma(onscr[rowbase + s * 128:rowbase + (s + 1) * 128, :], og1[:], ("og1",), (), "og1")

                stage_a(0, 0)
                for s in range(4):
                    if s + 1 < 4:
                        stage_a(s + 1, (s + 1) % 2)
                    stage_b(s, s % 2)

            for i in range(NT):
                phase_d_tile(xp[i * 512:(i + 1) * 512, :], i * 512, i * 512, False)
            dma(sp_o.rearrange("h k v -> k h v"), Sst[:], ALLS, (), "Sst")
            phase_d_tile(xs[:, :], T, T, True)
            emit_phase("D")

        with contextlib.ExitStack() as esE:
            S = mkS(esE)
            P[0] = Prog()
            WO1 = S("WO1e", [128, 8, D], BF16)
            gfinb = S("gfinbe", [128, 1024], F32)
            on_t = [S(f"on_t{i}", [128, 4, D], BF16) for i in range(2)]
            sg_t = [S(f"sg_t{i}", [128, 4, D], BF16) for i in range(2)]
            yp_t = [S(f"yp_t{i}", [128, 4, D], F32) for i in range(2)]
            og1 = S("og1e", [128, 4, D], BF16)
            og1T = S("og1Te", [128, 8, 512], BF16)
            y2 = S("y2e", [128, 4, D], F32)
            yo_ = [S(f"yoe{i}", [128, 4, D], F32) for i in range(2)]
            ss2 = S("ss2e", [128, 8], F32)
            rs2 = S("rs2e", [128, 8], F32)
            dma(gfinb[:], gfin[:, :], (), ("gfinb",), "gfinb")
            wstE = S("wstE", [128, 1024], F32)
            load_weight(WO1, "WO1", wo1, D, None, None, wstE)
            nch = T // 64

            def phase_e_tile(p0, rowbase, perm, yout, orow, bi):
                ont, sgt, ypt_, yo = on_t[bi], sg_t[bi], yp_t[bi], yo_[bi]
                nm = (f"on_t{bi}", f"sg_t{bi}", f"yp_t{bi}", f"yo{bi}")
                if perm:
                    for sb_ in range(4):
                        done = 0
                        while done < 128:
                            p = p0 + sb_ * 128 + done
                            c, i0 = p // nch, p % nch
                            ln = min(nch - i0, 128 - done)
                            src = onscr[i0 * 64 + c:(i0 + ln - 1) * 64 + c + 1:64, :]
                            dma(ont[done:done + ln, sb_, :], src, (), (nm[0],), nm[0])
                            done += ln
                else:
                    dma(ont[:], onscr[rowbase + p0:rowbase + p0 + 512, :].rearrange("(s p) d -> p s d", p=128), (),
                        (nm[0],), nm[0])
                dma(sgt[:], sgscr[rowbase + p0:rowbase + p0 + 512, :].rearrange("(s p) d -> p s d", p=128), (),
                    (nm[1],), nm[1])
                dma(ypt_[:], ypscr[rowbase + p0:rowbase + p0 + 512, :].rearrange("(s p) d -> p s d", p=128), (),
                    (nm[2],), nm[2])
                for sb_ in range(4):
                    P[0].op("dve" if sb_ % 2 == 0 else "pool", lambda e, sb_=sb_: e.tensor_tensor(
                        out=og1[:, sb_, :], in0=ont[:, sb_, :], in1=sgt[:, sb_, :], op=ALU.mult),
                        reads=(nm[0], nm[1]), writes=("og1",))
                transpose_tok(lambda s_: og1[:, s_, :], "og1", 4, og1T, "og1T")
                for sb_ in range(4):
                    for nh in range(2):
                        g = gbank()
                        mm_group(f"psG{g}", [(psG[g][:, :], og1T[:, fc, sb_ * 128:(sb_ + 1) * 128],
                                              WO1[:, fc, nh * 512:(nh + 1) * 512], fc == 0, fc == 7, ("og1T", "WO1"))
                                             for fc in range(8)])
                        P[0].op("dve", lambda e, g=g, nh=nh, sb_=sb_: e.tensor_tensor(
                            out=y2[:, sb_, nh * 512:(nh + 1) * 512], in0=psG[g][:, :],
                            in1=ypt_[:, sb_, nh * 512:(nh + 1) * 512], op=ALU.add),
                            reads=(f"psG{g}", nm[2]), writes=("y2",))
                    P[0].op("act", lambda e, sb_=sb_: e.activation(out=junk[:, :], in_=y2[:, sb_, :], func=AF.Square,
                                                                   accum_out=ss2[:, sb_:sb_ + 1]),
                            reads=("y2",), writes=("junk", f"ss2{sb_}"))
                P[0].op("dve", lambda e: e.tensor_scalar(out=rs2[:, 0:4], in0=ss2[:, 0:4], scalar1=1.0 / D,
                                                         scalar2=EPS, op0=ALU.mult, op1=ALU.add),
                        reads=tuple(f"ss2{i}" for i in range(4)), writes=("rs2",))
                P[0].op("act", lambda e: e.activation(out=rs2[:, 0:4], in_=rs2[:, 0:4], func=AF.Ln), reads=("rs2",),
                        writes=("rs2",))
                P[0].op("act", lambda e: e.activation(out=rs2[:, 0:4], in_=rs2[:, 0:4], func=AF.Exp, scale=-0.5),
                        reads=("rs2",), writes=("rs2",))
                for sb_ in range(4):
                    P[0].op("dve", lambda e, sb_=sb_: e.scalar_tensor_tensor(
                        out=yo[:, sb_, :], in0=y2[:, sb_, :], scalar=rs2[:, sb_:sb_ + 1], in1=gfinb[:],
                        op0=ALU.mult, op1=ALU.mult), reads=("y2", "rs2", "gfinb"), writes=(nm[3],))
                dma(yout[orow:orow + 512, :].rearrange("(s p) d -> p s d", p=128), yo[:], (nm[3],), (), nm[3])

            for j in range(T // 512):
                phase_e_tile(j * 512, 0, True, yp_o, j * 512, j % 2)
            phase_e_tile(0, T, False, ys_o, 0, (T // 512) % 2)
            emit_phase("E")
    return nc


def make_in_maps(inp, T, ncores):
    f = np.float32
    w_in_fox = np.asarray(inp["w_in_fox"], f)
    wf = []
    for hg in range(4):
        cols = [w_in_fox[:, k * 1024 + hg * 256:k * 1024 + (hg + 1) * 256] for k in range(4)]
        cols.append(w_in_fox[:, 4096 + hg * 4:4096 + (hg + 1) * 4])
        wf.append(np.concatenate(cols, axis=1))
    wf = np.ascontiguousarray(np.stack(wf))

    def pc(v):
        return np.ascontiguousarray(np.asarray(v, f).reshape(8, 128).T)

    def rep(v):
        v = np.asarray(v, f)
        return np.ascontiguousarray(np.broadcast_to(v[None, :], (128, v.shape[0])))

    s_idx = np.arange(128)[:, None]
    t_idx = np.arange(128)[None, :]
    gsc = np.zeros((128, 2), f)
    gsc[:, 0] = -1.0 / 16
    gsc[:64, 1] = -1.0 / 16
    consts = {
        "c_ident": (s_idx == t_idx).astype(f), "c_trii": (s_idx <= t_idx).astype(f),
        "c_trir": (s_idx > t_idx).astype(f), "c_ones": np.ones((128, 128), f),
        "c_mneg": np.where(s_idx > t_idx, -30000.0, 0.0).astype(f), "c_gsc": gsc,
    }
    shared = {
        "wfox": wf, "gfox": pc(inp["g_norm_fox"]), "bff": rep(inp["b_fox_f"]),
        "wo0": np.ascontiguousarray(np.asarray(inp["w_out_fox"], f)), "ggla": pc(inp["g_norm_gla"]),
        "wgla": np.ascontiguousarray(np.asarray(inp["w_in_gla"], f)),
        "wa2": np.ascontiguousarray(np.asarray(inp["w_gla_a2"], f)), "bga": rep(inp["b_gla_a"]),
        "ggo": rep(np.tile(np.asarray(inp["g_gla_o"], f), 4)),
        "wo1": np.ascontiguousarray(np.asarray(inp["w_out_gla"], f)), "gfin": rep(inp["g_final"]),
    }
    shared.update(consts)
    xprompt = np.asarray(inp["x_prompt"], f)
    xsample = np.asarray(inp["x_sample"], f)
    nb = xprompt.shape[0]
    maps = []
    for c in range(ncores):
        sl = slice(c * SEQ_PER_CORE, (c + 1) * SEQ_PER_CORE)
        xs = np.zeros((SEQ_PER_CORE, 128, D), f)
        xs[:, :DEC, :] = xsample[sl]
        m = dict(shared)
        m["xp"] = np.ascontiguousarray(xprompt[c % nb])
        m["xs"] = xs.reshape(SEQ_PER_CORE * 128, D)
        m["ck"] = np.ascontiguousarray(np.asarray(inp["cache_fox_k"], f)[sl].reshape(SEQ_PER_CORE, PAST, 1024))
        m["cv"] = np.ascontiguousarray(np.asarray(inp["cache_fox_v"], f)[sl].reshape(SEQ_PER_CORE, PAST, 1024))
        m["clf"] = np.ascontiguousarray(np.asarray(inp["cache_fox_logf"], f)[sl])
        m["sg"] = np.ascontiguousarray(np.asarray(inp["state_gla"], f)[sl])
        maps.append(m)
    return maps


_NC_CACHE = {}


def run(inp, ncores=NCORES):
    xprompt = np.asarray(inp["x_prompt"])
    B, T, _ = xprompt.shape
    if T not in _NC_CACHE:
        _NC_CACHE[T] = build_nc(T)
    nc = _NC_CACHE[T]
    maps = make_in_maps(inp, T, ncores)
    res = run_bass_kernel_spmd(nc, maps, core_ids=list(range(ncores))).results
    f = np.float32
    nseq = ncores * SEQ_PER_CORE

    def samp(name, last):
        a = np.stack([res[c][name].reshape(SEQ_PER_CORE, 128, last)[:, :DEC] for c in range(ncores)])
        return a.reshape(nseq, DEC, last)

    y_prompt = np.stack([res[b]["yp"] for b in range(B)]).astype(f)
    y_sample = samp("ys", D).astype(f)
    kp = np.stack([res[b]["kp"] for b in range(B)]).reshape(B, T, 16, 64).astype(f)
    vp = np.stack([res[b]["vp"] for b in range(B)]).reshape(B, T, 16, 64).astype(f)
    lp = np.stack([res[b]["lp"] for b in range(B)]).astype(f)
    stp = np.stack([res[b]["stp"] for b in range(B)]).astype(f)
    ks = samp("ks", 1024).reshape(nseq, DEC, 16, 64).astype(f)
    vs = samp("vs", 1024).reshape(nseq, DEC, 16, 64).astype(f)
    ls = samp("ls", 16).astype(f)
    sts = np.concatenate([res[c]["sts"] for c in range(ncores)], axis=0).astype(f)
    return (y_prompt, y_sample, kp, vp, lp, stp, ks, vs, ls, sts)


def kernel(**inputs):
    return run(inputs)
```

```python
import numpy as np
import concourse.bass as bass
import concourse.mybir as mybir
from concourse.bass_utils import run_bass_kernel_spmd

F32 = mybir.dt.float32
BF16 = mybir.dt.bfloat16
AF = mybir.ActivationFunctionType
ALU = mybir.AluOpType

D = 1024
EPS = 1e-6
NCORES = 8
SEQ_PER_CORE = 4
PAST = 1024
DEC = 64


class Prog:
    ENG = ("pe", "act", "dve", "pool", "sp")

    def __init__(self):
        self.ops = []
        self.last_w = {}
        self.readers = {}

    def op(self, eng, fn, reads=(), writes=(), dma=None):
        oid = len(self.ops)
        deps = set()
        for r in reads:
            if r in self.last_w:
                deps.add(self.last_w[r])
        for w in writes:
            if w in self.last_w:
                deps.add(self.last_w[w])
            for rd in self.readers.get(w, ()):
                deps.add(rd)
        deps.discard(oid)
        for r in reads:
            self.readers.setdefault(r, []).append(oid)
        for w in writes:
            self.last_w[w] = oid
            self.readers[w] = []
        self.ops.append(dict(eng=eng, fn=fn, deps=deps, dma=dma, sig=False))
        return oid

    def emit(self, nc, block_engines, sems, dma_sems):
        ops = self.ops
        for o in ops:
            if o["eng"] == "pe" and o["dma"] is None:
                o["deps"] = {d for d in o["deps"] if not (ops[d]["eng"] == "pe" and ops[d]["dma"] is None)}
        for o in ops:
            for d in o["deps"]:
                ops[d]["sig"] = True
        lastop = {}
        for o in ops:
            if o["dma"] is None:
                lastop[o["eng"]] = o
        for o in lastop.values():
            o["sig"] = True
        cnt = {e: 0 for e in self.ENG}
        dcnt = {}
        for o in ops:
            if o["dma"] is not None:
                k = o["dma"]
                dcnt[k] = dcnt.get(k, 0) + 16
                o["semkey"] = ("dma", k)
                o["semval"] = dcnt[k]
            elif o["sig"]:
                cnt[o["eng"]] += 1
                o["semkey"] = ("eng", o["eng"])
                o["semval"] = cnt[o["eng"]]
        per_eng = {e: [] for e in self.ENG}
        for o in ops:
            per_eng[o["eng"]].append(o)

        def run(engname, eng):
            waited = {}
            for o in per_eng[engname]:
                need = {}
                for d in o["deps"]:
                    dk, dv = ops[d]["semkey"], ops[d]["semval"]
                    if need.get(dk, 0) < dv:
                        need[dk] = dv
                pend = [(dk, dv) for dk, dv in need.items() if waited.get(dk, 0) < dv]
                attach = None
                if engname == "pe" and pend:
                    latest = max((d for d in o["deps"]), key=lambda d: d)
                    lk = ops[latest]["semkey"]
                    for it in pend:
                        if it[0] == lk:
                            attach = it
                    if attach is None:
                        attach = pend[-1]
                    pend = [it for it in pend if it is not attach]
                for dk, dv in pend:
                    s = dma_sems[dk[1]] if dk[0] == "dma" else sems[dk[1]]
                    eng.wait_ge(s, dv)
                    waited[dk] = dv
                ins = o["fn"](eng)
                if isinstance(ins, tuple):
                    first_ins, ins = ins
                else:
                    first_ins = ins
                if attach is not None:
                    dk, dv = attach
                    s = dma_sems[dk[1]] if dk[0] == "dma" else sems[dk[1]]
                    first_ins._wait_ge(s, dv)
                    waited[dk] = dv
                if o["dma"] is not None:
                    ins.then_inc(dma_sems[o["dma"]], 16)
                elif o["sig"]:
                    ins.then_inc(sems[engname], 1)
            for k, v in dcnt.items():
                eng.wait_ge(dma_sems[k], v)
            for e2, v in cnt.items():
                if v > 0:
                    eng.wait_ge(sems[e2], v)

        for engname, reg in block_engines.items():
            reg(lambda eng, _n=engname: run(_n, eng))


def build_nc(T):
    import contextlib
    NT = T // 512
    NBLK = max(T // 128, SEQ_PER_CORE * 12 + 4)
    TS = 512
    NG, NH, NP = 4, 4, 2
    GW = NH * 64
    nc = bass.Bass("TRN2", target_bir_lowering=False)

    def din(name, shape, dt=F32):
        return nc.dram_tensor(name, list(shape), dt, kind="ExternalInput").ap()

    def dout(name, shape, dt=F32):
        return nc.dram_tensor(name, list(shape), dt, kind="ExternalOutput").ap()

    xp = din("xp", [T, D]); xs = din("xs", [TS, D])
    ck = din("ck", [SEQ_PER_CORE, PAST, 1024]); cv = din("cv", [SEQ_PER_CORE, PAST, 1024])
    clf = din("clf", [SEQ_PER_CORE, PAST, 16]); sg = din("sg", [SEQ_PER_CORE, 4, 128, 256])
    wfox = din("wfox", [NG, D, 4 * GW + NH]); gfox = din("gfox", [128, 8]); bff = din("bff", [128, 16])
    wo0 = din("wo0", [D, D]); ggla = din("ggla", [128, 8]); wgla = din("wgla", [D, 3088])
    wa2 = din("wa2", [16, 512]); bga = din("bga", [128, 512]); ggo = din("ggo", [128, 1024])
    wo1 = din("wo1", [D, D]); gfin = din("gfin", [128, 1024])
    c_ident = din("c_ident", [128, 128]); c_trii = din("c_trii", [128, 128]); c_trir = din("c_trir", [128, 128])
    c_ones = din("c_ones", [128, 128]); c_mneg = din("c_mneg", [128, 128]); c_gsc = din("c_gsc", [128, 2])

    yp_o = dout("yp", [T, D]); ys_o = dout("ys", [TS, D])
    kp_o = dout("kp", [T, 1024]); vp_o = dout("vp", [T, 1024]); lp_o = dout("lp", [T, 16])
    sp_o = dout("stp", [4, 128, 256])
    ks_o = dout("ks", [TS, 1024]); vs_o = dout("vs", [TS, 1024]); ls_o = dout("ls", [TS, 16])
    ss_o = dout("sts", [SEQ_PER_CORE, 4, 128, 256])
    onscr = nc.dram_tensor("onscr", [T + TS, 1024], BF16).ap()
    sgscr = nc.dram_tensor("sgscr", [T + TS, 1024], BF16).ap()
    ypscr = nc.dram_tensor("ypscr", [T + TS, 1024], F32).ap()
    hTscr = nc.dram_tensor("hTscr", [T // 512 + 1, 128, 8 * 512], BF16).ap()
    ogs = nc.dram_tensor("ogscr", [1024, T + TS], BF16).ap()

    with contextlib.ExitStack() as es:
        def mkS(stack):
            def S(name, shape, dt):
                return stack.enter_context(nc.sbuf_tensor(name, list(shape), dt))
            return S

        S = mkS(es)
        wst = S("wst", [128, 1024], F32)
        hb = S("hb", [128, 4, D], BF16)
        hT = S("hT", [128, 8, 512], BF16)
        junk = S("junk", [128, D], BF16)
        ssq = S("ssq", [128, 8], F32)
        rstd = S("rstd", [128, 8], F32)
        ident = S("ident", [128, 128], BF16)
        identf = S("identf", [128, 128], F32)
        trii = S("trii", [128, 128], F32)
        trir = S("trir", [128, 128], F32)
        onesf = S("onesf", [128, 128], F32)
        mneg = S("mneg", [128, 128], BF16)
        mnegf = S("mnegf", [128, 128], F32)
        gsc = S("gsc", [128, 2], F32)
        gfx = S("gfx", [128, 8], F32)
        ggl = S("ggl", [128, 8], F32)
        bfb = S("bfb", [128, 16], F32)
        psG = [es.enter_context(nc.psum_tensor(f"psG{i}", [128, 512], F32)) for i in range(8)]
        psT = {6: psG[6][:, :].bitcast(BF16), 7: psG[7][:, :].bitcast(BF16)}
        rr = {"g": 0, "t": 0}

        def gbank():
            i = rr["g"] % 6
            rr["g"] += 1
            return i

        def tbank():
            i = 6 + rr["t"] % 2
            rr["t"] += 1
            return i

        P = [None]

        def dma(out, in_, reads, writes, key, q=None):
            if q is None:
                q = "pool" if (reads and not writes) or (writes == ("ogscr",)) else "sp"
            P[0].op(q, lambda e, o=out, i=in_: e.dma_start(out=o, in_=i), reads=reads, writes=writes, dma=key)

        def mm_group(psname, mms):
            reads = set()
            for m in mms:
                reads.update(m[5])

            def fn(e, mms=mms):
                ins = None
                first = None
                for (o, l, r, st, sp_, _) in mms:
                    ins = e.matmul(o, lhsT=l, rhs=r, start=st, stop=sp_)
                    if first is None:
                        first = ins
                return (first, ins)
            P[0].op("pe", fn, reads=tuple(reads), writes=(psname,))

        wrr = [0]

        def load_weight(dst, dstname, src2d, ncols, gname, gt, wst2=None):
            for kc in range(8):
                for c0 in range(0, ncols, 1024):
                    c1 = min(ncols, c0 + 1024)
                    wi = 0
                    if wst2 is not None:
                        wi = wrr[0] % 2
                        wrr[0] += 1
                    wb = wst if wi == 0 else wst2
                    wn = "wst" if wi == 0 else "wst2"
                    dma(wb[:, 0:c1 - c0], src2d[kc * 128:(kc + 1) * 128, c0:c1], (), (wn,), wn)
                    if gt is None:
                        P[0].op("dve", lambda e, kc=kc, c0=c0, c1=c1, wb=wb: e.tensor_copy(
                            out=dst[:, kc, c0:c1], in_=wb[:, 0:c1 - c0]), reads=(wn,), writes=(dstname,))
                    elif wi == 0:
                        P[0].op("act", lambda e, kc=kc, c0=c0, c1=c1, wb=wb: e.activation(
                            out=dst[:, kc, c0:c1], in_=wb[:, 0:c1 - c0], func=AF.Copy, scale=gt[:, kc:kc + 1]),
                            reads=(wn, gname), writes=(dstname,))
                    else:
                        P[0].op("dve", lambda e, kc=kc, c0=c0, c1=c1, wb=wb: e.tensor_scalar(
                            out=dst[:, kc, c0:c1], in0=wb[:, 0:c1 - c0], scalar1=gt[:, kc:kc + 1], scalar2=None,
                            op0=ALU.mult), reads=(wn, gname), writes=(dstname,))

        def rms_rstd(src, srcname, nsub, n_el):
            for s in range(nsub):
                P[0].op("act", lambda e, s=s: e.activation(out=junk[:, 0:n_el], in_=src(s), func=AF.Square,
                                                           accum_out=ssq[:, s:s + 1]),
                        reads=(srcname,), writes=("junk", "ssq" + str(s)))
            P[0].op("dve", lambda e: e.tensor_scalar(out=rstd[:, 0:nsub], in0=ssq[:, 0:nsub], scalar1=1.0 / n_el,
                                                     scalar2=EPS, op0=ALU.mult, op1=ALU.add),
                    reads=tuple("ssq" + str(s) for s in range(nsub)), writes=("rstd",))
            P[0].op("act", lambda e: e.activation(out=rstd[:, 0:nsub], in_=rstd[:, 0:nsub], func=AF.Ln), reads=("rstd",), writes=("rstd",))
            P[0].op("act", lambda e: e.activation(out=rstd[:, 0:nsub], in_=rstd[:, 0:nsub], func=AF.Exp, scale=-0.5), reads=("rstd",), writes=("rstd",))

        def transpose_tok(src_fn, srcname, nsub, dst, dstname):
            for kc in range(8):
                tb = tbank()

                def fn(e, kc=kc, tb=tb):
                    ins = None
                    first = None
                    for s in range(nsub):
                        ins = e.transpose(psT[tb][:, s * 128:(s + 1) * 128], src_fn(s)[:, kc * 128:(kc + 1) * 128],
                                          ident[:])
                        if first is None:
                            first = ins
                    return (first, ins)
                P[0].op("pe", fn, reads=(srcname, "ident"), writes=(f"psG{tb}",))
                P[0].op("dve", lambda e, kc=kc, tb=tb: e.tensor_copy(out=dst[:, kc, 0:nsub * 128],
                                                                    in_=psT[tb][:, 0:nsub * 128]),
                        reads=(f"psG{tb}",), writes=(dstname,))

        def emit_phase(tag):
            dma_keys = sorted({o["dma"] for o in P[0].ops if o["dma"] is not None})
            sems = {e: es.enter_context(nc.semaphore(f"s{tag}_{e}")) for e in Prog.ENG}
            dsems = {k: es.enter_context(nc.semaphore(f"d{tag}_{k}")) for k in dma_keys}
            with nc.Block() as block:
                P[0].emit(nc, {"pe": block.tensor, "act": block.scalar, "dve": block.vector, "pool": block.gpsimd,
                               "sp": block.sync}, sems, dsems)

        with contextlib.ExitStack() as esA:
            S = mkS(esA)
            P[0] = Prog()
            KA = [S(f"KA{h}", [68, NBLK * 128], BF16) for h in range(NH)]
            VA = S("VA", [128, NBLK, NH, 65], BF16)
            cT = S("cT", [128, NBLK, NH], F32)
            carry = S("carry", [128, NBLK + 1, NH], F32)
            biasT = S("biasT", [128, NBLK, NH], F32)
            W0 = S("W0", [128, 8, 4 * GW + NH], BF16)
            xt = S("xt", [128, 4, D], F32)
            QA = [S(f"QA{i}", [68, NH, 512], BF16) for i in range(2)]
            csp = [S(f"csp{i}", [NH, 512], F32) for i in range(2)]
            csb = [S(f"csb{i}", [NH, 512], BF16) for i in range(3)]
            ctmp = S("ctmp", [128, 4, NH], F32)
            arow = S("arow", [NH, 512], BF16)
            sgT = [S(f"sgT{i}", [128, NP, 512], BF16) for i in range(2)]
            ktok = S("ktok", [128, 4, GW], F32)
            ktb = S("ktb", [128, 4, GW], BF16)
            vtok = S("vtok", [128, 4, GW], F32)
            lft = S("lft", [128, 4, NH], F32)
            lfe = S("lfe", [128, 4, NH], F32)
            PT = [S(f"PT{i}", [128, 512], BF16) for i in range(5)]
            rd = S("rd", [128, 512], F32)
            otmp = S("otmp", [64, 512], F32)
            ogst = [S(f"ogst{i}", [64, NH, 512], BF16) for i in range(2)]
            ckt = S("ckt", [128, GW], F32)
            ckb = S("ckb", [128, GW], BF16)
            clt = S("clt", [128, 8, NH], F32)

            dma(identf[:], c_ident[:, :], (), ("identf",), "identf")
            dma(trii[:], c_trii[:, :], (), ("trii",), "trii")
            dma(trir[:], c_trir[:, :], (), ("trir",), "trir")
            dma(onesf[:], c_ones[:, :], (), ("onesf",), "onesf")
            dma(gsc[:], c_gsc[:, :], (), ("gsc",), "gsc")
            dma(gfx[:], gfox[:, :], (), ("gfx",), "gfx")
            dma(ggl[:], ggla[:, :], (), ("ggl",), "ggl")
            dma(bfb[:], bff[:, :], (), ("bfb",), "bfb")
            dma(mnegf[:], c_mneg[:, :], (), ("mnegf",), "mnegf")
            P[0].op("dve", lambda e: e.tensor_copy(out=ident[:], in_=identf[:]), reads=("identf",), writes=("ident",))
            P[0].op("dve", lambda e: e.tensor_copy(out=mneg[:], in_=mnegf[:]), reads=("mnegf",), writes=("mneg",))

            def KAn(h, blk):
                return f"KA{h}_{blk // 4}"

            def VAn(blk):
                return f"VA_{blk // 4}"
            NTB = NBLK // 4
            ALLKA = [tuple(f"KA{h}_{t}" for t in range(NTB)) for h in range(NH)]
            ALLVA = tuple(f"VA_{t}" for t in range(NTB))
            pa_rr = [0]

            def pbank():
                i = 6 + pa_rr[0] % 2
                pa_rr[0] += 1
                return i

            def transpose_tok_a(src_fn, srcname, nsub, dst, dstname):
                for kc in range(8):
                    tb = pbank()

                    def fn(e, kc=kc, tb=tb):
                        ins = None
                        first = None
                        for s in range(nsub):
                            ins = e.transpose(psT[tb][:, s * 128:(s + 1) * 128],
                                              src_fn(s)[:, kc * 128:(kc + 1) * 128], ident[:])
                            if first is None:
                                first = ins
                        return (first, ins)
                    P[0].op("pe", fn, reads=(srcname, "ident"), writes=(f"psG{tb}",))
                    P[0].op("dve", lambda e, kc=kc, tb=tb: e.tensor_copy(out=dst[:, kc, 0:nsub * 128],
                                                                        in_=psT[tb][:, 0:nsub * 128]),
                            reads=(f"psG{tb}",), writes=(dstname,))
                    yield

            def fox_inproj_gen(hg, xsrc, blk0, kout, vout, lout, row0, par, do_cumsum, tix):
                QAp, sgTp = QA[par], sgT[par]
                qan, sgn = f"QA{par}", f"sgT{par}"
                kt_col0 = blk0 * 128
                if hg == 0:
                    dma(xt[:], xsrc.rearrange("(s p) d -> p s d", p=128), (), ("xt",), "xt")
                    yield
                    rms_rstd(lambda s: xt[:, s, :], "xt", 4, D)
                    yield
                    for s in range(4):
                        P[0].op("dve", lambda e, s=s: e.tensor_scalar(out=hb[:, s, :], in0=xt[:, s, :],
                                                                      scalar1=rstd[:, s:s + 1], scalar2=None,
                                                                      op0=ALU.mult),
                                reads=("xt", "rstd"), writes=("hb",))
                        yield
                    yield from transpose_tok_a(lambda s: hb[:, s, :], "hb", 4, hT, "hT")
                    dma(hTscr[tix].rearrange("p (c t) -> p c t", c=8), hT[:], ("hT",), ("hTscr",), "hTst", q="pool")
                else:
                    dma(hT[:], hTscr[tix].rearrange("p (c t) -> p c t", c=8), ("hTscr",), ("hT",), "hT", q="sp")
                    yield
                for kind in (0, 2):
                    for pr in range(NP):
                        col = {0: 0, 1: GW, 2: 3 * GW}[kind] + pr * 128
                        g = pbank()
                        mm_group(f"psG{g}", [(psG[g][:, :], W0[:, kc, col:col + 128], hT[:, kc, :], kc == 0, kc == 7,
                                              ("W0", "hT")) for kc in range(8)])
                        if kind == 0:
                            for hf in range(2):
                                P[0].op("dve", lambda e, g=g, pr=pr, hf=hf: e.tensor_scalar(
                                    out=QAp[0:64, 2 * pr + hf, :], in0=psG[g][hf * 64:hf * 64 + 64, :],
                                    scalar1=0.125, scalar2=None, op0=ALU.mult),
                                    reads=(f"psG{g}",), writes=(qan,))
                        elif kind == 1:
                            for hf in range(2):
                                P[0].op("dve", lambda e, g=g, pr=pr, hf=hf: e.tensor_copy(
                                    out=KA[2 * pr + hf][0:64, kt_col0:kt_col0 + 512],
                                    in_=psG[g][hf * 64:hf * 64 + 64, :]),
                                    reads=(f"psG{g}",), writes=(KAn(2 * pr + hf, blk0),))
                        else:
                            P[0].op("act", lambda e, g=g, pr=pr: e.activation(out=sgTp[:, pr, :], in_=psG[g][:, :],
                                                                             func=AF.Silu),
                                    reads=(f"psG{g}",), writes=(sgn,))
                        yield
                for s in range(4):
                    g = pbank()
                    mm_group(f"psG{g}", [(psG[g][:, 0:2 * GW], hT[:, kc, s * 128:(s + 1) * 128],
                                          W0[:, kc, GW:3 * GW], kc == 0, kc == 7, ("W0", "hT")) for kc in range(8)])
                    P[0].op("dve", lambda e, g=g, s=s: e.tensor_copy(out=ktok[:, s, :], in_=psG[g][:, 0:GW]),
                            reads=(f"psG{g}",), writes=("ktok",))
                    P[0].op("act", lambda e, s=s: e.activation(out=ktb[:, s, :], in_=ktok[:, s, :], func=AF.Copy),
                            reads=("ktok",), writes=("ktb",))
                    P[0].op("dve", lambda e, g=g, s=s: e.tensor_copy(out=vtok[:, s, :], in_=psG[g][:, GW:2 * GW]),
                            reads=(f"psG{g}",), writes=("vtok",))
                    P[0].op("pool", lambda e, s=s: e.tensor_copy(out=VA[:, blk0 + s, :, 0:64],
                                                                 in_=vtok[:, s, :].rearrange("p (h d) -> p h d", h=NH)),
                            reads=("vtok",), writes=(VAn(blk0),))
                    yield
                    g = pbank()
                    mm_group(f"psG{g}", [(psG[g][:, 0:NH], hT[:, kc, s * 128:(s + 1) * 128],
                                          W0[:, kc, 4 * GW:4 * GW + NH], kc == 0, kc == 7, ("W0", "hT"))
                                         for kc in range(8)])
                    P[0].op("dve", lambda e, g=g, s=s: e.tensor_tensor(out=lfe[:, s, :], in0=psG[g][:, 0:NH],
                                                                       in1=bfb[:, hg * NH:(hg + 1) * NH], op=ALU.add),
                            reads=(f"psG{g}", "bfb"), writes=("lfe",))
                    yield
                for pr in range(NP):
                    tb = pbank()

                    def fn_kt(e, tb=tb, pr=pr):
                        ins = None
                        first = None
                        for s in range(4):
                            ins = e.transpose(psT[tb][:, s * 128:(s + 1) * 128],
                                              ktb[:, s, pr * 128:(pr + 1) * 128], ident[:])
                            if first is None:
                                first = ins
                        return (first, ins)
                    P[0].op("pe", fn_kt, reads=("ktb", "ident"), writes=(f"psG{tb}",))
                    for hf in range(2):
                        h = 2 * pr + hf
                        P[0].op("dve", lambda e, tb=tb, h=h, hf=hf: e.tensor_copy(
                            out=KA[h][0:64, kt_col0:kt_col0 + 512], in_=psT[tb][hf * 64:hf * 64 + 64, 0:512]),
                            reads=(f"psG{tb}",), writes=(KAn(h, blk0),))
                yield
                P[0].op("act", lambda e: e.activation(out=lfe[:], in_=lfe[:], func=AF.Exp, scale=-1.0),
                        reads=("lfe",), writes=("lfe",))
                P[0].op("act", lambda e: e.activation(out=lfe[:], in_=lfe[:], func=AF.Ln, bias=1.0),
                        reads=("lfe",), writes=("lfe",))
                P[0].op("dve", lambda e: e.tensor_scalar(out=lft[:], in0=lfe[:], scalar1=-1.0, scalar2=None,
                                                         op0=ALU.mult),
                        reads=("lfe",), writes=("lft",))
                yield
                dst3 = lambda o, w: o[row0:row0 + 512, hg * w:(hg + 1) * w].rearrange("(s p) c -> p s c", p=128)
                dma(dst3(kout, GW), ktok[:], ("ktok",), (), "ktok")
                dma(dst3(vout, GW), vtok[:], ("vtok",), (), "vtok")
                dma(dst3(lout, NH), lft[:], ("lft",), (), "lft")
                yield
                if do_cumsum:
                    for s in range(4):
                        cumsum_block(lft[:, s, :], "lft", blk0 + s)
                        yield
                    arow_prep(par, 0, 512, blk0 + 2, blk0, prompt=True)
                    yield

            def drain(gen):
                if gen is not None:
                    for _ in gen:
                        pass

            def cumsum_block(lsrc, lname, blk):
                g = pbank()
                mm_group(f"psG{g}", [(psG[g][:, 0:NH], trii[:, :], lsrc, True, True, (lname, "trii")),
                                     (psG[g][:, 8:8 + NH], onesf[:, :], lsrc, True, True, (lname, "onesf"))])
                P[0].op("dve", lambda e, g=g: e.tensor_tensor(out=cT[:, blk, :], in0=psG[g][:, 0:NH],
                                                              in1=carry[:, blk, :], op=ALU.add),
                        reads=(f"psG{g}", "carry"), writes=("cT",))
                P[0].op("dve", lambda e, g=g: e.tensor_tensor(out=carry[:, blk + 1, :], in0=psG[g][:, 8:8 + NH],
                                                              in1=carry[:, blk, :], op=ALU.add),
                        reads=(f"psG{g}", "carry"), writes=("carry",))

            def arow_prep(par, qc0, nq, ref_idx, qblk0, prompt=False):
                QAp, qan = QA[par], f"QA{par}"
                nqb = nq // 128
                if prompt:
                    src_fn = lambda qb: cT[:, qblk0 + qb, :]
                    srcname = "cT"
                else:
                    P[0].op("dve", lambda e: e.tensor_tensor(
                        out=ctmp[:, 0:nqb, :], in0=cT[:, qblk0:qblk0 + nqb, :],
                        in1=carry[:, ref_idx:ref_idx + 1, :].to_broadcast([128, nqb, NH]), op=ALU.subtract),
                        reads=("cT", "carry"), writes=("ctmp",))
                    src_fn = lambda qb: ctmp[:, qb, :]
                    srcname = "ctmp"
                ga_ = pbank()

                def fn_tr(e, ga_=ga_):
                    ins = None
                    first = None
                    for qb in range(nqb):
                        ins = e.transpose(psG[ga_][0:NH, qb * 128:(qb + 1) * 128], src_fn(qb), identf[:])
                        if first is None:
                            first = ins
                    return (first, ins)
                P[0].op("pe", fn_tr, reads=(srcname, "identf"), writes=(f"psG{ga_}",))
                P[0].op("dve", lambda e, ga_=ga_: e.tensor_copy(out=arow[:, 0:nq], in_=psG[ga_][0:NH, 0:nq]),
                        reads=(f"psG{ga_}",), writes=("arow",))
                for hh in range(NH):
                    dma(QAp[64:65, hh, qc0:qc0 + nq], arow[hh:hh + 1, 0:nq], ("arow",), (qan,), "arow", q="sp")
                if prompt:
                    n1, r1 = csp
                    hi, mid, lo = csb
                    P[0].op("dve", lambda e, ga_=ga_: e.tensor_scalar(out=n1[:], in0=psG[ga_][0:NH, 0:512],
                                                                      scalar1=-1.0, scalar2=None, op0=ALU.mult),
                            reads=(f"psG{ga_}",), writes=("csp0",))
                    P[0].op("dve", lambda e: e.tensor_copy(out=hi[:], in_=n1[:]), reads=("csp0",), writes=("csb0",))
                    P[0].op("dve", lambda e: e.tensor_tensor(out=r1[:], in0=n1[:], in1=hi[:], op=ALU.subtract),
                            reads=("csp0", "csb0"), writes=("csp1",))
                    P[0].op("dve", lambda e: e.tensor_copy(out=mid[:], in_=r1[:]), reads=("csp1",), writes=("csb1",))
                    P[0].op("dve", lambda e: e.tensor_tensor(out=n1[:], in0=r1[:], in1=mid[:], op=ALU.subtract),
                            reads=("csp1", "csb1"), writes=("csp0",))
                    P[0].op("dve", lambda e: e.tensor_copy(out=lo[:], in_=n1[:]), reads=("csp0",), writes=("csb2",))
                    c0 = qblk0 * 128
                    for hh in range(NH):
                        for j in range(3):
                            dma(KA[hh][65 + j:66 + j, c0:c0 + 512], csb[j][hh:hh + 1, :], (f"csb{j}",),
                                (KAn(hh, qblk0),), f"csb{j}", q="sp")

            pt_rr = [0]
            s_rr = [0]
            LA = 3

            def attention_qtile(hg, qc0, nq, kblocks, ref_idx, scr_col0, stbuf, qblk0, par, filler=None, use_bias=True):
                QAp, sgTp = QA[par], sgT[par]
                qan, sgn = f"QA{par}", f"sgT{par}"
                blks = [b for (b, _, _) in kblocks]
                b_lo, b_hi = min(blks), max(blks) + 1
                if use_bias:
                    P[0].op("dve", lambda e: e.tensor_tensor(
                        out=biasT[:, b_lo:b_hi, :],
                        in0=carry[:, ref_idx:ref_idx + 1, :].to_broadcast([128, b_hi - b_lo, NH]),
                        in1=cT[:, b_lo:b_hi, :], op=ALU.subtract), reads=("carry", "cT"), writes=("biasT",))
                og = ogst[stbuf]
                ogname = f"ogst{stbuf}"
                nkb = len(kblocks)
                units = [(hh, bi) for hh in range(NH) for bi in range(nkb)]
                nu = len(units)
                sb_of, pt_of = {}, {}

                def issue_mm1(u):
                    hh, bi = units[u]
                    blk, qoff, masked = kblocks[bi]
                    n = nq - qoff
                    gs_ = s_rr[0] % 4
                    s_rr[0] += 1
                    sb_of[u] = gs_
                    mms = [(psG[gs_][:, 0:n], KA[hh][0:68, blk * 128:(blk + 1) * 128],
                            QAp[0:68, hh, qc0 + qoff:qc0 + nq], True, not masked, (KAn(hh, blk), qan))]
                    if masked:
                        mms.append((psG[gs_][:, 0:128], ident[:, :], mneg[:, :], False, True, ("ident", "mneg")))
                    mm_group(f"psG{gs_}", mms)

                def issue_exp(u):
                    hh, bi = units[u]
                    blk, qoff, masked = kblocks[bi]
                    n = nq - qoff
                    gs_ = sb_of[u]
                    pi = pt_rr[0] % 5
                    pt_rr[0] += 1
                    pt_of[u] = pi
                    if use_bias:
                        P[0].op("act", lambda e: e.activation(
                            out=PT[pi][:, 0:n], in_=psG[gs_][:, 0:n], func=AF.Exp, bias=biasT[:, blk, hh:hh + 1],
                            scale=1.0), reads=(f"psG{gs_}", "biasT"), writes=(f"PT{pi}",))
                    else:
                        P[0].op("act", lambda e: e.activation(
                            out=PT[pi][:, 0:n], in_=psG[gs_][:, 0:n], func=AF.Exp),
                            reads=(f"psG{gs_}",), writes=(f"PT{pi}",))

                def issue_mm2(u):
                    hh, bi = units[u]
                    blk, qoff, masked = kblocks[bi]
                    n = nq - qoff
                    go = 4 + hh % 2
                    pi = pt_of[u]
                    mm_group(f"psG{go}", [(psG[go][0:65, qoff:nq], VA[:, blk, hh, :], PT[pi][:, 0:n], bi == 0,
                                           bi == nkb - 1, (VAn(blk), f"PT{pi}"))])

                def norm_a(hh):
                    go = 4 + hh % 2
                    P[0].op("dve", lambda e: e.reciprocal(out=rd[64:65, 0:nq], in_=psG[go][64:65, 0:nq]),
                            reads=(f"psG{go}",), writes=("rd",))

                def norm_b(hh):
                    go = 4 + hh % 2
                    pr, pb = hh // 2, (hh % 2) * 64
                    gb = s_rr[0] % 4
                    s_rr[0] += 1
                    mm_group(f"psG{gb}", [(psG[gb][0:64, 0:nq], onesf[64:65, 0:64], rd[64:65, 0:nq], True, True,
                                           ("onesf", "rd"))])
                    P[0].op("dve", lambda e: e.tensor_tensor(
                        out=otmp[:, 0:nq], in0=psG[go][0:64, 0:nq], in1=sgTp[pb:pb + 64, pr, qc0:qc0 + nq],
                        op=ALU.mult), reads=(f"psG{go}", sgn), writes=("otmp",))
                    P[0].op("dve", lambda e: e.tensor_tensor(
                        out=og[:, hh, 0:nq], in0=psG[gb][0:64, 0:nq], in1=otmp[:, 0:nq], op=ALU.mult),
                        reads=(f"psG{gb}", "otmp"), writes=(ogname,))

                stride = max(1, nu // 48)
                pending = []
                for u in range(min(LA, nu)):
                    issue_mm1(u)
                for u in range(nu):
                    issue_exp(u)
                    if u + LA < nu:
                        issue_mm1(u + LA)
                    issue_mm2(u)
                    for it in pending:
                        it[0] -= 1
                    while pending and pending[0][0] <= 0:
                        norm_b(pending.pop(0)[1])
                    hh, bi = units[u]
                    if bi == nkb - 1:
                        norm_a(hh)
                        pending.append([min(8, nkb), hh])
                    if filler is not None and u % stride == 0:
                        next(filler, None)
                while pending:
                    norm_b(pending.pop(0)[1])
                dst = ogs[hg * GW:(hg + 1) * GW, scr_col0:scr_col0 + nq].rearrange("(h p) t -> p h t", p=64)
                dma(dst, og[:, :, 0:nq], (ogname,), ("ogscr",), ogname)

            P[0].op("pool", lambda e: e.memset(VA[:, :, :, 64:65], 1.0), reads=(), writes=ALLVA)
            for h in range(NH):
                P[0].op("pool", lambda e, h=h: e.memset(KA[h][64:68, :], 1.0), reads=(), writes=ALLKA[h])
            for hg in range(NG):
                load_weight(W0, "W0", wfox[hg], 4 * GW + NH, "gfx", gfx)
                P[0].op("pool", lambda e: e.memset(carry[:, 0, :], 0.0), reads=(), writes=("carry",))
                for par_ in range(2):
                    P[0].op("pool", lambda e, par_=par_: e.memset(QA[par_][64:68, :, :], 1.0), reads=(),
                            writes=(f"QA{par_}",))
                drain(fox_inproj_gen(hg, xp[0:512, :], 0, kp_o, vp_o, lp_o, 0, 0, True, 0))
                for i in range(NT):
                    nxt = None
                    if i + 1 < NT:
                        nxt = fox_inproj_gen(hg, xp[(i + 1) * 512:(i + 2) * 512, :], (i + 1) * 4, kp_o, vp_o, lp_o,
                                             (i + 1) * 512, (i + 1) % 2, True, i + 1)
                    kb = [(j, 0, False) for j in range(4 * i)] + [(4 * i + jj, 128 * jj, True) for jj in range(4)]
                    attention_qtile(hg, 0, 512, kb, 4 * i + 2, i * 512, i % 2, 4 * i, i % 2, nxt, use_bias=False)
                    drain(nxt)
                for sq in range(SEQ_PER_CORE):
                    base = sq * 12
                    P[0].op("pool", lambda e, base=base: e.memset(carry[:, base, :], 0.0), reads=(),
                            writes=("carry",))
                    dma(clt[:], clf[sq, :, hg * NH:(hg + 1) * NH].rearrange("(b p) h -> p b h", p=128), (), ("clt",),
                        "clt")
                    xk = xt[:, 0:2, :].rearrange("p a (b c) -> p (a b) c", c=GW)
                    xv = xt[:, 2:4, :].rearrange("p a (b c) -> p (a b) c", c=GW)
                    hk = hb[:, 0:2, :].rearrange("p a (b c) -> p (a b) c", c=GW)
                    dma(xk, ck[sq, :, hg * GW:(hg + 1) * GW].rearrange("(b p) c -> p b c", p=128), (), ("xt",), "xt")
                    dma(xv, cv[sq, :, hg * GW:(hg + 1) * GW].rearrange("(b p) c -> p b c", p=128), (), ("xt",), "xt")
                    P[0].op("dve", lambda e: e.tensor_copy(out=hb[:, 0:2, :], in_=xt[:, 0:2, :]), reads=("xt",),
                            writes=("hb",))
                    P[0].op("act", lambda e, base=base, xv=xv: e.activation(
                        out=VA[:, base:base + 8, :, 0:64], in_=xv.rearrange("p b (h d) -> p b h d", h=NH),
                        func=AF.Copy), reads=("xt",), writes=(VAn(base), VAn(base + 4)))
                    for b in range(8):
                        cumsum_block(clt[:, b, :], "clt", base + b)
                    for half in range(2):
                        for pr in range(NP):
                            tb = pbank()

                            def fn(e, tb=tb, half=half, pr=pr, hk=hk):
                                ins = None
                                first = None
                                for j in range(4):
                                    ins = e.transpose(psT[tb][:, j * 128:(j + 1) * 128],
                                                      hk[:, 4 * half + j, pr * 128:(pr + 1) * 128], ident[:])
                                    if first is None:
                                        first = ins
                                return (first, ins)
                            P[0].op("pe", fn, reads=("hb", "ident"), writes=(f"psG{tb}",))
                            for hf in range(2):
                                h = 2 * pr + hf
                                c0 = (base + 4 * half) * 128
                                P[0].op("dve", lambda e, tb=tb, h=h, hf=hf, c0=c0: e.tensor_copy(
                                    out=KA[h][0:64, c0:c0 + 512], in_=psT[tb][hf * 64:hf * 64 + 64, 0:512]),
                                    reads=(f"psG{tb}",), writes=(KAn(h, base + 4 * half),))
                P[0].op("pool", lambda e: e.memset(QA[0][64:68, :, :], 0.0), reads=(), writes=("QA0",))
                drain(fox_inproj_gen(hg, xs[:, :], NBLK - 4, ks_o, vs_o, ls_o, 0, 0, False, T // 512))
                for sq in range(SEQ_PER_CORE):
                    base = sq * 12
                    src_blk = NBLK - 4 + sq
                    for h in range(NH):
                        P[0].op("pool", lambda e, base=base, src_blk=src_blk, h=h: e.tensor_copy(
                            out=KA[h][0:64, (base + 8) * 128:(base + 9) * 128],
                            in_=KA[h][0:64, src_blk * 128:(src_blk + 1) * 128]), reads=(KAn(h, src_blk),),
                            writes=(KAn(h, base + 8),))
                    P[0].op("pool", lambda e, base=base, src_blk=src_blk: e.tensor_copy(
                        out=VA[:, base + 8, :, :], in_=VA[:, src_blk, :, :]), reads=(VAn(src_blk),),
                        writes=(VAn(base + 8),))
                    cumsum_block(lft[:, sq, :], "lft", base + 8)
                    kb = [(base + b, 0, False) for b in range(8)] + [(base + 8, 0, True)]
                    arow_prep(0, sq * 128, 128, base + 8, base + 8)
                    attention_qtile(hg, sq * 128, 128, kb, base + 8, T + sq * 128, sq % 2, base + 8, 0)
            emit_phase("A")

        with contextlib.ExitStack() as esD:
            S = mkS(esD)
            P[0] = Prog()
            WO0 = S("WO0", [128, 8, D], BF16)
            WG = S("WG", [128, 8, 3088], BF16)
            wa2s = S("wa2s", [16, 512], F32)
            bgab = S("bgab", [128, 512], F32)
            ggob = S("ggob", [128, 1024], F32)
            gfinb = S("gfinb", [128, 1024], F32)
            ogT = S("ogT", [128, 8, 512], BF16)
            ypt = S("ypt", [128, 4, D], F32)
            qTf = S("qTf", [128, 4, 512], F32)
            kTf = S("kTf", [128, 4, 512], F32)
            a1T = S("a1T", [16, 512], F32)
            k2 = S("k2", [128, 512], F32)
            v2 = [S(f"v2{i}", [128, 1024], BF16) for i in range(2)]
            sgg = [S(f"sgg{i}", [128, 1024], BF16) for i in range(2)]
            gg = S("gg", [128, 512], F32)
            ge = S("ge", [128, 512], F32)
            ebT = [[S(f"ebT{q}{h}", [128, 128], F32) for h in range(4)] for q in range(2)]
            enbT = [S(f"enbT{h}", [128, 128], F32) for h in range(4)]
            qeT = [[S(f"qeT{q}{h}", [128, 128], BF16) for h in range(4)] for q in range(2)]
            keT = [S(f"keT{h}", [128, 128], BF16) for h in range(4)]
            ebr = S("ebr", [128, 512], F32)
            kd = [S(f"kd{i}", [128, 512], BF16) for i in range(2)]
            ATs = [[S(f"ATs{q}{h}", [128, 128], BF16) for h in range(4)] for q in range(2)]
            Sst = S("Sst", [128, 4, 256], F32)
            Sbf = S("Sbf", [128, 4, 256], BF16)
            og1 = S("og1", [128, D], BF16)
            ss2 = S("ss2", [128, 8], F32)
            rs2 = S("rs2", [128, 8], F32)

            dma(wa2s[:], wa2[:, :], (), ("wa2s",), "wa2s")
            dma(bgab[:], bga[:, :], (), ("bgab",), "bgab")
            dma(ggob[:], ggo[:, :], (), ("ggob",), "ggob")
            dma(gfinb[:], gfin[:, :], (), ("gfinb",), "gfinb")
            wstD = S("wstD", [128, 1024], F32)
            load_weight(WO0, "WO0", wo0, D, None, None, wstD)
            load_weight(WG, "WG", wgla, 3088, "ggl", ggl, wstD)
            ALLS = tuple(f"Sst{h}" for h in range(4))
            ALLB = tuple(f"Sbf{h}" for h in range(4))
            P[0].op("pool", lambda e: e.memset(Sst[:], 0.0), reads=(), writes=ALLS)
            P[0].op("pool", lambda e: e.memset(Sbf[:], 0.0), reads=(), writes=ALLB)

            gba_rr = [0]

            def gba():
                i = 4 + gba_rr[0] % 4
                gba_rr[0] += 1
                return i

            def phase_d_tile(xsrc, scr_col0, rowbase, sample):
                dma(ypt[:], xsrc.rearrange("(s p) d -> p s d", p=128), (), ("ypt",), "ypt")
                dma(ogT[:], ogs[:, scr_col0:scr_col0 + 512].rearrange("(c p) t -> p c t", p=128), (), ("ogT",), "ogT")
                for s in range(4):
                    for nh in range(2):
                        g = gbank()
                        mm_group(f"psG{g}", [(psG[g][:, :], ogT[:, fc, s * 128:(s + 1) * 128],
                                              WO0[:, fc, nh * 512:(nh + 1) * 512], fc == 0, fc == 7, ("ogT", "WO0"))
                                             for fc in range(8)])
                        P[0].op("dve", lambda e, g=g, s=s, nh=nh: e.tensor_tensor(
                            out=ypt[:, s, nh * 512:(nh + 1) * 512], in0=psG[g][:, :],
                            in1=ypt[:, s, nh * 512:(nh + 1) * 512], op=ALU.add),
                            reads=(f"psG{g}", "ypt"), writes=("ypt",))
                dma(ypscr[rowbase:rowbase + 512, :].rearrange("(s p) d -> p s d", p=128), ypt[:], ("ypt",), (), "ypt")
                rms_rstd(lambda s: ypt[:, s, :], "ypt", 4, D)
                for s in range(4):
                    if s % 2 == 0:
                        P[0].op("act", lambda e, s=s: e.activation(out=hb[:, s, :], in_=ypt[:, s, :], func=AF.Copy,
                                                                   scale=rstd[:, s:s + 1]),
                                reads=("ypt", "rstd"), writes=("hb",))
                    else:
                        P[0].op("dve", lambda e, s=s: e.tensor_scalar(out=hb[:, s, :], in0=ypt[:, s, :],
                                                                      scalar1=rstd[:, s:s + 1], scalar2=None,
                                                                      op0=ALU.mult),
                                reads=("ypt", "rstd"), writes=("hb",))
                transpose_tok(lambda s: hb[:, s, :], "hb", 4, hT, "hT")
                for kind in range(2):
                    for h in range(4):
                        col = kind * 512 + h * 128
                        g = gbank()
                        mm_group(f"psG{g}", [(psG[g][:, :], WG[:, kc, col:col + 128], hT[:, kc, :], kc == 0, kc == 7,
                                              ("WG", "hT")) for kc in range(8)])
                        if kind == 0:
                            P[0].op("dve", lambda e, g=g, h=h: e.tensor_scalar(out=qTf[:, h, :], in0=psG[g][:, :],
                                                                              scalar1=128.0 ** -0.5, scalar2=None,
                                                                              op0=ALU.mult),
                                    reads=(f"psG{g}",), writes=("qTf",))
                        else:
                            P[0].op("dve", lambda e, g=g, h=h: e.tensor_copy(out=kTf[:, h, :], in_=psG[g][:, :]),
                                    reads=(f"psG{g}",), writes=("kTf",))
                g = gbank()
                mm_group(f"psG{g}", [(psG[g][0:16, :], WG[:, kc, 3072:3088], hT[:, kc, :], kc == 0, kc == 7,
                                      ("WG", "hT")) for kc in range(8)])
                P[0].op("dve", lambda e, g=g: e.tensor_copy(out=a1T[:, :], in_=psG[g][0:16, :]), reads=(f"psG{g}",),
                        writes=("a1T",))
                def stage_a(s, q):
                    ts_ = slice(s * 128, (s + 1) * 128)
                    v2q, kdq, sggq = v2[q], kd[q], sgg[q]
                    g = gba()
                    mm_group(f"psG{g}", [(psG[g][:, :], hT[:, kc, ts_], WG[:, kc, 512:1024], kc == 0, kc == 7,
                                          ("WG", "hT")) for kc in range(8)])
                    P[0].op("dve", lambda e, g=g: e.tensor_copy(out=k2[:], in_=psG[g][:, :]), reads=(f"psG{g}",),
                            writes=("k2",))
                    for nh in range(2):
                        g = gba()
                        mm_group(f"psG{g}", [(psG[g][:, :], hT[:, kc, ts_],
                                              WG[:, kc, 1024 + nh * 512:1536 + nh * 512], kc == 0, kc == 7,
                                              ("WG", "hT")) for kc in range(8)])
                        P[0].op("dve", lambda e, g=g, nh=nh: e.tensor_copy(out=v2q[:, nh * 512:(nh + 1) * 512],
                                                                          in_=psG[g][:, :]),
                                reads=(f"psG{g}",), writes=(f"v2{q}",))
                    for nh in range(2):
                        g = gba()
                        mm_group(f"psG{g}", [(psG[g][:, :], hT[:, kc, ts_],
                                              WG[:, kc, 2048 + nh * 512:2560 + nh * 512], kc == 0, kc == 7,
                                              ("WG", "hT")) for kc in range(8)])
                        P[0].op("act", lambda e, g=g, nh=nh: e.activation(out=sggq[:, nh * 512:(nh + 1) * 512],
                                                                         in_=psG[g][:, :], func=AF.Silu),
                                reads=(f"psG{g}",), writes=(f"sgg{q}",))
                    dma(sgscr[rowbase + s * 128:rowbase + (s + 1) * 128, :], sggq[:], (f"sgg{q}",), (), f"sgg{q}")
                    g = gba()
                    mm_group(f"psG{g}", [(psG[g][:, :], a1T[0:16, ts_], wa2s[0:16, :], True, True, ("a1T", "wa2s"))])
                    P[0].op("dve", lambda e, g=g: e.tensor_tensor(out=ge[:], in0=psG[g][:, :], in1=bgab[:],
                                                                  op=ALU.add),
                            reads=(f"psG{g}", "bgab"), writes=("ge",))
                    P[0].op("act", lambda e: e.activation(out=ge[:], in_=ge[:], func=AF.Exp, scale=-1.0),
                            reads=("ge",), writes=("ge",))
                    P[0].op("act", lambda e: e.activation(out=ge[:], in_=ge[:], func=AF.Ln, bias=1.0), reads=("ge",),
                            writes=("ge",))
                    gcol = 1 if sample else 0
                    P[0].op("dve", lambda e, gcol=gcol: e.tensor_scalar(out=gg[:], in0=ge[:],
                                                                        scalar1=gsc[:, gcol:gcol + 1], scalar2=None,
                                                                        op0=ALU.mult),
                            reads=("ge", "gsc"), writes=("gg",))
                    g = gba()
                    mm_group(f"psG{g}", [(psG[g][:, :], trir[:, :], gg[:, :], True, True, ("trir", "gg"))])
                    P[0].op("act", lambda e, g=g: e.activation(out=ebr[:], in_=psG[g][:, :], func=AF.Exp),
                            reads=(f"psG{g}",), writes=("ebr",))
                    P[0].op("dve", lambda e: e.tensor_tensor(out=kdq[:], in0=k2[:], in1=ebr[:], op=ALU.mult),
                            reads=("k2", "ebr"), writes=(f"kd{q}",))
                    for h in range(4):
                        hs = slice(h * 128, (h + 1) * 128)
                        g = gba()
                        mm_group(f"psG{g}", [(psG[g][:, 0:128], gg[:, hs], trii[:, :], True, True, ("gg", "trii"))])
                        P[0].op("act", lambda e, g=g, h=h: e.activation(out=ebT[q][h][:], in_=psG[g][:, 0:128],
                                                                        func=AF.Exp),
                                reads=(f"psG{g}",), writes=(f"ebT{q}{h}",))
                        P[0].op("act", lambda e, g=g, h=h: e.activation(out=enbT[h][:], in_=psG[g][:, 0:128],
                                                                        func=AF.Exp, scale=-1.0),
                                reads=(f"psG{g}",), writes=(f"enbT{h}",))
                    for h in range(4):
                        P[0].op("dve", lambda e, h=h: e.tensor_tensor(out=qeT[q][h][:], in0=qTf[:, h, ts_],
                                                                      in1=ebT[q][h][:], op=ALU.mult),
                                reads=("qTf", f"ebT{q}{h}"), writes=(f"qeT{q}{h}",))
                        P[0].op("dve", lambda e, h=h: e.tensor_tensor(out=keT[h][:], in0=kTf[:, h, ts_],
                                                                      in1=enbT[h][:], op=ALU.mult),
                                reads=("kTf", f"enbT{h}"), writes=(f"keT{h}",))
                    for h in range(4):
                        g2 = gba()
                        mm_group(f"psG{g2}", [(psG[g2][:, 0:128], keT[h][:, :], qeT[q][h][:, :], True, True,
                                               (f"keT{h}", f"qeT{q}{h}"))])
                        P[0].op("dve", lambda e, g2=g2, h=h: e.tensor_tensor(out=ATs[q][h][:], in0=psG[g2][:, 0:128],
                                                                             in1=trii[:], op=ALU.mult),
                                reads=(f"psG{g2}", "trii"), writes=(f"ATs{q}{h}",))

                def stage_b(s, q):
                    v2q, kdq = v2[q], kd[q]
                    go = (0, 1) if q == 0 else (2, 3)
                    if sample:
                        dma(Sst[:], sg[s].rearrange("h k v -> k h v"), (), ALLS, "Sst", q="sp")
                        P[0].op("pool", lambda e: e.tensor_copy(out=Sbf[:], in_=Sst[:]), reads=ALLS,
                                writes=ALLB)
                    for h in range(4):
                        hs = slice(h * 128, (h + 1) * 128)
                        ob = go[h // 2]
                        oc = slice((h % 2) * 256, (h % 2) * 256 + 256)
                        vs_ = slice(h * 256, (h + 1) * 256)
                        mm_group(f"psG{ob}", [(psG[ob][:, oc], ATs[q][h][:, :], v2q[:, vs_], True, False,
                                               (f"ATs{q}{h}", f"v2{q}")),
                                              (psG[ob][:, oc], qeT[q][h][:, :], Sbf[:, h, :], False, True,
                                               (f"qeT{q}{h}", f"Sbf{h}"))])
                        g3 = gba()
                        mm_group(f"psG{g3}", [(psG[g3][:, 0:256], kdq[:, hs], v2q[:, vs_], True, True,
                                               (f"kd{q}", f"v2{q}"))])
                        P[0].op("dve", lambda e, g3=g3, h=h: e.scalar_tensor_tensor(
                            out=Sst[:, h, :], in0=Sst[:, h, :], scalar=ebT[q][h][:, 127:128], in1=psG[g3][:, 0:256],
                            op0=ALU.mult, op1=ALU.add), reads=(f"psG{g3}", f"ebT{q}{h}", f"Sst{h}"),
                            writes=(f"Sst{h}",))
                        P[0].op("act", lambda e, h=h: e.activation(out=Sbf[:, h, :], in_=Sst[:, h, :], func=AF.Copy),
                                reads=(f"Sst{h}",), writes=(f"Sbf{h}",))
                    if sample:
                        dma(ss_o[s].rearrange("h k v -> k h v"), Sst[:], ALLS, (), "Sst", q="pool")
                    for h in range(4):
                        ob = go[h // 2]
                        oc = slice((h % 2) * 256, (h % 2) * 256 + 256)
                        P[0].op("act", lambda e, ob=ob, oc=oc, h=h: e.activation(
                            out=junk[:, 0:256], in_=psG[ob][:, oc], func=AF.Square, accum_out=ss2[:, h:h + 1]),
                            reads=(f"psG{ob}",), writes=("junk", f"ss2{h}"))
                    P[0].op("dve", lambda e: e.tensor_scalar(out=rs2[:, 0:4], in0=ss2[:, 0:4], scalar1=1.0 / 256,
                                                             scalar2=EPS, op0=ALU.mult, op1=ALU.add),
                            reads=tuple(f"ss2{h}" for h in range(4)), writes=("rs2",))
                    P[0].op("act", lambda e: e.activation(out=rs2[:, 0:4], in_=rs2[:, 0:4], func=AF.Ln),
                            reads=("rs2",), writes=("rs2",))
                    P[0].op("act", lambda e: e.activation(out=rs2[:, 0:4], in_=rs2[:, 0:4], func=AF.Exp, scale=-0.5),
                            reads=("rs2",), writes=("rs2",))
                    for h in range(4):
                        ob = go[h // 2]
                        oc = slice((h % 2) * 256, (h % 2) * 256 + 256)
                        P[0].op("dve", lambda e, ob=ob, oc=oc, h=h: e.scalar_tensor_tensor(
                            out=og1[:, h * 256:(h + 1) * 256], in0=psG[ob][:, oc], scalar=rs2[:, h:h + 1],
                            in1=ggob[:, h * 256:(h + 1) * 256], op0=ALU.mult, op1=ALU.mult),
                            reads=(f"psG{ob}", "rs2", "ggob"), writes=("og1",))
                    dma(onscr[rowbase + s * 128:rowbase + (s + 1) * 128, :], og1[:], ("og1",), (), "og1")

                stage_a(0, 0)
                for s in range(4):
                    if s + 1 < 4:
                        stage_a(s + 1, (s + 1) % 2)
                    stage_b(s, s % 2)

            for i in range(NT):
                phase_d_tile(xp[i * 512:(i + 1) * 512, :], i * 512, i * 512, False)
            dma(sp_o.rearrange("h k v -> k h v"), Sst[:], ALLS, (), "Sst")
            phase_d_tile(xs[:, :], T, T, True)
            emit_phase("D")

        with contextlib.ExitStack() as esE:
            S = mkS(esE)
            P[0] = Prog()
            WO1 = S("WO1e", [128, 8, D], BF16)
            gfinb = S("gfinbe", [128, 1024], F32)
            on_t = [S(f"on_t{i}", [128, 4, D], BF16) for i in range(2)]
            sg_t = [S(f"sg_t{i}", [128, 4, D], BF16) for i in range(2)]
            yp_t = [S(f"yp_t{i}", [128, 4, D], F32) for i in range(2)]
            og1 = S("og1e", [128, 4, D], BF16)
            og1T = S("og1Te", [128, 8, 512], BF16)
            y2 = S("y2e", [128, 4, D], F32)
            yo_ = [S(f"yoe{i}", [128, 4, D], F32) for i in range(2)]
            ss2 = S("ss2e", [128, 8], F32)
            rs2 = S("rs2e", [128, 8], F32)
            dma(gfinb[:], gfin[:, :], (), ("gfinb",), "gfinb")
            wstE = S("wstE", [128, 1024], F32)
            load_weight(WO1, "WO1", wo1, D, None, None, wstE)
            nch = T // 64

            def phase_e_tile(p0, rowbase, perm, yout, orow, bi):
                ont, sgt, ypt_, yo = on_t[bi], sg_t[bi], yp_t[bi], yo_[bi]
                nm = (f"on_t{bi}", f"sg_t{bi}", f"yp_t{bi}", f"yo{bi}")
                if perm:
                    for sb_ in range(4):
                        done = 0
                        while done < 128:
                            p = p0 + sb_ * 128 + done
                            c, i0 = p // nch, p % nch
                            ln = min(nch - i0, 128 - done)
                            src = onscr[i0 * 64 + c:(i0 + ln - 1) * 64 + c + 1:64, :]
                            dma(ont[done:done + ln, sb_, :], src, (), (nm[0],), nm[0])
                            done += ln
                else:
                    dma(ont[:], onscr[rowbase + p0:rowbase + p0 + 512, :].rearrange("(s p) d -> p s d", p=128), (),
                        (nm[0],), nm[0])
                dma(sgt[:], sgscr[rowbase + p0:rowbase + p0 + 512, :].rearrange("(s p) d -> p s d", p=128), (),
                    (nm[1],), nm[1])
                dma(ypt_[:], ypscr[rowbase + p0:rowbase + p0 + 512, :].rearrange("(s p) d -> p s d", p=128), (),
                    (nm[2],), nm[2])
                for sb_ in range(4):
                    P[0].op("dve" if sb_ % 2 == 0 else "pool", lambda e, sb_=sb_: e.tensor_tensor(
                        out=og1[:, sb_, :], in0=ont[:, sb_, :], in1=sgt[:, sb_, :], op=ALU.mult),
                        reads=(nm[0], nm[1]), writes=("og1",))
                transpose_tok(lambda s_: og1[:, s_, :], "og1", 4, og1T, "og1T")
                for sb_ in range(4):
                    for nh in range(2):
                        g = gbank()
                        mm_group(f"psG{g}", [(psG[g][:, :], og1T[:, fc, sb_ * 128:(sb_ + 1) * 128],
                                              WO1[:, fc, nh * 512:(nh + 1) * 512], fc == 0, fc == 7, ("og1T", "WO1"))
                                             for fc in range(8)])
                        P[0].op("dve", lambda e, g=g, nh=nh, sb_=sb_: e.tensor_tensor(
                            out=y2[:, sb_, nh * 512:(nh + 1) * 512], in0=psG[g][:, :],
                            in1=ypt_[:, sb_, nh * 512:(nh + 1) * 512], op=ALU.add),
                            reads=(f"psG{g}", nm[2]), writes=("y2",))
                    P[0].op("act", lambda e, sb_=sb_: e.activation(out=junk[:, :], in_=y2[:, sb_, :], func=AF.Square,
                                                                   accum_out=ss2[:, sb_:sb_ + 1]),
                            reads=("y2",), writes=("junk", f"ss2{sb_}"))
                P[0].op("dve", lambda e: e.tensor_scalar(out=rs2[:, 0:4], in0=ss2[:, 0:4], scalar1=1.0 / D,
                                                         scalar2=EPS, op0=ALU.mult, op1=ALU.add),
                        reads=tuple(f"ss2{i}" for i in range(4)), writes=("rs2",))
                P[0].op("act", lambda e: e.activation(out=rs2[:, 0:4], in_=rs2[:, 0:4], func=AF.Ln), reads=("rs2",),
                        writes=("rs2",))
                P[0].op("act", lambda e: e.activation(out=rs2[:, 0:4], in_=rs2[:, 0:4], func=AF.Exp, scale=-0.5),
                        reads=("rs2",), writes=("rs2",))
                for sb_ in range(4):
                    P[0].op("dve", lambda e, sb_=sb_: e.scalar_tensor_tensor(
                        out=yo[:, sb_, :], in0=y2[:, sb_, :], scalar=rs2[:, sb_:sb_ + 1], in1=gfinb[:],
                        op0=ALU.mult, op1=ALU.mult), reads=("y2", "rs2", "gfinb"), writes=(nm[3],))
                dma(yout[orow:orow + 512, :].rearrange("(s p) d -> p s d", p=128), yo[:], (nm[3],), (), nm[3])

            for j in range(T // 512):
                phase_e_tile(j * 512, 0, True, yp_o, j * 512, j % 2)
            phase_e_tile(0, T, False, ys_o, 0, (T // 512) % 2)
            emit_phase("E")
    return nc


def make_in_maps(inp, T, ncores):
    f = np.float32
    w_in_fox = np.asarray(inp["w_in_fox"], f)
    wf = []
    for hg in range(4):
        cols = [w_in_fox[:, k * 1024 + hg * 256:k * 1024 + (hg + 1) * 256] for k in range(4)]
        cols.append(w_in_fox[:, 4096 + hg * 4:4096 + (hg + 1) * 4])
        wf.append(np.concatenate(cols, axis=1))
    wf = np.ascontiguousarray(np.stack(wf))

    def pc(v):
        return np.ascontiguousarray(np.asarray(v, f).reshape(8, 128).T)

    def rep(v):
        v = np.asarray(v, f)
        return np.ascontiguousarray(np.broadcast_to(v[None, :], (128, v.shape[0])))

    s_idx = np.arange(128)[:, None]
    t_idx = np.arange(128)[None, :]
    gsc = np.zeros((128, 2), f)
    gsc[:, 0] = -1.0 / 16
    gsc[:64, 1] = -1.0 / 16
    consts = {
        "c_ident": (s_idx == t_idx).astype(f), "c_trii": (s_idx <= t_idx).astype(f),
        "c_trir": (s_idx > t_idx).astype(f), "c_ones": np.ones((128, 128), f),
        "c_mneg": np.where(s_idx > t_idx, -30000.0, 0.0).astype(f), "c_gsc": gsc,
    }
    shared = {
        "wfox": wf, "gfox": pc(inp["g_norm_fox"]), "bff": rep(inp["b_fox_f"]),
        "wo0": np.ascontiguousarray(np.asarray(inp["w_out_fox"], f)), "ggla": pc(inp["g_norm_gla"]),
        "wgla": np.ascontiguousarray(np.asarray(inp["w_in_gla"], f)),
        "wa2": np.ascontiguousarray(np.asarray(inp["w_gla_a2"], f)), "bga": rep(inp["b_gla_a"]),
        "ggo": rep(np.tile(np.asarray(inp["g_gla_o"], f), 4)),
        "wo1": np.ascontiguousarray(np.asarray(inp["w_out_gla"], f)), "gfin": rep(inp["g_final"]),
    }
    shared.update(consts)
    xprompt = np.asarray(inp["x_prompt"], f)
    xsample = np.asarray(inp["x_sample"], f)
    nb = xprompt.shape[0]
    maps = []
    for c in range(ncores):
        sl = slice(c * SEQ_PER_CORE, (c + 1) * SEQ_PER_CORE)
        xs = np.zeros((SEQ_PER_CORE, 128, D), f)
        xs[:, :DEC, :] = xsample[sl]
        m = dict(shared)
        m["xp"] = np.ascontiguousarray(xprompt[c % nb])
        m["xs"] = xs.reshape(SEQ_PER_CORE * 128, D)
        m["ck"] = np.ascontiguousarray(np.asarray(inp["cache_fox_k"], f)[sl].reshape(SEQ_PER_CORE, PAST, 1024))
        m["cv"] = np.ascontiguousarray(np.asarray(inp["cache_fox_v"], f)[sl].reshape(SEQ_PER_CORE, PAST, 1024))
        m["clf"] = np.ascontiguousarray(np.asarray(inp["cache_fox_logf"], f)[sl])
        m["sg"] = np.ascontiguousarray(np.asarray(inp["state_gla"], f)[sl])
        maps.append(m)
    return maps


_NC_CACHE = {}


def run(inp, ncores=NCORES):
    xprompt = np.asarray(inp["x_prompt"])
    B, T, _ = xprompt.shape
    if T not in _NC_CACHE:
        _NC_CACHE[T] = build_nc(T)
    nc = _NC_CACHE[T]
    maps = make_in_maps(inp, T, ncores)
    res = run_bass_kernel_spmd(nc, maps, core_ids=list(range(ncores))).results
    f = np.float32
    nseq = ncores * SEQ_PER_CORE

    def samp(name, last):
        a = np.stack([res[c][name].reshape(SEQ_PER_CORE, 128, last)[:, :DEC] for c in range(ncores)])
        return a.reshape(nseq, DEC, last)

    y_prompt = np.stack([res[b]["yp"] for b in range(B)]).astype(f)
    y_sample = samp("ys", D).astype(f)
    kp = np.stack([res[b]["kp"] for b in range(B)]).reshape(B, T, 16, 64).astype(f)
    vp = np.stack([res[b]["vp"] for b in range(B)]).reshape(B, T, 16, 64).astype(f)
    lp = np.stack([res[b]["lp"] for b in range(B)]).astype(f)
    stp = np.stack([res[b]["stp"] for b in range(B)]).astype(f)
    ks = samp("ks", 1024).reshape(nseq, DEC, 16, 64).astype(f)
    vs = samp("vs", 1024).reshape(nseq, DEC, 16, 64).astype(f)
    ls = samp("ls", 16).astype(f)
    sts = np.concatenate([res[c]["sts"] for c in range(ncores)], axis=0).astype(f)
    return (y_prompt, y_sample, kp, vp, lp, stp, ks, vs, ls, sts)


def kernel(**inputs):
    return run(inputs)
```

```python
import numpy as np
import concourse.bass as bass
import concourse.mybir as mybir
from concourse.bass_utils import run_bass_kernel_spmd

F32 = mybir.dt.float32
BF16 = mybir.dt.bfloat16
AF = mybir.ActivationFunctionType
ALU = mybir.AluOpType

D = 1024
EPS = 1e-6
NCORES = 8
SEQ_PER_CORE = 4
PAST = 1024
DEC = 64


class Prog:
    ENG = ("pe", "act", "dve", "pool", "sp")

    def __init__(self):
        self.ops = []
        self.last_w = {}
        self.readers = {}

    def op(self, eng, fn, reads=(), writes=(), dma=None):
        oid = len(self.ops)
        deps = set()
        for r in reads:
            if r in self.last_w:
                deps.add(self.last_w[r])
        for w in writes:
            if w in self.last_w:
                deps.add(self.last_w[w])
            for rd in self.readers.get(w, ()):
                deps.add(rd)
        deps.discard(oid)
        for r in reads:
            self.readers.setdefault(r, []).append(oid)
        for w in writes:
            self.last_w[w] = oid
            self.readers[w] = []
        self.ops.append(dict(eng=eng, fn=fn, deps=deps, dma=dma, sig=False))
        return oid

    def emit(self, nc, block_engines, sems, dma_sems):
        ops = self.ops
        for o in ops:
            if o["eng"] == "pe" and o["dma"] is None:
                o["deps"] = {d for d in o["deps"] if not (ops[d]["eng"] == "pe" and ops[d]["dma"] is None)}
        for o in ops:
            for d in o["deps"]:
                ops[d]["sig"] = True
        lastop = {}
        for o in ops:
            if o["dma"] is None:
                lastop[o["eng"]] = o
        for o in lastop.values():
            o["sig"] = True
        cnt = {e: 0 for e in self.ENG}
        dcnt = {}
        for o in ops:
            if o["dma"] is not None:
                k = o["dma"]
                dcnt[k] = dcnt.get(k, 0) + 16
                o["semkey"] = ("dma", k)
                o["semval"] = dcnt[k]
            elif o["sig"]:
                cnt[o["eng"]] += 1
                o["semkey"] = ("eng", o["eng"])
                o["semval"] = cnt[o["eng"]]
        per_eng = {e: [] for e in self.ENG}
        for o in ops:
            per_eng[o["eng"]].append(o)

        def run(engname, eng):
            waited = {}
            for o in per_eng[engname]:
                need = {}
                for d in o["deps"]:
                    dk, dv = ops[d]["semkey"], ops[d]["semval"]
                    if need.get(dk, 0) < dv:
                        need[dk] = dv
                pend = [(dk, dv) for dk, dv in need.items() if waited.get(dk, 0) < dv]
                attach = None
                if engname == "pe" and pend:
                    latest = max((d for d in o["deps"]), key=lambda d: d)
                    lk = ops[latest]["semkey"]
                    for it in pend:
                        if it[0] == lk:
                            attach = it
                    if attach is None:
                        attach = pend[-1]
                    pend = [it for it in pend if it is not attach]
                for dk, dv in pend:
                    s = dma_sems[dk[1]] if dk[0] == "dma" else sems[dk[1]]
                    eng.wait_ge(s, dv)
                    waited[dk] = dv
                ins = o["fn"](eng)
                if isinstance(ins, tuple):
                    first_ins, ins = ins
                else:
                    first_ins = ins
                if attach is not None:
                    dk, dv = attach
                    s = dma_sems[dk[1]] if dk[0] == "dma" else sems[dk[1]]
                    first_ins._wait_ge(s, dv)
                    waited[dk] = dv
                if o["dma"] is not None:
                    ins.then_inc(dma_sems[o["dma"]], 16)
                elif o["sig"]:
                    ins.then_inc(sems[engname], 1)
            for k, v in dcnt.items():
                eng.wait_ge(dma_sems[k], v)
            for e2, v in cnt.items():
                if v > 0:
                    eng.wait_ge(sems[e2], v)

        for engname, reg in block_engines.items():
            reg(lambda eng, _n=engname: run(_n, eng))


def build_nc(T):
    import contextlib
    NT = T // 512
    NBLK = max(T // 128, SEQ_PER_CORE * 12 + 4)
    TS = 512
    NG, NH, NP = 4, 4, 2
    GW = NH * 64
    nc = bass.Bass("TRN2", target_bir_lowering=False)

    def din(name, shape, dt=F32):
        return nc.dram_tensor(name, list(shape), dt, kind="ExternalInput").ap()

    def dout(name, shape, dt=F32):
        return nc.dram_tensor(name, list(shape), dt, kind="ExternalOutput").ap()

    xp = din("xp", [T, D]); xs = din("xs", [TS, D])
    ck = din("ck", [SEQ_PER_CORE, PAST, 1024]); cv = din("cv", [SEQ_PER_CORE, PAST, 1024])
    clf = din("clf", [SEQ_PER_CORE, PAST, 16]); sg = din("sg", [SEQ_PER_CORE, 4, 128, 256])
    wfox = din("wfox", [NG, D, 4 * GW + NH]); gfox = din("gfox", [128, 8]); bff = din("bff", [128, 16])
    wo0 = din("wo0", [D, D]); ggla = din("ggla", [128, 8]); wgla = din("wgla", [D, 3088])
    wa2 = din("wa2", [16, 512]); bga = din("bga", [128, 512]); ggo = din("ggo", [128, 1024])
    wo1 = din("wo1", [D, D]); gfin = din("gfin", [128, 1024])
    c_ident = din("c_ident", [128, 128]); c_trii = din("c_trii", [128, 128]); c_trir = din("c_trir", [128, 128])
    c_ones = din("c_ones", [128, 128]); c_mneg = din("c_mneg", [128, 128]); c_gsc = din("c_gsc", [128, 2])

    yp_o = dout("yp", [T, D]); ys_o = dout("ys", [TS, D])
    kp_o = dout("kp", [T, 1024]); vp_o = dout("vp", [T, 1024]); lp_o = dout("lp", [T, 16])
    sp_o = dout("stp", [4, 128, 256])
    ks_o = dout("ks", [TS, 1024]); vs_o = dout("vs", [TS, 1024]); ls_o = dout("ls", [TS, 16])
    ss_o = dout("sts", [SEQ_PER_CORE, 4, 128, 256])
    onscr = nc.dram_tensor("onscr", [T + TS, 1024], BF16).ap()
    sgscr = nc.dram_tensor("sgscr", [T + TS, 1024], BF16).ap()
    ypscr = nc.dram_tensor("ypscr", [T + TS, 1024], F32).ap()
    hTscr = nc.dram_tensor("hTscr", [T // 512 + 1, 128, 8 * 512], BF16).ap()
    ogs = nc.dram_tensor("ogscr", [1024, T + TS], BF16).ap()

    with contextlib.ExitStack() as es:
        def mkS(stack):
            def S(name, shape, dt):
                return stack.enter_context(nc.sbuf_tensor(name, list(shape), dt))
            return S

        S = mkS(es)
        wst = S("wst", [128, 1024], F32)
        hb = S("hb", [128, 4, D], BF16)
        hT = S("hT", [128, 8, 512], BF16)
        junk = S("junk", [128, D], BF16)
        ssq = S("ssq", [128, 8], F32)
        rstd = S("rstd", [128, 8], F32)
        ident = S("ident", [128, 128], BF16)
        identf = S("identf", [128, 128], F32)
        trii = S("trii", [128, 128], F32)
        trir = S("trir", [128, 128], F32)
        onesf = S("onesf", [128, 128], F32)
        mneg = S("mneg", [128, 128], BF16)
        mnegf = S("mnegf", [128, 128], F32)
        gsc = S("gsc", [128, 2], F32)
        gfx = S("gfx", [128, 8], F32)
        ggl = S("ggl", [128, 8], F32)
        bfb = S("bfb", [128, 16], F32)
        psG = [es.enter_context(nc.psum_tensor(f"psG{i}", [128, 512], F32)) for i in range(8)]
        psT = {6: psG[6][:, :].bitcast(BF16), 7: psG[7][:, :].bitcast(BF16)}
        rr = {"g": 0, "t": 0}

        def gbank():
            i = rr["g"] % 6
            rr["g"] += 1
            return i

        def tbank():
            i = 6 + rr["t"] % 2
            rr["t"] += 1
            return i

        P = [None]

        def dma(out, in_, reads, writes, key, q=None):
            if q is None:
                q = "pool" if (reads and not writes) or (writes == ("ogscr",)) else "sp"
            P[0].op(q, lambda e, o=out, i=in_: e.dma_start(out=o, in_=i), reads=reads, writes=writes, dma=key)

        def mm_group(psname, mms):
            reads = set()
            for m in mms:
                reads.update(m[5])

            def fn(e, mms=mms):
                ins = None
                first = None
                for (o, l, r, st, sp_, _) in mms:
                    ins = e.matmul(o, lhsT=l, rhs=r, start=st, stop=sp_)
                    if first is None:
                        first = ins
                return (first, ins)
            P[0].op("pe", fn, reads=tuple(reads), writes=(psname,))

        wrr = [0]

        def load_weight(dst, dstname, src2d, ncols, gname, gt, wst2=None):
            for kc in range(8):
                for c0 in range(0, ncols, 1024):
                    c1 = min(ncols, c0 + 1024)
                    wi = 0
                    if wst2 is not None:
                        wi = wrr[0] % 2
                        wrr[0] += 1
                    wb = wst if wi == 0 else wst2
                    wn = "wst" if wi == 0 else "wst2"
                    dma(wb[:, 0:c1 - c0], src2d[kc * 128:(kc + 1) * 128, c0:c1], (), (wn,), wn)
                    if gt is None:
                        P[0].op("dve", lambda e, kc=kc, c0=c0, c1=c1, wb=wb: e.tensor_copy(
                            out=dst[:, kc, c0:c1], in_=wb[:, 0:c1 - c0]), reads=(wn,), writes=(dstname,))
                    elif wi == 0:
                        P[0].op("act", lambda e, kc=kc, c0=c0, c1=c1, wb=wb: e.activation(
                            out=dst[:, kc, c0:c1], in_=wb[:, 0:c1 - c0], func=AF.Copy, scale=gt[:, kc:kc + 1]),
                            reads=(wn, gname), writes=(dstname,))
                    else:
                        P[0].op("dve", lambda e, kc=kc, c0=c0, c1=c1, wb=wb: e.tensor_scalar(
                            out=dst[:, kc, c0:c1], in0=wb[:, 0:c1 - c0], scalar1=gt[:, kc:kc + 1], scalar2=None,
                            op0=ALU.mult), reads=(wn, gname), writes=(dstname,))

        def rms_rstd(src, srcname, nsub, n_el):
            for s in range(nsub):
                P[0].op("act", lambda e, s=s: e.activation(out=junk[:, 0:n_el], in_=src(s), func=AF.Square,
                                                           accum_out=ssq[:, s:s + 1]),
                        reads=(srcname,), writes=("junk", "ssq" + str(s)))
            P[0].op("dve", lambda e: e.tensor_scalar(out=rstd[:, 0:nsub], in0=ssq[:, 0:nsub], scalar1=1.0 / n_el,
                                                     scalar2=EPS, op0=ALU.mult, op1=ALU.add),
                    reads=tuple("ssq" + str(s) for s in range(nsub)), writes=("rstd",))
            P[0].op("act", lambda e: e.activation(out=rstd[:, 0:nsub], in_=rstd[:, 0:nsub], func=AF.Ln), reads=("rstd",), writes=("rstd",))
            P[0].op("act", lambda e: e.activation(out=rstd[:, 0:nsub], in_=rstd[:, 0:nsub], func=AF.Exp, scale=-0.5), reads=("rstd",), writes=("rstd",))

        def transpose_tok(src_fn, srcname, nsub, dst, dstname):
            for kc in range(8):
                tb = tbank()

                def fn(e, kc=kc, tb=tb):
                    ins = None
                    first = None
                    for s in range(nsub):
                        ins = e.transpose(psT[tb][:, s * 128:(s + 1) * 128], src_fn(s)[:, kc * 128:(kc + 1) * 128],
                                          ident[:])
                        if first is None:
                            first = ins
                    return (first, ins)
                P[0].op("pe", fn, reads=(srcname, "ident"), writes=(f"psG{tb}",))
                P[0].op("dve", lambda e, kc=kc, tb=tb: e.tensor_copy(out=dst[:, kc, 0:nsub * 128],
                                                                    in_=psT[tb][:, 0:nsub * 128]),
                        reads=(f"psG{tb}",), writes=(dstname,))

        def emit_phase(tag):
            dma_keys = sorted({o["dma"] for o in P[0].ops if o["dma"] is not None})
            sems = {e: es.enter_context(nc.semaphore(f"s{tag}_{e}")) for e in Prog.ENG}
            dsems = {k: es.enter_context(nc.semaphore(f"d{tag}_{k}")) for k in dma_keys}
            with nc.Block() as block:
                P[0].emit(nc, {"pe": block.tensor, "act": block.scalar, "dve": block.vector, "pool": block.gpsimd,
                               "sp": block.sync}, sems, dsems)

        with contextlib.ExitStack() as esA:
            S = mkS(esA)
            P[0] = Prog()
            KA = [S(f"KA{h}", [68, NBLK * 128], BF16) for h in range(NH)]
            VA = S("VA", [128, NBLK, NH, 65], BF16)
            cT = S("cT", [128, NBLK, NH], F32)
            carry = S("carry", [128, NBLK + 1, NH], F32)
            biasT = S("biasT", [128, NBLK, NH], F32)
            W0 = S("W0", [128, 8, 4 * GW + NH], BF16)
            xt = S("xt", [128, 4, D], F32)
            QA = [S(f"QA{i}", [68, NH, 512], BF16) for i in range(2)]
            csp = [S(f"csp{i}", [NH, 512], F32) for i in range(2)]
            csb = [S(f"csb{i}", [NH, 512], BF16) for i in range(3)]
            ctmp = S("ctmp", [128, 4, NH], F32)
            arow = S("arow", [NH, 512], BF16)
            sgT = [S(f"sgT{i}", [128, NP, 512], BF16) for i in range(2)]
            ktok = S("ktok", [128, 4, GW], F32)
            ktb = S("ktb", [128, 4, GW], BF16)
            vtok = S("vtok", [128, 4, GW], F32)
            lft = S("lft", [128, 4, NH], F32)
            lfe = S("lfe", [128, 4, NH], F32)
            PT = [S(f"PT{i}", [128, 512], BF16) for i in range(5)]
            rd = S("rd", [128, 512], F32)
            otmp = S("otmp", [64, 512], F32)
            ogst = [S(f"ogst{i}", [64, NH, 512], BF16) for i in range(2)]
            ckt = S("ckt", [128, GW], F32)
            ckb = S("ckb", [128, GW], BF16)
            clt = S("clt", [128, 8, NH], F32)

            dma(identf[:], c_ident[:, :], (), ("identf",), "identf")
            dma(trii[:], c_trii[:, :], (), ("trii",), "trii")
            dma(trir[:], c_trir[:, :], (), ("trir",), "trir")
            dma(onesf[:], c_ones[:, :], (), ("onesf",), "onesf")
            dma(gsc[:], c_gsc[:, :], (), ("gsc",), "gsc")
            dma(gfx[:], gfox[:, :], (), ("gfx",), "gfx")
            dma(ggl[:], ggla[:, :], (), ("ggl",), "ggl")
            dma(bfb[:], bff[:, :], (), ("bfb",), "bfb")
            dma(mnegf[:], c_mneg[:, :], (), ("mnegf",), "mnegf")
            P[0].op("dve", lambda e: e.tensor_copy(out=ident[:], in_=identf[:]), reads=("identf",), writes=("ident",))
            P[0].op("dve", lambda e: e.tensor_copy(out=mneg[:], in_=mnegf[:]), reads=("mnegf",), writes=("mneg",))

            def KAn(h, blk):
                return f"KA{h}_{blk // 4}"

            def VAn(blk):
                return f"VA_{blk // 4}"
            NTB = NBLK // 4
            ALLKA = [tuple(f"KA{h}_{t}" for t in range(NTB)) for h in range(NH)]
            ALLVA = tuple(f"VA_{t}" for t in range(NTB))
            pa_rr = [0]

            def pbank():
                i = 6 + pa_rr[0] % 2
                pa_rr[0] += 1
                return i

            def transpose_tok_a(src_fn, srcname, nsub, dst, dstname):
                for kc in range(8):
                    tb = pbank()

                    def fn(e, kc=kc, tb=tb):
                        ins = None
                        first = None
                        for s in range(nsub):
                            ins = e.transpose(psT[tb][:, s * 128:(s + 1) * 128],
                                              src_fn(s)[:, kc * 128:(kc + 1) * 128], ident[:])
                            if first is None:
                                first = ins
                        return (first, ins)
                    P[0].op("pe", fn, reads=(srcname, "ident"), writes=(f"psG{tb}",))
                    P[0].op("dve", lambda e, kc=kc, tb=tb: e.tensor_copy(out=dst[:, kc, 0:nsub * 128],
                                                                        in_=psT[tb][:, 0:nsub * 128]),
                            reads=(f"psG{tb}",), writes=(dstname,))
                    yield

            def fox_inproj_gen(hg, xsrc, blk0, kout, vout, lout, row0, par, do_cumsum, tix):
                QAp, sgTp = QA[par], sgT[par]
                qan, sgn = f"QA{par}", f"sgT{par}"
                kt_col0 = blk0 * 128
                if hg == 0:
                    dma(xt[:], xsrc.rearrange("(s p) d -> p s d", p=128), (), ("xt",), "xt")
                    yield
                    rms_rstd(lambda s: xt[:, s, :], "xt", 4, D)
                    yield
                    for s in range(4):
                        P[0].op("dve", lambda e, s=s: e.tensor_scalar(out=hb[:, s, :], in0=xt[:, s, :],
                                                                      scalar1=rstd[:, s:s + 1], scalar2=None,
                                                                      op0=ALU.mult),
                                reads=("xt", "rstd"), writes=("hb",))
                        yield
                    yield from transpose_tok_a(lambda s: hb[:, s, :], "hb", 4, hT, "hT")
                    dma(hTscr[tix].rearrange("p (c t) -> p c t", c=8), hT[:], ("hT",), ("hTscr",), "hTst", q="pool")
                else:
                    dma(hT[:], hTscr[tix].rearrange("p (c t) -> p c t", c=8), ("hTscr",), ("hT",), "hT", q="sp")
                    yield
                for kind in (0, 2):
                    for pr in range(NP):
                        col = {0: 0, 1: GW, 2: 3 * GW}[kind] + pr * 128
                        g = pbank()
                        mm_group(f"psG{g}", [(psG[g][:, :], W0[:, kc, col:col + 128], hT[:, kc, :], kc == 0, kc == 7,
                                              ("W0", "hT")) for kc in range(8)])
                        if kind == 0:
                            for hf in range(2):
                                P[0].op("dve", lambda e, g=g, pr=pr, hf=hf: e.tensor_scalar(
                                    out=QAp[0:64, 2 * pr + hf, :], in0=psG[g][hf * 64:hf * 64 + 64, :],
                                    scalar1=0.125, scalar2=None, op0=ALU.mult),
                                    reads=(f"psG{g}",), writes=(qan,))
                        elif kind == 1:
                            for hf in range(2):
                                P[0].op("dve", lambda e, g=g, pr=pr, hf=hf: e.tensor_copy(
                                    out=KA[2 * pr + hf][0:64, kt_col0:kt_col0 + 512],
                                    in_=psG[g][hf * 64:hf * 64 + 64, :]),
                                    reads=(f"psG{g}",), writes=(KAn(2 * pr + hf, blk0),))
                        else:
                            P[0].op("act", lambda e, g=g, pr=pr: e.activation(out=sgTp[:, pr, :], in_=psG[g][:, :],
                                                                             func=AF.Silu),
                                    reads=(f"psG{g}",), writes=(sgn,))
                        yield
                for s in range(4):
                    g = pbank()
                    mm_group(f"psG{g}", [(psG[g][:, 0:2 * GW], hT[:, kc, s * 128:(s + 1) * 128],
                                          W0[:, kc, GW:3 * GW], kc == 0, kc == 7, ("W0", "hT")) for kc in range(8)])
                    P[0].op("dve", lambda e, g=g, s=s: e.tensor_copy(out=ktok[:, s, :], in_=psG[g][:, 0:GW]),
                            reads=(f"psG{g}",), writes=("ktok",))
                    P[0].op("act", lambda e, s=s: e.activation(out=ktb[:, s, :], in_=ktok[:, s, :], func=AF.Copy),
                            reads=("ktok",), writes=("ktb",))
                    P[0].op("dve", lambda e, g=g, s=s: e.tensor_copy(out=vtok[:, s, :], in_=psG[g][:, GW:2 * GW]),
                            reads=(f"psG{g}",), writes=("vtok",))
                    P[0].op("pool", lambda e, s=s: e.tensor_copy(out=VA[:, blk0 + s, :, 0:64],
                                                                 in_=vtok[:, s, :].rearrange("p (h d) -> p h d", h=NH)),
                            reads=("vtok",), writes=(VAn(blk0),))
                    yield
                    g = pbank()
                    mm_group(f"psG{g}", [(psG[g][:, 0:NH], hT[:, kc, s * 128:(s + 1) * 128],
                                          W0[:, kc, 4 * GW:4 * GW + NH], kc == 0, kc == 7, ("W0", "hT"))
                                         for kc in range(8)])
                    P[0].op("dve", lambda e, g=g, s=s: e.tensor_tensor(out=lfe[:, s, :], in0=psG[g][:, 0:NH],
                                                                       in1=bfb[:, hg * NH:(hg + 1) * NH], op=ALU.add),
                            reads=(f"psG{g}", "bfb"), writes=("lfe",))
                    yield
                for pr in range(NP):
                    tb = pbank()

                    def fn_kt(e, tb=tb, pr=pr):
                        ins = None
                        first = None
                        for s in range(4):
                            ins = e.transpose(psT[tb][:, s * 128:(s + 1) * 128],
                                              ktb[:, s, pr * 128:(pr + 1) * 128], ident[:])
                            if first is None:
                                first = ins
                        return (first, ins)
                    P[0].op("pe", fn_kt, reads=("ktb", "ident"), writes=(f"psG{tb}",))
                    for hf in range(2):
                        h = 2 * pr + hf
                        P[0].op("dve", lambda e, tb=tb, h=h, hf=hf: e.tensor_copy(
                            out=KA[h][0:64, kt_col0:kt_col0 + 512], in_=psT[tb][hf * 64:hf * 64 + 64, 0:512]),
                            reads=(f"psG{tb}",), writes=(KAn(h, blk0),))
                yield
                P[0].op("act", lambda e: e.activation(out=lfe[:], in_=lfe[:], func=AF.Exp, scale=-1.0),
                        reads=("lfe",), writes=("lfe",))
                P[0].op("act", lambda e: e.activation(out=lfe[:], in_=lfe[:], func=AF.Ln, bias=1.0),
                        reads=("lfe",), writes=("lfe",))
                P[0].op("dve", lambda e: e.tensor_scalar(out=lft[:], in0=lfe[:], scalar1=-1.0, scalar2=None,
                                                         op0=ALU.mult),
                        reads=("lfe",), writes=("lft",))
                yield
                dst3 = lambda o, w: o[row0:row0 + 512, hg * w:(hg + 1) * w].rearrange("(s p) c -> p s c", p=128)
                dma(dst3(kout, GW), ktok[:], ("ktok",), (), "ktok")
                dma(dst3(vout, GW), vtok[:], ("vtok",), (), "vtok")
                dma(dst3(lout, NH), lft[:], ("lft",), (), "lft")
                yield
                if do_cumsum:
                    for s in range(4):
                        cumsum_block(lft[:, s, :], "lft", blk0 + s)
                        yield
                    arow_prep(par, 0, 512, blk0 + 2, blk0, prompt=True)
                    yield

            def drain(gen):
                if gen is not None:
                    for _ in gen:
                        pass

            def cumsum_block(lsrc, lname, blk):
                g = pbank()
                mm_group(f"psG{g}", [(psG[g][:, 0:NH], trii[:, :], lsrc, True, True, (lname, "trii")),
                                     (psG[g][:, 8:8 + NH], onesf[:, :], lsrc, True, True, (lname, "onesf"))])
                P[0].op("dve", lambda e, g=g: e.tensor_tensor(out=cT[:, blk, :], in0=psG[g][:, 0:NH],
                                                              in1=carry[:, blk, :], op=ALU.add),
                        reads=(f"psG{g}", "carry"), writes=("cT",))
                P[0].op("dve", lambda e, g=g: e.tensor_tensor(out=carry[:, blk + 1, :], in0=psG[g][:, 8:8 + NH],
                                                              in1=carry[:, blk, :], op=ALU.add),
                        reads=(f"psG{g}", "carry"), writes=("carry",))

            def arow_prep(par, qc0, nq, ref_idx, qblk0, prompt=False):
                QAp, qan = QA[par], f"QA{par}"
                nqb = nq // 128
                if prompt:
                    src_fn = lambda qb: cT[:, qblk0 + qb, :]
                    srcname = "cT"
                else:
                    P[0].op("dve", lambda e: e.tensor_tensor(
                        out=ctmp[:, 0:nqb, :], in0=cT[:, qblk0:qblk0 + nqb, :],
                        in1=carry[:, ref_idx:ref_idx + 1, :].to_broadcast([128, nqb, NH]), op=ALU.subtract),
                        reads=("cT", "carry"), writes=("ctmp",))
                    src_fn = lambda qb: ctmp[:, qb, :]
                    srcname = "ctmp"
                ga_ = pbank()

                def fn_tr(e, ga_=ga_):
                    ins = None
                    first = None
                    for qb in range(nqb):
                        ins = e.transpose(psG[ga_][0:NH, qb * 128:(qb + 1) * 128], src_fn(qb), identf[:])
                        if first is None:
                            first = ins
                    return (first, ins)
                P[0].op("pe", fn_tr, reads=(srcname, "identf"), writes=(f"psG{ga_}",))
                P[0].op("dve", lambda e, ga_=ga_: e.tensor_copy(out=arow[:, 0:nq], in_=psG[ga_][0:NH, 0:nq]),
                        reads=(f"psG{ga_}",), writes=("arow",))
                for hh in range(NH):
                    dma(QAp[64:65, hh, qc0:qc0 + nq], arow[hh:hh + 1, 0:nq], ("arow",), (qan,), "arow", q="sp")
                if prompt:
                    n1, r1 = csp
                    hi, mid, lo = csb
                    P[0].op("dve", lambda e, ga_=ga_: e.tensor_scalar(out=n1[:], in0=psG[ga_][0:NH, 0:512],
                                                                      scalar1=-1.0, scalar2=None, op0=ALU.mult),
                            reads=(f"psG{ga_}",), writes=("csp0",))
                    P[0].op("dve", lambda e: e.tensor_copy(out=hi[:], in_=n1[:]), reads=("csp0",), writes=("csb0",))
                    P[0].op("dve", lambda e: e.tensor_tensor(out=r1[:], in0=n1[:], in1=hi[:], op=ALU.subtract),
                            reads=("csp0", "csb0"), writes=("csp1",))
                    P[0].op("dve", lambda e: e.tensor_copy(out=mid[:], in_=r1[:]), reads=("csp1",), writes=("csb1",))
                    P[0].op("dve", lambda e: e.tensor_tensor(out=n1[:], in0=r1[:], in1=mid[:], op=ALU.subtract),
                            reads=("csp1", "csb1"), writes=("csp0",))
                    P[0].op("dve", lambda e: e.tensor_copy(out=lo[:], in_=n1[:]), reads=("csp0",), writes=("csb2",))
                    c0 = qblk0 * 128
                    for hh in range(NH):
                        for j in range(3):
                            dma(KA[hh][65 + j:66 + j, c0:c0 + 512], csb[j][hh:hh + 1, :], (f"csb{j}",),
                                (KAn(hh, qblk0),), f"csb{j}", q="sp")

            pt_rr = [0]
            s_rr = [0]
            LA = 3

            def attention_qtile(hg, qc0, nq, kblocks, ref_idx, scr_col0, stbuf, qblk0, par, filler=None, use_bias=True):
                QAp, sgTp = QA[par], sgT[par]
                qan, sgn = f"QA{par}", f"sgT{par}"
                blks = [b for (b, _, _) in kblocks]
                b_lo, b_hi = min(blks), max(blks) + 1
                if use_bias:
                    P[0].op("dve", lambda e: e.tensor_tensor(
                        out=biasT[:, b_lo:b_hi, :],
                        in0=carry[:, ref_idx:ref_idx + 1, :].to_broadcast([128, b_hi - b_lo, NH]),
                        in1=cT[:, b_lo:b_hi, :], op=ALU.subtract), reads=("carry", "cT"), writes=("biasT",))
                og = ogst[stbuf]
                ogname = f"ogst{stbuf}"
                nkb = len(kblocks)
                units = [(hh, bi) for hh in range(NH) for bi in range(nkb)]
                nu = len(units)
                sb_of, pt_of = {}, {}

                def issue_mm1(u):
                    hh, bi = units[u]
                    blk, qoff, masked = kblocks[bi]
                    n = nq - qoff
                    gs_ = s_rr[0] % 4
                    s_rr[0] += 1
                    sb_of[u] = gs_
                    mms = [(psG[gs_][:, 0:n], KA[hh][0:68, blk * 128:(blk + 1) * 128],
                            QAp[0:68, hh, qc0 + qoff:qc0 + nq], True, not masked, (KAn(hh, blk), qan))]
                    if masked:
                        mms.append((psG[gs_][:, 0:128], ident[:, :], mneg[:, :], False, True, ("ident", "mneg")))
                    mm_group(f"psG{gs_}", mms)

                def issue_exp(u):
                    hh, bi = units[u]
                    blk, qoff, masked = kblocks[bi]
                    n = nq - qoff
                    gs_ = sb_of[u]
                    pi = pt_rr[0] % 5
                    pt_rr[0] += 1
                    pt_of[u] = pi
                    if use_bias:
                        P[0].op("act", lambda e: e.activation(
                            out=PT[pi][:, 0:n], in_=psG[gs_][:, 0:n], func=AF.Exp, bias=biasT[:, blk, hh:hh + 1],
                            scale=1.0), reads=(f"psG{gs_}", "biasT"), writes=(f"PT{pi}",))
                    else:
                        P[0].op("act", lambda e: e.activation(
                            out=PT[pi][:, 0:n], in_=psG[gs_][:, 0:n], func=AF.Exp),
                            reads=(f"psG{gs_}",), writes=(f"PT{pi}",))

                def issue_mm2(u):
                    hh, bi = units[u]
                    blk, qoff, masked = kblocks[bi]
                    n = nq - qoff
                    go = 4 + hh % 2
                    pi = pt_of[u]
                    mm_group(f"psG{go}", [(psG[go][0:65, qoff:nq], VA[:, blk, hh, :], PT[pi][:, 0:n], bi == 0,
                                           bi == nkb - 1, (VAn(blk), f"PT{pi}"))])

                def norm_a(hh):
                    go = 4 + hh % 2
                    P[0].op("dve", lambda e: e.reciprocal(out=rd[64:65, 0:nq], in_=psG[go][64:65, 0:nq]),
                            reads=(f"psG{go}",), writes=("rd",))

                def norm_b(hh):
                    go = 4 + hh % 2
                    pr, pb = hh // 2, (hh % 2) * 64
                    gb = s_rr[0] % 4
                    s_rr[0] += 1
                    mm_group(f"psG{gb}", [(psG[gb][0:64, 0:nq], onesf[64:65, 0:64], rd[64:65, 0:nq], True, True,
                                           ("onesf", "rd"))])
                    P[0].op("dve", lambda e: e.tensor_tensor(
                        out=otmp[:, 0:nq], in0=psG[go][0:64, 0:nq], in1=sgTp[pb:pb + 64, pr, qc0:qc0 + nq],
                        op=ALU.mult), reads=(f"psG{go}", sgn), writes=("otmp",))
                    P[0].op("dve", lambda e: e.tensor_tensor(
                        out=og[:, hh, 0:nq], in0=psG[gb][0:64, 0:nq], in1=otmp[:, 0:nq], op=ALU.mult),
                        reads=(f"psG{gb}", "otmp"), writes=(ogname,))

                stride = max(1, nu // 48)
                pending = []
                for u in range(min(LA, nu)):
                    issue_mm1(u)
                for u in range(nu):
                    issue_exp(u)
                    if u + LA < nu:
                        issue_mm1(u + LA)
                    issue_mm2(u)
                    for it in pending:
                        it[0] -= 1
                    while pending and pending[0][0] <= 0:
                        norm_b(pending.pop(0)[1])
                    hh, bi = units[u]
                    if bi == nkb - 1:
                        norm_a(hh)
                        pending.append([min(12, nkb), hh])
                    if filler is not None and u % stride == 0:
                        next(filler, None)
                while pending:
                    norm_b(pending.pop(0)[1])
                dst = ogs[hg * GW:(hg + 1) * GW, scr_col0:scr_col0 + nq].rearrange("(h p) t -> p h t", p=64)
                dma(dst, og[:, :, 0:nq], (ogname,), ("ogscr",), ogname)

            P[0].op("pool", lambda e: e.memset(VA[:, :, :, 64:65], 1.0), reads=(), writes=ALLVA)
            for h in range(NH):
                P[0].op("pool", lambda e, h=h: e.memset(KA[h][64:68, :], 1.0), reads=(), writes=ALLKA[h])
            for hg in range(NG):
                load_weight(W0, "W0", wfox[hg], 4 * GW + NH, "gfx", gfx)
                P[0].op("pool", lambda e: e.memset(carry[:, 0, :], 0.0), reads=(), writes=("carry",))
                for par_ in range(2):
                    P[0].op("pool", lambda e, par_=par_: e.memset(QA[par_][64:68, :, :], 1.0), reads=(),
                            writes=(f"QA{par_}",))
                drain(fox_inproj_gen(hg, xp[0:512, :], 0, kp_o, vp_o, lp_o, 0, 0, True, 0))
                for i in range(NT):
                    nxt = None
                    if i + 1 < NT:
                        nxt = fox_inproj_gen(hg, xp[(i + 1) * 512:(i + 2) * 512, :], (i + 1) * 4, kp_o, vp_o, lp_o,
                                             (i + 1) * 512, (i + 1) % 2, True, i + 1)
                    kb = [(j, 0, False) for j in range(4 * i)] + [(4 * i + jj, 128 * jj, True) for jj in range(4)]
                    attention_qtile(hg, 0, 512, kb, 4 * i + 2, i * 512, i % 2, 4 * i, i % 2, nxt, use_bias=False)
                    drain(nxt)
                for sq in range(SEQ_PER_CORE):
                    base = sq * 12
                    P[0].op("pool", lambda e, base=base: e.memset(carry[:, base, :], 0.0), reads=(),
                            writes=("carry",))
                    dma(clt[:], clf[sq, :, hg * NH:(hg + 1) * NH].rearrange("(b p) h -> p b h", p=128), (), ("clt",),
                        "clt")
                    xk = xt[:, 0:2, :].rearrange("p a (b c) -> p (a b) c", c=GW)
                    xv = xt[:, 2:4, :].rearrange("p a (b c) -> p (a b) c", c=GW)
                    hk = hb[:, 0:2, :].rearrange("p a (b c) -> p (a b) c", c=GW)
                    dma(xk, ck[sq, :, hg * GW:(hg + 1) * GW].rearrange("(b p) c -> p b c", p=128), (), ("xt",), "xt")
                    dma(xv, cv[sq, :, hg * GW:(hg + 1) * GW].rearrange("(b p) c -> p b c", p=128), (), ("xt",), "xt")
                    P[0].op("dve", lambda e: e.tensor_copy(out=hb[:, 0:2, :], in_=xt[:, 0:2, :]), reads=("xt",),
                            writes=("hb",))
                    P[0].op("act", lambda e, base=base, xv=xv: e.activation(
                        out=VA[:, base:base + 8, :, 0:64], in_=xv.rearrange("p b (h d) -> p b h d", h=NH),
                        func=AF.Copy), reads=("xt",), writes=(VAn(base), VAn(base + 4)))
                    for b in range(8):
                        cumsum_block(clt[:, b, :], "clt", base + b)
                    for half in range(2):
                        for pr in range(NP):
                            tb = pbank()

                            def fn(e, tb=tb, half=half, pr=pr, hk=hk):
                                ins = None
                                first = None
                                for j in range(4):
                                    ins = e.transpose(psT[tb][:, j * 128:(j + 1) * 128],
                                                      hk[:, 4 * half + j, pr * 128:(pr + 1) * 128], ident[:])
                                    if first is None:
                                        first = ins
                                return (first, ins)
                            P[0].op("pe", fn, reads=("hb", "ident"), writes=(f"psG{tb}",))
                            for hf in range(2):
                                h = 2 * pr + hf
                                c0 = (base + 4 * half) * 128
                                P[0].op("dve", lambda e, tb=tb, h=h, hf=hf, c0=c0: e.tensor_copy(
                                    out=KA[h][0:64, c0:c0 + 512], in_=psT[tb][hf * 64:hf * 64 + 64, 0:512]),
                                    reads=(f"psG{tb}",), writes=(KAn(h, base + 4 * half),))
                P[0].op("pool", lambda e: e.memset(QA[0][64:68, :, :], 0.0), reads=(), writes=("QA0",))
                drain(fox_inproj_gen(hg, xs[:, :], NBLK - 4, ks_o, vs_o, ls_o, 0, 0, False, T // 512))
                for sq in range(SEQ_PER_CORE):
                    base = sq * 12
                    src_blk = NBLK - 4 + sq
                    for h in range(NH):
                        P[0].op("pool", lambda e, base=base, src_blk=src_blk, h=h: e.tensor_copy(
                            out=KA[h][0:64, (base + 8) * 128:(base + 9) * 128],
                            in_=KA[h][0:64, src_blk * 128:(src_blk + 1) * 128]), reads=(KAn(h, src_blk),),
                            writes=(KAn(h, base + 8),))
                    P[0].op("pool", lambda e, base=base, src_blk=src_blk: e.tensor_copy(
                        out=VA[:, base + 8, :, :], in_=VA[:, src_blk, :, :]), reads=(VAn(src_blk),),
                        writes=(VAn(base + 8),))
                    cumsum_block(lft[:, sq, :], "lft", base + 8)
                    kb = [(base + b, 0, False) for b in range(8)] + [(base + 8, 0, True)]
                    arow_prep(0, sq * 128, 128, base + 8, base + 8)
                    attention_qtile(hg, sq * 128, 128, kb, base + 8, T + sq * 128, sq % 2, base + 8, 0)
            emit_phase("A")

        with contextlib.ExitStack() as esD:
            S = mkS(esD)
            P[0] = Prog()
            WO0 = S("WO0", [128, 8, D], BF16)
            WG = S("WG", [128, 8, 3088], BF16)
            wa2s = S("wa2s", [16, 512], F32)
            bgab = S("bgab", [128, 512], F32)
            ggob = S("ggob", [128, 1024], F32)
            gfinb = S("gfinb", [128, 1024], F32)
            ogT = S("ogT", [128, 8, 512], BF16)
            ypt = S("ypt", [128, 4, D], F32)
            qTf = S("qTf", [128, 4, 512], F32)
            kTf = S("kTf", [128, 4, 512], F32)
            a1T = S("a1T", [16, 512], F32)
            k2 = S("k2", [128, 512], F32)
            v2 = [S(f"v2{i}", [128, 1024], BF16) for i in range(2)]
            sgg = [S(f"sgg{i}", [128, 1024], BF16) for i in range(2)]
            gg = S("gg", [128, 512], F32)
            ge = S("ge", [128, 512], F32)
            ebT = [[S(f"ebT{q}{h}", [128, 128], F32) for h in range(4)] for q in range(2)]
            enbT = [S(f"enbT{h}", [128, 128], F32) for h in range(4)]
            qeT = [[S(f"qeT{q}{h}", [128, 128], BF16) for h in range(4)] for q in range(2)]
            keT = [S(f"keT{h}", [128, 128], BF16) for h in range(4)]
            ebr = S("ebr", [128, 512], F32)
            kd = [S(f"kd{i}", [128, 512], BF16) for i in range(2)]
            ATs = [[S(f"ATs{q}{h}", [128, 128], BF16) for h in range(4)] for q in range(2)]
            Sst = S("Sst", [128, 4, 256], F32)
            Sbf = S("Sbf", [128, 4, 256], BF16)
            og1 = S("og1", [128, D], BF16)
            ss2 = S("ss2", [128, 8], F32)
            rs2 = S("rs2", [128, 8], F32)

            dma(wa2s[:], wa2[:, :], (), ("wa2s",), "wa2s")
            dma(bgab[:], bga[:, :], (), ("bgab",), "bgab")
            dma(ggob[:], ggo[:, :], (), ("ggob",), "ggob")
            dma(gfinb[:], gfin[:, :], (), ("gfinb",), "gfinb")
            wstD = S("wstD", [128, 1024], F32)
            load_weight(WO0, "WO0", wo0, D, None, None, wstD)
            load_weight(WG, "WG", wgla, 3088, "ggl", ggl, wstD)
            ALLS = tuple(f"Sst{h}" for h in range(4))
            ALLB = tuple(f"Sbf{h}" for h in range(4))
            P[0].op("pool", lambda e: e.memset(Sst[:], 0.0), reads=(), writes=ALLS)
            P[0].op("pool", lambda e: e.memset(Sbf[:], 0.0), reads=(), writes=ALLB)

            gba_rr = [0]

            def gba():
                i = 4 + gba_rr[0] % 4
                gba_rr[0] += 1
                return i

            def phase_d_tile(xsrc, scr_col0, rowbase, sample):
                dma(ypt[:], xsrc.rearrange("(s p) d -> p s d", p=128), (), ("ypt",), "ypt")
                dma(ogT[:], ogs[:, scr_col0:scr_col0 + 512].rearrange("(c p) t -> p c t", p=128), (), ("ogT",), "ogT")
                for s in range(4):
                    for nh in range(2):
                        g = gbank()
                        mm_group(f"psG{g}", [(psG[g][:, :], ogT[:, fc, s * 128:(s + 1) * 128],
                                              WO0[:, fc, nh * 512:(nh + 1) * 512], fc == 0, fc == 7, ("ogT", "WO0"))
                                             for fc in range(8)])
                        P[0].op("dve", lambda e, g=g, s=s, nh=nh: e.tensor_tensor(
                            out=ypt[:, s, nh * 512:(nh + 1) * 512], in0=psG[g][:, :],
                            in1=ypt[:, s, nh * 512:(nh + 1) * 512], op=ALU.add),
                            reads=(f"psG{g}", "ypt"), writes=("ypt",))
                dma(ypscr[rowbase:rowbase + 512, :].rearrange("(s p) d -> p s d", p=128), ypt[:], ("ypt",), (), "ypt")
                rms_rstd(lambda s: ypt[:, s, :], "ypt", 4, D)
                for s in range(4):
                    if s % 2 == 0:
                        P[0].op("act", lambda e, s=s: e.activation(out=hb[:, s, :], in_=ypt[:, s, :], func=AF.Copy,
                                                                   scale=rstd[:, s:s + 1]),
                                reads=("ypt", "rstd"), writes=("hb",))
                    else:
                        P[0].op("dve", lambda e, s=s: e.tensor_scalar(out=hb[:, s, :], in0=ypt[:, s, :],
                                                                      scalar1=rstd[:, s:s + 1], scalar2=None,
                                                                      op0=ALU.mult),
                                reads=("ypt", "rstd"), writes=("hb",))
                transpose_tok(lambda s: hb[:, s, :], "hb", 4, hT, "hT")
                for kind in range(2):
                    for h in range(4):
                        col = kind * 512 + h * 128
                        g = gbank()
                        mm_group(f"psG{g}", [(psG[g][:, :], WG[:, kc, col:col + 128], hT[:, kc, :], kc == 0, kc == 7,
                                              ("WG", "hT")) for kc in range(8)])
                        if kind == 0:
                            P[0].op("dve", lambda e, g=g, h=h: e.tensor_scalar(out=qTf[:, h, :], in0=psG[g][:, :],
                                                                              scalar1=128.0 ** -0.5, scalar2=None,
                                                                              op0=ALU.mult),
                                    reads=(f"psG{g}",), writes=("qTf",))
                        else:
                            P[0].op("dve", lambda e, g=g, h=h: e.tensor_copy(out=kTf[:, h, :], in_=psG[g][:, :]),
                                    reads=(f"psG{g}",), writes=("kTf",))
                g = gbank()
                mm_group(f"psG{g}", [(psG[g][0:16, :], WG[:, kc, 3072:3088], hT[:, kc, :], kc == 0, kc == 7,
                                      ("WG", "hT")) for kc in range(8)])
                P[0].op("dve", lambda e, g=g: e.tensor_copy(out=a1T[:, :], in_=psG[g][0:16, :]), reads=(f"psG{g}",),
                        writes=("a1T",))
                def stage_a(s, q):
                    ts_ = slice(s * 128, (s + 1) * 128)
                    v2q, kdq, sggq = v2[q], kd[q], sgg[q]
                    g = gba()
                    mm_group(f"psG{g}", [(psG[g][:, :], hT[:, kc, ts_], WG[:, kc, 512:1024], kc == 0, kc == 7,
                                          ("WG", "hT")) for kc in range(8)])
                    P[0].op("dve", lambda e, g=g: e.tensor_copy(out=k2[:], in_=psG[g][:, :]), reads=(f"psG{g}",),
                            writes=("k2",))
                    for nh in range(2):
                        g = gba()
                        mm_group(f"psG{g}", [(psG[g][:, :], hT[:, kc, ts_],
                                              WG[:, kc, 1024 + nh * 512:1536 + nh * 512], kc == 0, kc == 7,
                                              ("WG", "hT")) for kc in range(8)])
                        P[0].op("dve", lambda e, g=g, nh=nh: e.tensor_copy(out=v2q[:, nh * 512:(nh + 1) * 512],
                                                                          in_=psG[g][:, :]),
                                reads=(f"psG{g}",), writes=(f"v2{q}",))
                    for nh in range(2):
                        g = gba()
                        mm_group(f"psG{g}", [(psG[g][:, :], hT[:, kc, ts_],
                                              WG[:, kc, 2048 + nh * 512:2560 + nh * 512], kc == 0, kc == 7,
                                              ("WG", "hT")) for kc in range(8)])
                        P[0].op("act", lambda e, g=g, nh=nh: e.activation(out=sggq[:, nh * 512:(nh + 1) * 512],
                                                                         in_=psG[g][:, :], func=AF.Silu),
                                reads=(f"psG{g}",), writes=(f"sgg{q}",))
                    dma(sgscr[rowbase + s * 128:rowbase + (s + 1) * 128, :], sggq[:], (f"sgg{q}",), (), f"sgg{q}")
                    g = gba()
                    mm_group(f"psG{g}", [(psG[g][:, :], a1T[0:16, ts_], wa2s[0:16, :], True, True, ("a1T", "wa2s"))])
                    P[0].op("dve", lambda e, g=g: e.tensor_tensor(out=ge[:], in0=psG[g][:, :], in1=bgab[:],
                                                                  op=ALU.add),
                            reads=(f"psG{g}", "bgab"), writes=("ge",))
                    P[0].op("act", lambda e: e.activation(out=ge[:], in_=ge[:], func=AF.Exp, scale=-1.0),
                            reads=("ge",), writes=("ge",))
                    P[0].op("act", lambda e: e.activation(out=ge[:], in_=ge[:], func=AF.Ln, bias=1.0), reads=("ge",),
                            writes=("ge",))
                    gcol = 1 if sample else 0
                    P[0].op("dve", lambda e, gcol=gcol: e.tensor_scalar(out=gg[:], in0=ge[:],
                                                                        scalar1=gsc[:, gcol:gcol + 1], scalar2=None,
                                                                        op0=ALU.mult),
                            reads=("ge", "gsc"), writes=("gg",))
                    g = gba()
                    mm_group(f"psG{g}", [(psG[g][:, :], trir[:, :], gg[:, :], True, True, ("trir", "gg"))])
                    P[0].op("act", lambda e, g=g: e.activation(out=ebr[:], in_=psG[g][:, :], func=AF.Exp),
                            reads=(f"psG{g}",), writes=("ebr",))
                    P[0].op("dve", lambda e: e.tensor_tensor(out=kdq[:], in0=k2[:], in1=ebr[:], op=ALU.mult),
                            reads=("k2", "ebr"), writes=(f"kd{q}",))
                    for h in range(4):
                        hs = slice(h * 128, (h + 1) * 128)
                        g = gba()
                        mm_group(f"psG{g}", [(psG[g][:, 0:128], gg[:, hs], trii[:, :], True, True, ("gg", "trii"))])
                        P[0].op("act", lambda e, g=g, h=h: e.activation(out=ebT[q][h][:], in_=psG[g][:, 0:128],
                                                                        func=AF.Exp),
                                reads=(f"psG{g}",), writes=(f"ebT{q}{h}",))
                        P[0].op("act", lambda e, g=g, h=h: e.activation(out=enbT[h][:], in_=psG[g][:, 0:128],
                                                                        func=AF.Exp, scale=-1.0),
                                reads=(f"psG{g}",), writes=(f"enbT{h}",))
                    for h in range(4):
                        P[0].op("dve", lambda e, h=h: e.tensor_tensor(out=qeT[q][h][:], in0=qTf[:, h, ts_],
                                                                      in1=ebT[q][h][:], op=ALU.mult),
                                reads=("qTf", f"ebT{q}{h}"), writes=(f"qeT{q}{h}",))
                        P[0].op("dve", lambda e, h=h: e.tensor_tensor(out=keT[h][:], in0=kTf[:, h, ts_],
                                                                      in1=enbT[h][:], op=ALU.mult),
                                reads=("kTf", f"enbT{h}"), writes=(f"keT{h}",))
                    for h in range(4):
                        g2 = gba()
                        mm_group(f"psG{g2}", [(psG[g2][:, 0:128], keT[h][:, :], qeT[q][h][:, :], True, True,
                                               (f"keT{h}", f"qeT{q}{h}"))])
                        P[0].op("dve", lambda e, g2=g2, h=h: e.tensor_tensor(out=ATs[q][h][:], in0=psG[g2][:, 0:128],
                                                                             in1=trii[:], op=ALU.mult),
                                reads=(f"psG{g2}", "trii"), writes=(f"ATs{q}{h}",))

                def stage_b(s, q):
                    v2q, kdq = v2[q], kd[q]
                    go = (0, 1) if q == 0 else (2, 3)
                    if sample:
                        dma(Sst[:], sg[s].rearrange("h k v -> k h v"), (), ALLS, "Sst", q="sp")
                        P[0].op("pool", lambda e: e.tensor_copy(out=Sbf[:], in_=Sst[:]), reads=ALLS,
                                writes=ALLB)
                    for h in range(4):
                        hs = slice(h * 128, (h + 1) * 128)
                        ob = go[h // 2]
                        oc = slice((h % 2) * 256, (h % 2) * 256 + 256)
                        vs_ = slice(h * 256, (h + 1) * 256)
                        mm_group(f"psG{ob}", [(psG[ob][:, oc], ATs[q][h][:, :], v2q[:, vs_], True, False,
                                               (f"ATs{q}{h}", f"v2{q}")),
                                              (psG[ob][:, oc], qeT[q][h][:, :], Sbf[:, h, :], False, True,
                                               (f"qeT{q}{h}", f"Sbf{h}"))])
                        g3 = gba()
                        mm_group(f"psG{g3}", [(psG[g3][:, 0:256], kdq[:, hs], v2q[:, vs_], True, True,
                                               (f"kd{q}", f"v2{q}"))])
                        P[0].op("dve", lambda e, g3=g3, h=h: e.scalar_tensor_tensor(
                            out=Sst[:, h, :], in0=Sst[:, h, :], scalar=ebT[q][h][:, 127:128], in1=psG[g3][:, 0:256],
                            op0=ALU.mult, op1=ALU.add), reads=(f"psG{g3}", f"ebT{q}{h}", f"Sst{h}"),
                            writes=(f"Sst{h}",))
                        P[0].op("act", lambda e, h=h: e.activation(out=Sbf[:, h, :], in_=Sst[:, h, :], func=AF.Copy),
                                reads=(f"Sst{h}",), writes=(f"Sbf{h}",))
                    if sample:
                        dma(ss_o[s].rearrange("h k v -> k h v"), Sst[:], ALLS, (), "Sst", q="pool")
                    for h in range(4):
                        ob = go[h // 2]
                        oc = slice((h % 2) * 256, (h % 2) * 256 + 256)
                        P[0].op("act", lambda e, ob=ob, oc=oc, h=h: e.activation(
                            out=junk[:, 0:256], in_=psG[ob][:, oc], func=AF.Square, accum_out=ss2[:, h:h + 1]),
                            reads=(f"psG{ob}",), writes=("junk", f"ss2{h}"))
                    P[0].op("dve", lambda e: e.tensor_scalar(out=rs2[:, 0:4], in0=ss2[:, 0:4], scalar1=1.0 / 256,
                                                             scalar2=EPS, op0=ALU.mult, op1=ALU.add),
                            reads=tuple(f"ss2{h}" for h in range(4)), writes=("rs2",))
                    P[0].op("act", lambda e: e.activation(out=rs2[:, 0:4], in_=rs2[:, 0:4], func=AF.Ln),
                            reads=("rs2",), writes=("rs2",))
                    P[0].op("act", lambda e: e.activation(out=rs2[:, 0:4], in_=rs2[:, 0:4], func=AF.Exp, scale=-0.5),
                            reads=("rs2",), writes=("rs2",))
                    for h in range(4):
                        ob = go[h // 2]
                        oc = slice((h % 2) * 256, (h % 2) * 256 + 256)
                        P[0].op("dve", lambda e, ob=ob, oc=oc, h=h: e.scalar_tensor_tensor(
                            out=og1[:, h * 256:(h + 1) * 256], in0=psG[ob][:, oc], scalar=rs2[:, h:h + 1],
                            in1=ggob[:, h * 256:(h + 1) * 256], op0=ALU.mult, op1=ALU.mult),
                            reads=(f"psG{ob}", "rs2", "ggob"), writes=("og1",))
                    dma(onscr[rowbase + s * 128:rowbase + (s + 1) * 128, :], og1[:], ("og1",), (), "og1")

                stage_a(0, 0)
                for s in range(4):
                    if s + 1 < 4:
                        stage_a(s + 1, (s + 1) % 2)
                    stage_b(s, s % 2)

            for i in range(NT):
                phase_d_tile(xp[i * 512:(i + 1) * 512, :], i * 512, i * 512, False)
            dma(sp_o.rearrange("h k v -> k h v"), Sst[:], ALLS, (), "Sst")
            phase_d_tile(xs[:, :], T, T, True)
            emit_phase("D")

        with contextlib.ExitStack() as esE:
            S = mkS(esE)
            P[0] = Prog()
            WO1 = S("WO1e", [128, 8, D], BF16)
            gfinb = S("gfinbe", [128, 1024], F32)
            on_t = [S(f"on_t{i}", [128, 4, D], BF16) for i in range(2)]
            sg_t = [S(f"sg_t{i}", [128, 4, D], BF16) for i in range(2)]
            yp_t = [S(f"yp_t{i}", [128, 4, D], F32) for i in range(2)]
            og1 = S("og1e", [128, 4, D], BF16)
            og1T = S("og1Te", [128, 8, 512], BF16)
            y2 = S("y2e", [128, 4, D], F32)
            yo_ = [S(f"yoe{i}", [128, 4, D], F32) for i in range(2)]
            ss2 = S("ss2e", [128, 8], F32)
            rs2 = S("rs2e", [128, 8], F32)
            dma(gfinb[:], gfin[:, :], (), ("gfinb",), "gfinb")
            wstE = S("wstE", [128, 1024], F32)
            load_weight(WO1, "WO1", wo1, D, None, None, wstE)
            nch = T // 64

            def phase_e_tile(p0, rowbase, perm, yout, orow, bi):
                ont, sgt, ypt_, yo = on_t[bi], sg_t[bi], yp_t[bi], yo_[bi]
                nm = (f"on_t{bi}", f"sg_t{bi}", f"yp_t{bi}", f"yo{bi}")
                if perm:
                    for sb_ in range(4):
                        done = 0
                        while done < 128:
                            p = p0 + sb_ * 128 + done
                            c, i0 = p // nch, p % nch
                            ln = min(nch - i0, 128 - done)
                            src = onscr[i0 * 64 + c:(i0 + ln - 1) * 64 + c + 1:64, :]
                            dma(ont[done:done + ln, sb_, :], src, (), (nm[0],), nm[0])
                            done += ln
                else:
                    dma(ont[:], onscr[rowbase + p0:rowbase + p0 + 512, :].rearrange("(s p) d -> p s d", p=128), (),
                        (nm[0],), nm[0])
                dma(sgt[:], sgscr[rowbase + p0:rowbase + p0 + 512, :].rearrange("(s p) d -> p s d", p=128), (),
                    (nm[1],), nm[1])
                dma(ypt_[:], ypscr[rowbase + p0:rowbase + p0 + 512, :].rearrange("(s p) d -> p s d", p=128), (),
                    (nm[2],), nm[2])
                for sb_ in range(4):
                    P[0].op("dve" if sb_ % 2 == 0 else "pool", lambda e, sb_=sb_: e.tensor_tensor(
                        out=og1[:, sb_, :], in0=ont[:, sb_, :], in1=sgt[:, sb_, :], op=ALU.mult),
                        reads=(nm[0], nm[1]), writes=("og1",))
                transpose_tok(lambda s_: og1[:, s_, :], "og1", 4, og1T, "og1T")
                for sb_ in range(4):
                    for nh in range(2):
                        g = gbank()
                        mm_group(f"psG{g}", [(psG[g][:, :], og1T[:, fc, sb_ * 128:(sb_ + 1) * 128],
                                              WO1[:, fc, nh * 512:(nh + 1) * 512], fc == 0, fc == 7, ("og1T", "WO1"))
                                             for fc in range(8)])
                        P[0].op("dve", lambda e, g=g, nh=nh, sb_=sb_: e.tensor_tensor(
                            out=y2[:, sb_, nh * 512:(nh + 1) * 512], in0=psG[g][:, :],
                            in1=ypt_[:, sb_, nh * 512:(nh + 1) * 512], op=ALU.add),
                            reads=(f"psG{g}", nm[2]), writes=("y2",))
                    P[0].op("act", lambda e, sb_=sb_: e.activation(out=junk[:, :], in_=y2[:, sb_, :], func=AF.Square,
                                                                   accum_out=ss2[:, sb_:sb_ + 1]),
                            reads=("y2",), writes=("junk", f"ss2{sb_}"))
                P[0].op("dve", lambda e: e.tensor_scalar(out=rs2[:, 0:4], in0=ss2[:, 0:4], scalar1=1.0 / D,
                                                         scalar2=EPS, op0=ALU.mult, op1=ALU.add),
                        reads=tuple(f"ss2{i}" for i in range(4)), writes=("rs2",))
                P[0].op("act", lambda e: e.activation(out=rs2[:, 0:4], in_=rs2[:, 0:4], func=AF.Ln), reads=("rs2",),
                        writes=("rs2",))
                P[0].op("act", lambda e: e.activation(out=rs2[:, 0:4], in_=rs2[:, 0:4], func=AF.Exp, scale=-0.5),
                        reads=("rs2",), writes=("rs2",))
                for sb_ in range(4):
                    P[0].op("dve", lambda e, sb_=sb_: e.scalar_tensor_tensor(
                        out=yo[:, sb_, :], in0=y2[:, sb_, :], scalar=rs2[:, sb_:sb_ + 1], in1=gfinb[:],
                        op0=ALU.mult, op1=ALU.mult), reads=("y2", "rs2", "gfinb"), writes=(nm[3],))
                dma(yout[orow:orow + 512, :].rearrange("(s p) d -> p s d", p=128), yo[:], (nm[3],), (), nm[3])

            for j in range(T // 512):
                phase_e_tile(j * 512, 0, True, yp_o, j * 512, j % 2)
            phase_e_tile(0, T, False, ys_o, 0, (T // 512) % 2)
            emit_phase("E")
    return nc


def make_in_maps(inp, T, ncores):
    f = np.float32
    w_in_fox = np.asarray(inp["w_in_fox"], f)
    wf = []
    for hg in range(4):
        cols = [w_in_fox[:, k * 1024 + hg * 256:k * 1024 + (hg + 1) * 256] for k in range(4)]
        cols.append(w_in_fox[:, 4096 + hg * 4:4096 + (hg + 1) * 4])
        wf.append(np.concatenate(cols, axis=1))
    wf = np.ascontiguousarray(np.stack(wf))

    def pc(v):
        return np.ascontiguousarray(np.asarray(v, f).reshape(8, 128).T)

    def rep(v):
        v = np.asarray(v, f)
        return np.ascontiguousarray(np.broadcast_to(v[None, :], (128, v.shape[0])))

    s_idx = np.arange(128)[:, None]
    t_idx = np.arange(128)[None, :]
    gsc = np.zeros((128, 2), f)
    gsc[:, 0] = -1.0 / 16
    gsc[:64, 1] = -1.0 / 16
    consts = {
        "c_ident": (s_idx == t_idx).astype(f), "c_trii": (s_idx <= t_idx).astype(f),
        "c_trir": (s_idx > t_idx).astype(f), "c_ones": np.ones((128, 128), f),
        "c_mneg": np.where(s_idx > t_idx, -30000.0, 0.0).astype(f), "c_gsc": gsc,
    }
    shared = {
        "wfox": wf, "gfox": pc(inp["g_norm_fox"]), "bff": rep(inp["b_fox_f"]),
        "wo0": np.ascontiguousarray(np.asarray(inp["w_out_fox"], f)), "ggla": pc(inp["g_norm_gla"]),
        "wgla": np.ascontiguousarray(np.asarray(inp["w_in_gla"], f)),
        "wa2": np.ascontiguousarray(np.asarray(inp["w_gla_a2"], f)), "bga": rep(inp["b_gla_a"]),
        "ggo": rep(np.tile(np.asarray(inp["g_gla_o"], f), 4)),
        "wo1": np.ascontiguousarray(np.asarray(inp["w_out_gla"], f)), "gfin": rep(inp["g_final"]),
    }
    shared.update(consts)
    xprompt = np.asarray(inp["x_prompt"], f)
    xsample = np.asarray(inp["x_sample"], f)
    nb = xprompt.shape[0]
    maps = []
    for c in range(ncores):
        sl = slice(c * SEQ_PER_CORE, (c + 1) * SEQ_PER_CORE)
        xs = np.zeros((SEQ_PER_CORE, 128, D), f)
        xs[:, :DEC, :] = xsample[sl]
        m = dict(shared)
        m["xp"] = np.ascontiguousarray(xprompt[c % nb])
        m["xs"] = xs.reshape(SEQ_PER_CORE * 128, D)
        m["ck"] = np.ascontiguousarray(np.asarray(inp["cache_fox_k"], f)[sl].reshape(SEQ_PER_CORE, PAST, 1024))
        m["cv"] = np.ascontiguousarray(np.asarray(inp["cache_fox_v"], f)[sl].reshape(SEQ_PER_CORE, PAST, 1024))
        m["clf"] = np.ascontiguousarray(np.asarray(inp["cache_fox_logf"], f)[sl])
        m["sg"] = np.ascontiguousarray(np.asarray(inp["state_gla"], f)[sl])
        maps.append(m)
    return maps


_NC_CACHE = {}


def run(inp, ncores=NCORES):
    xprompt = np.asarray(inp["x_prompt"])
    B, T, _ = xprompt.shape
    if T not in _NC_CACHE:
        _NC_CACHE[T] = build_nc(T)
    nc = _NC_CACHE[T]
    maps = make_in_maps(inp, T, ncores)
    res = run_bass_kernel_spmd(nc, maps, core_ids=list(range(ncores))).results
    f = np.float32
    nseq = ncores * SEQ_PER_CORE

    def samp(name, last):
        a = np.stack([res[c][name].reshape(SEQ_PER_CORE, 128, last)[:, :DEC] for c in range(ncores)])
        return a.reshape(nseq, DEC, last)

    y_prompt = np.stack([res[b]["yp"] for b in range(B)]).astype(f)
    y_sample = samp("ys", D).astype(f)
    kp = np.stack([res[b]["kp"] for b in range(B)]).reshape(B, T, 16, 64).astype(f)
    vp = np.stack([res[b]["vp"] for b in range(B)]).reshape(B, T, 16, 64).astype(f)
    lp = np.stack([res[b]["lp"] for b in range(B)]).astype(f)
    stp = np.stack([res[b]["stp"] for b in range(B)]).astype(f)
    ks = samp("ks", 1024).reshape(nseq, DEC, 16, 64).astype(f)
    vs = samp("vs", 1024).reshape(nseq, DEC, 16, 64).astype(f)
    ls = samp("ls", 16).astype(f)
    sts = np.concatenate([res[c]["sts"] for c in range(ncores)], axis=0).astype(f)
    return (y_prompt, y_sample, kp, vp, lp, stp, ks, vs, ls, sts)


def kernel(**inputs):
    return run(inputs)
```
